# Optimizing a Trainium2 kernel written in Bass

```python
import jax, jax.numpy as jnp
from jax import lax
import numpy as np

D_MODEL = 1024
BATCH = 16
SEQ = 4096
DEPTH = 4

CHUNK = 64
Q_BLOCK = 128
HEAD_DIM = 64
SB_HEADS = 8
FOX_HEADS = 8
SB_WIDTH = SB_HEADS * HEAD_DIM
FOX_WIDTH = FOX_HEADS * HEAD_DIM
SGU_GROUPS = 8
SGU_WIDTH = 512
SGU_CHUNK = 128
N_BRANCH = 3
BRANCH_WIDTH = 512
D_FF = 2816
CONV_WIDTH = 3
EPS = 1e-6

SPLIT_SIZES = (SB_WIDTH, SB_WIDTH, SB_WIDTH,
               FOX_WIDTH, FOX_WIDTH, FOX_WIDTH,
               FOX_HEADS,
               2 * SGU_WIDTH,
               N_BRANCH * D_MODEL)
IN_COLS = 3 * SB_WIDTH + 3 * FOX_WIDTH + FOX_HEADS + 2 * SGU_WIDTH + N_BRANCH * D_MODEL

kernel_name = "hybrid_sb_fox_sgu_gated_trunk"


def rmsnorm(x, g):
    xf = x.astype(jnp.float32)
    y = xf * lax.rsqrt(jnp.mean(xf * xf, axis=-1, keepdims=True) + EPS)
    return (y * g.astype(jnp.float32)).astype(x.dtype)


def split_heads(t, n_heads):
    b, s, _ = t.shape
    return t.reshape(b, s, n_heads, HEAD_DIM).transpose(0, 2, 1, 3)


def merge_heads(t):
    b, h, s, d = t.shape
    return t.transpose(0, 2, 1, 3).reshape(b, s, h * d)


def stick_breaking_attention(q, k, v):
    seq = q.shape[2]
    scale = HEAD_DIM ** -0.5
    outs = []
    for start in range(0, seq, Q_BLOCK):
        end = start + Q_BLOCK
        qb = q[:, :, start:end].astype(jnp.float32)
        kb = k[:, :, :end].astype(jnp.float32)
        vb = v[:, :, :end]
        z = jnp.einsum('bhqd,bhkd->bhqk', qb, kb) * scale
        t_pos = jnp.arange(start, end)[:, None]
        s_pos = jnp.arange(end)[None, :]
        strict = s_pos < t_pos
        log_stay = jnp.where(strict, jax.nn.log_sigmoid(-z), 0.0)
        later = lax.cumsum(log_stay, axis=3, reverse=True) - log_stay
        w = jnp.where(strict, jnp.exp(jax.nn.log_sigmoid(z) + later), 0.0)
        outs.append(jnp.einsum('bhqk,bhkd->bhqd', w.astype(vb.dtype), vb))
    return jnp.concatenate(outs, axis=2)


def forgetting_attention(q, k, v, log_f):
    seq = q.shape[2]
    scale = HEAD_DIM ** -0.5
    c = jnp.cumsum(log_f, axis=-1)
    outs = []
    for start in range(0, seq, Q_BLOCK):
        end = start + Q_BLOCK
        qb = q[:, :, start:end].astype(jnp.float32)
        kb = k[:, :, :end].astype(jnp.float32)
        vb = v[:, :, :end]
        z = (jnp.einsum('bhqd,bhkd->bhqk', qb, kb) * scale
             + c[:, :, start:end, None] - c[:, :, None, :end])
        causal = jnp.arange(end)[None, :] <= jnp.arange(start, end)[:, None]
        p = jax.nn.softmax(jnp.where(causal, z, -jnp.inf), axis=-1)
        outs.append(jnp.einsum('bhqk,bhkd->bhqd', p.astype(vb.dtype), vb))
    return jnp.concatenate(outs, axis=2)


def spatial_gating(u, v, ln_g, w_s, b_s):
    b, s, _ = v.shape
    vf = v.astype(jnp.float32)
    mu = jnp.mean(vf, axis=-1, keepdims=True)
    var = jnp.mean(jnp.square(vf - mu), axis=-1, keepdims=True)
    vn = (vf - mu) * lax.rsqrt(var + EPS) * ln_g.astype(jnp.float32)
    vc = vn.reshape(b, s // SGU_CHUNK, SGU_CHUNK, SGU_GROUPS, SGU_WIDTH // SGU_GROUPS)
    pos = jnp.arange(SGU_CHUNK)
    mask = (pos[None, :] // CHUNK) <= (pos[:, None] // CHUNK)
    w = jnp.where(mask[None], w_s, 0.0).astype(jnp.float32)
    mixed = (jnp.einsum('gts,bnsgc->bntgc', w, vc)
             + b_s.astype(jnp.float32).T[None, None, :, :, None])
    return (u.astype(jnp.float32) * mixed.reshape(b, s, SGU_WIDTH)).astype(u.dtype)


def conv_glu_ffn(x, w_up, conv_w, conv_b, w_down):
    s = x.shape[1]
    gate, value = jnp.split(x @ w_up, 2, axis=-1)
    gp = jnp.pad(gate, ((0, 0), (CONV_WIDTH - 1, 0), (0, 0)))
    conv = conv_b + gp[:, 0:s] * conv_w[0]
    for i in range(1, CONV_WIDTH):
        conv = conv + gp[:, i:i + s] * conv_w[i]
    return (jax.nn.gelu(conv, approximate=False) * value) @ w_down


def setup_inputs(seed: int = 0) -> dict:
    key = jax.random.key(seed)
    ks = jax.random.split(key, 16)

    def nrm(k, shape, scale):
        return jax.random.normal(k, shape, jnp.float32) * scale

    x = nrm(ks[0], (BATCH, SEQ, D_MODEL), 1.0)
    norm_mix_g = 1.0 + nrm(ks[1], (DEPTH, D_MODEL), 0.1)
    w_in = nrm(ks[2], (DEPTH, D_MODEL, IN_COLS), D_MODEL ** -0.5)
    fox_bias = jnp.linspace(1.0, 4.0, FOX_HEADS, dtype=jnp.float32)[None, :] + nrm(ks[3], (DEPTH, FOX_HEADS), 0.1)
    sgu_ln_g = 1.0 + nrm(ks[4], (DEPTH, SGU_WIDTH), 0.1)
    sgu_w = nrm(ks[5], (DEPTH, SGU_GROUPS, SGU_CHUNK, SGU_CHUNK), SGU_CHUNK ** -0.5)
    sgu_b = 1.0 + nrm(ks[6], (DEPTH, SGU_GROUPS, SGU_CHUNK), 0.1)
    w_branch = nrm(ks[7], (DEPTH, N_BRANCH, BRANCH_WIDTH, D_MODEL), BRANCH_WIDTH ** -0.5)
    w_out = nrm(ks[8], (DEPTH, D_MODEL, D_MODEL), D_MODEL ** -0.5)
    norm_ffn_g = 1.0 + nrm(ks[9], (DEPTH, D_MODEL), 0.1)
    w_up = nrm(ks[10], (DEPTH, D_MODEL, 2 * D_FF), D_MODEL ** -0.5)
    conv_w = nrm(ks[11], (DEPTH, CONV_WIDTH, D_FF), CONV_WIDTH ** -0.5)
    conv_b = nrm(ks[12], (DEPTH, D_FF), 0.02)
    w_down = nrm(ks[13], (DEPTH, D_FF, D_MODEL), D_FF ** -0.5)
    final_g = 1.0 + nrm(ks[14], (D_MODEL,), 0.1)
    return {"x": x, "norm_mix_g": norm_mix_g, "w_in": w_in, "fox_bias": fox_bias,
            "sgu_ln_g": sgu_ln_g, "sgu_w": sgu_w, "sgu_b": sgu_b, "w_branch": w_branch,
            "w_out": w_out, "norm_ffn_g": norm_ffn_g, "w_up": w_up, "conv_w": conv_w,
            "conv_b": conv_b, "w_down": w_down, "final_g": final_g}


def reference(x, norm_mix_g, w_in, fox_bias, sgu_ln_g, sgu_w, sgu_b, w_branch,
              w_out, norm_ffn_g, w_up, conv_w, conv_b, w_down, final_g):
    b, s, _ = x.shape
    offsets = np.cumsum(SPLIT_SIZES)[:-1].tolist()
    h = x
    for l in range(DEPTH):
        xn = rmsnorm(h, norm_mix_g[l])
        proj = xn @ w_in[l]
        q_a, k_a, v_a, q_b, k_b, v_b, f_b, uv_c, gates = jnp.split(proj, offsets, axis=-1)

        y_a = merge_heads(stick_breaking_attention(
            split_heads(q_a, SB_HEADS), split_heads(k_a, SB_HEADS), split_heads(v_a, SB_HEADS)))

        log_f = jax.nn.log_sigmoid(f_b.astype(jnp.float32) + fox_bias[l].astype(jnp.float32))
        y_b = merge_heads(forgetting_attention(
            split_heads(q_b, FOX_HEADS), split_heads(k_b, FOX_HEADS), split_heads(v_b, FOX_HEADS),
            log_f.transpose(0, 2, 1)))

        u_c, v_c = jnp.split(jax.nn.gelu(uv_c, approximate=False), 2, axis=-1)
        y_c = spatial_gating(u_c, v_c, sgu_ln_g[l], sgu_w[l], sgu_b[l])

        g = jax.nn.sigmoid(gates.reshape(b, s, N_BRANCH, D_MODEL))
        merged = (g[:, :, 0] * (y_a @ w_branch[l, 0])
                  + g[:, :, 1] * (y_b @ w_branch[l, 1])
                  + g[:, :, 2] * (y_c @ w_branch[l, 2]))
        h = h + merged @ w_out[l]

        h = h + conv_glu_ffn(rmsnorm(h, norm_ffn_g[l]), w_up[l], conv_w[l], conv_b[l], w_down[l])
    return rmsnorm(h, final_g)
```

```python
import numpy as np
import ml_dtypes
from contextlib import ExitStack
import concourse.bass as bass
import concourse.mybir as mybir
from concourse.bass_utils import run_bass_kernel_spmd

F32 = mybir.dt.float32
BF16 = mybir.dt.bfloat16
AF = mybir.ActivationFunctionType
ALU = mybir.AluOpType

D = 1024
KD = D // 128
HD = 64
NH = 8
IN_COLS = 7176
D_FF = 2816
NFF = D_FF // 128
EPS = 1e-6
AUG = 70
C_QA, C_KA, C_VA, C_QB, C_KB, C_VB, C_F, C_U, C_V, C_G = 0, 512, 1024, 1536, 2048, 2560, 3072, 3080, 3592, 4104


class Buf:
    __slots__ = ("name", "w", "r")

    def __init__(self, name):
        self.name = name
        self.w = {}
        self.r = {}


class EngState:
    def __init__(self, name, eng, sem):
        self.name = name
        self.eng = eng
        self.sem = sem
        self.key = "E_" + name
        self.count = 0
        self.waited = {}


class Tr:
    def __init__(self, nc, ndma=8):
        self.nc = nc
        self.E = {}
        for name, attr in (("PE", "tensor"), ("ACT", "scalar"), ("DVE", "vector"), ("POOL", "gpsimd"), ("SP", "sync")):
            self.E[name] = EngState(name, getattr(nc, attr), nc.alloc_semaphore("s_" + name))
        self.Q = {}
        for q in ("SP", "POOL"):
            sems = [nc.alloc_semaphore(f"d_{q}_{i}") for i in range(ndma)]
            self.Q[q] = dict(sems=sems, cnts=[0] * ndma, nxt=0)
        self.n_inst = 0

    def _wait(self, E, key, sem, val):
        if E.waited.get(key, 0) >= val:
            return
        E.eng.wait_ge(sem, val)
        E.waited[key] = val

    def _deps(self, E, reads, writes, skip_same_raw=False):
        for b in reads:
            for key, (sem, val) in b.w.items():
                if key == E.key and skip_same_raw:
                    continue
                self._wait(E, key, sem, val)
        for b in writes:
            for key, (sem, val) in b.w.items():
                if key == E.key and skip_same_raw:
                    continue
                self._wait(E, key, sem, val)
            for key, (sem, val) in b.r.items():
                if key == E.key:
                    continue
                self._wait(E, key, sem, val)

    def _stamp(self, key, sem, val, reads, writes):
        for b in reads:
            b.r[key] = (sem, val)
        for b in writes:
            b.w = {key: (sem, val)}
            b.r = {}

    def op(self, eng, fn, reads=(), writes=()):
        E = self.E[eng]
        self._deps(E, reads, writes, skip_same_raw=(eng == "PE"))
        ins = fn(E.eng)
        E.count += 1
        ins.then_inc(E.sem, 1)
        self._stamp(E.key, E.sem, E.count, reads, writes)
        self.n_inst += 1
        return ins

    def mm_group(self, fns, reads=(), writes=()):
        E = self.E["PE"]
        self._deps(E, reads, writes, skip_same_raw=True)
        ins = None
        for fn in fns:
            ins = fn(E.eng)
            self.n_inst += 1
        E.count += 1
        ins.then_inc(E.sem, 1)
        self._stamp(E.key, E.sem, E.count, reads, writes)

    def dma(self, q, out, in_, reads=(), writes=(), part=False, **kw):
        Q = self.Q[q]
        E = self.E[q]
        slot = Q["nxt"]
        Q["nxt"] = (slot + 1) % len(Q["sems"])
        sem = Q["sems"][slot]
        key = f"D_{q}_{slot}"
        if Q["cnts"][slot] > 0:
            self._wait(E, key, sem, Q["cnts"][slot])
        if part:
            self._deps(E, reads, ())
            for b in writes:
                for k2, (s2, v2) in b.r.items():
                    self._wait(E, k2, s2, v2)
        else:
            self._deps(E, reads, writes)
        ins = E.eng.dma_start(out=out, in_=in_, **kw)
        Q["cnts"][slot] += 16
        ins.then_inc(sem, 16)
        if part:
            for b in reads:
                b.r[key] = (sem, Q["cnts"][slot])
            for b in writes:
                b.w[key] = (sem, Q["cnts"][slot])
                b.r = {}
        else:
            self._stamp(key, sem, Q["cnts"][slot], reads, writes)
        self.n_inst += 1

    def barrier(self):
        items = []
        for e in self.E.values():
            if e.count > 0:
                items.append((e.key, e.sem, e.count))
        for q, Q in self.Q.items():
            for i, s in enumerate(Q["sems"]):
                if Q["cnts"][i] > 0:
                    items.append((f"D_{q}_{i}", s, Q["cnts"][i]))
        for e in self.E.values():
            for key, sem, val in items:
                if key == e.key:
                    continue
                self._wait(e, key, sem, val)

    def finish(self, eng="SP"):
        e = self.E[eng]
        for q, Q in self.Q.items():
            for i, s in enumerate(Q["sems"]):
                if Q["cnts"][i] > 0:
                    self._wait(e, f"D_{q}_{i}", s, Q["cnts"][i])
        for e2 in self.E.values():
            if e2.count > 0 and e2 is not e:
                self._wait(e, e2.key, e2.sem, e2.count)


class Ctx:
    pass


_UID = [0]


def uq(name):
    _UID[0] += 1
    return f"{name}_{_UID[0]}"


def build(NB, S, depth, dbg=None):
    assert S % 512 == 0
    NG = S // 512
    NBLK = S // 128
    nc = bass.Bass("TRN2", target_bir_lowering=False)
    tr = Tr(nc)
    c = Ctx()
    c.nc, c.tr, c.NB, c.S, c.NG, c.NBLK, c.depth = nc, tr, NB, S, NG, NBLK, depth

    def din(name, shape, dt=F32):
        return nc.dram_tensor(name, list(shape), dt, kind="ExternalInput").ap()

    def dscr(name, shape, dt):
        kind = "ExternalOutput" if (dbg and name in dbg) else "Internal"
        return nc.dram_tensor(name, list(shape), dt, kind=kind).ap()

    L = max(depth, 1)
    c.x = din("x", [NB, S, D])
    c.norm_mix_g = din("norm_mix_g", [L, D])
    c.w_in = din("w_in", [L, D, IN_COLS])
    c.fox_bias = din("fox_bias", [L, NH])
    c.sgu_ln_g = din("sgu_ln_g", [L, 512])
    c.sgu_w = din("sgu_w", [L, 8, 128, 128])
    c.sgu_b = din("sgu_b", [L, 8, 128])
    c.w_branch = din("w_branch", [L, 3, 512, D])
    c.w_out = din("w_out", [L, D, D])
    c.norm_ffn_g = din("norm_ffn_g", [L, D])
    c.w_up = din("w_up", [L, D, 2 * D_FF])
    c.conv_w = din("conv_w", [L, 3, D_FF])
    c.conv_b = din("conv_b", [L, D_FF])
    c.w_down = din("w_down", [L, D_FF, D])
    c.final_g = din("final_g", [D])
    c.k_ident = din("k_ident", [128, 128])
    c.k_masks = din("k_masks", [128, 4, 128])
    c.out = nc.dram_tensor("out", [NB, S, D], F32, kind="ExternalOutput").ap()

    c.hT = dscr("hT", [NB, D, S], F32)
    c.xnT = dscr("xnT", [NB, D, S], BF16)
    c.QaT = dscr("QaT", [NB, 512, S], BF16)
    c.KaT = dscr("KaT", [NB, 512, S], BF16)
    c.Va = dscr("Va", [NB, S, 512], BF16)
    c.QbT = dscr("QbT", [NB, NH, AUG, S], BF16)
    c.KbT = dscr("KbT", [NB, NH, AUG, S], BF16)
    c.Vb = dscr("Vb", [NB, S, 512], BF16)
    c.YaT = dscr("YaT", [NB, 512, S], BF16)
    c.YbT = dscr("YbT", [NB, 512, S], BF16)
    c.YcT = dscr("YcT", [NB, 512, S], BF16)
    c.hmid = dscr("hmid", [NB, D_FF, S], BF16)

    with ExitStack() as gs:
        def sb(name, shape, dt):
            return gs.enter_context(nc.sbuf_tensor(name, list(shape), dt))

        c.ident = sb("ident", [128, 128], F32)
        c.ones_bf = sb("ones_bf", [128, 128], BF16)
        c.b_const = Buf("const")
        tr.dma("SP", c.ident[:], c.k_ident[:, :], writes=[c.b_const])
        tr.op("DVE", lambda e: e.memset(c.ones_bf[:], 1.0), writes=[c.b_const])
        c.eps_col = sb("eps_col", [128, 1], F32)
        c.one_col = sb("one_col", [128, 1], F32)
        tr.op("DVE", lambda e: e.memset(c.one_col[:], 1.0), writes=[c.b_const])
        tr.op("DVE", lambda e: e.memset(c.eps_col[:], EPS), writes=[c.b_const])
        c.ps = [gs.enter_context(nc.psum_tensor(f"ps{i}", [128, 512], F32)) for i in range(8)]
        c.psb = [Buf(f"ps{i}") for i in range(8)]

        c.ones_big = sb("ones_big", [8, 3 * 512], BF16)
        tr.op("DVE", lambda e: e.memset(c.ones_big[:], 1.0), writes=[c.b_const])
        c.masks = sb("masks", [128, 4, 128], F32)
        tr.dma("SP", c.masks[:], c.k_masks[:, :, :], writes=[c.b_const])
        c.masks_bf = sb("masks_bf", [128, 4, 128], BF16)
        tr.op("DVE", lambda e: e.tensor_copy(out=c.masks_bf[:], in_=c.masks[:]), reads=[c.b_const], writes=[c.b_const])
        c.negtri_bf = sb("negtri_bf", [128, 2, 128], BF16)
        tr.op("DVE", lambda e: e.tensor_scalar(out=c.negtri_bf[:], in0=c.masks[:, 2:4, :], scalar1=-1.0, scalar2=None, op0=ALU.mult),
              reads=[c.b_const], writes=[c.b_const])
        c.ident_bf = sb("ident_bf", [128, 128], BF16)
        tr.op("DVE", lambda e: e.tensor_copy(out=c.ident_bf[:], in_=c.ident[:]), reads=[c.b_const], writes=[c.b_const])
        c.nb_sb = sb("nb_sb", [128, 128], BF16)
        c.nb_fox = sb("nb_fox", [128, 128], BF16)
        tr.op("DVE", lambda e: e.tensor_scalar(out=c.nb_sb[:], in0=c.masks[:, 0, :], scalar1=-1.0, scalar2=30000.0, op0=ALU.add, op1=ALU.mult),
              reads=[c.b_const], writes=[c.b_const])
        tr.op("DVE", lambda e: e.tensor_scalar(out=c.nb_fox[:], in0=c.masks[:, 1, :], scalar1=-1.0, scalar2=30000.0, op0=ALU.add, op1=ALU.mult),
              reads=[c.b_const], writes=[c.b_const])
        init_small_vectors(c, gs)
        phase_in_transpose(c)
        tr.barrier()
        stop_after = (dbg or {}).get("stop_after") if isinstance(dbg, dict) else None
        for l in range(depth):
            phase_inproj(c, l)
            tr.barrier()
            if stop_after == "inproj":
                break
            phase_attn(c, l, "sb")
            tr.barrier()
            phase_attn(c, l, "fox")
            tr.barrier()
            if stop_after == "attn":
                break
            phase_merge(c, l)
            tr.barrier()
            if stop_after == "merge":
                break
            phase_ffn_up(c, l)
            tr.barrier()
            phase_ffn_down(c, l)
            tr.barrier()
        phase_final(c)
        tr.finish("SP")
    return nc


def phase_in_transpose(c):
    nc, tr = c.nc, c.tr
    with ExitStack() as st:
        xt = [st.enter_context(nc.sbuf_tensor(f"p0_x{i}", [128, D], F32)) for i in range(2)]
        xb = [Buf(f"p0_x{i}") for i in range(2)]
        ht = [st.enter_context(nc.sbuf_tensor(f"p0_h{i}", [128, KD, 512], F32)) for i in range(2)]
        hb = [Buf(f"p0_h{i}") for i in range(2)]
        it = 0
        for b in range(c.NB):
            hT_v = c.hT[b].rearrange("(k p) t -> p k t", p=128)
            for g in range(c.NG):
                H, HB = ht[g % 2], hb[g % 2]
                for tb in range(4):
                    X, XB = xt[it % 2], xb[it % 2]
                    t0 = g * 512 + tb * 128
                    tr.dma("SP", X[:], c.x[b, t0:t0 + 128, :], writes=[XB])
                    for half in range(2):
                        pi = (it * 2 + half) % 4
                        P, PB = c.ps[pi], c.psb[pi]
                        tr.mm_group(
                            [(lambda e, j=j, P=P, X=X, half=half: e.transpose(
                                P[:, j * 128:(j + 1) * 128], X[:, (half * 4 + j) * 128:(half * 4 + j + 1) * 128], c.ident[:]))
                             for j in range(4)],
                            reads=[XB, c.b_const], writes=[PB])
                        eng = "ACT" if half == 0 else "DVE"
                        if eng == "ACT":
                            tr.op("ACT", lambda e, P=P, H=H, half=half, tb=tb: e.activation(
                                out=H[:, half * 4:(half + 1) * 4, tb * 128:(tb + 1) * 128],
                                in_=P[:].rearrange("p (j t) -> p j t", j=4), func=AF.Copy),
                                reads=[PB], writes=[HB])
                        else:
                            tr.op("DVE", lambda e, P=P, H=H, half=half, tb=tb: e.tensor_copy(
                                out=H[:, half * 4:(half + 1) * 4, tb * 128:(tb + 1) * 128],
                                in_=P[:].rearrange("p (j t) -> p j t", j=4)),
                                reads=[PB], writes=[HB])
                    it += 1
                tr.dma("SP", hT_v[:, :, g * 512:(g + 1) * 512], H[:], reads=[HB])


def emit_rmsnorm(c, H, HB, XN, XNB, g_col, sq, sqb, rstd, rstdb, pbank, out_dt_note=None):
    nc, tr = c.nc, c.tr
    P, PB = c.ps[pbank], c.psb[pbank]
    tr.op("ACT", lambda e: e.activation(out=sq[:], in_=H[:], func=AF.Square), reads=[HB], writes=[sqb])
    tr.mm_group([(lambda e, k=k: e.matmul(P[:], c.ones_bf[:], sq[:, k, :], start=(k == 0), stop=(k == KD - 1)))
                 for k in range(KD)], reads=[sqb, c.b_const], writes=[PB])
    tr.op("ACT", lambda e: e.activation(out=rstd[:], in_=P[:], func=AF.Ln, scale=1.0 / D, bias=c.eps_col[:, 0:1]),
          reads=[PB, c.b_const], writes=[rstdb])
    tr.op("ACT", lambda e: e.activation(out=rstd[:], in_=rstd[:], func=AF.Exp, scale=-0.5),
          reads=[rstdb], writes=[rstdb])
    for k in range(KD):
        tr.op("DVE", lambda e, k=k: e.scalar_tensor_tensor(
            out=XN[:, k, :], in0=H[:, k, :], scalar=g_col[:, k:k + 1], in1=rstd[:], op0=ALU.mult, op1=ALU.mult),
            reads=[HB, rstdb, c.b_const], writes=[XNB])


def init_small_vectors(c, gs):
    nc, tr = c.nc, c.tr
    L = max(c.depth, 1)
    c.smallv = gs.enter_context(nc.sbuf_tensor("smallv", [128, 512], F32))
    c.o_gmix, c.o_gffn, c.o_fg, c.o_cb, c.o_cw = 0, L * KD, 2 * L * KD, 2 * L * KD + KD, 2 * L * KD + KD + L * NFF
    assert c.o_cw + L * 3 * NFF <= 512
    P, PB = c.ps[0], c.psb[0]
    with ExitStack() as st:
        rows = st.enter_context(nc.sbuf_tensor(uq("sv_rows"), [128, 8, 128], F32))
        rb = Buf("sv_rows")
        srcs = [(c.norm_mix_g.rearrange("l (k p) -> (l k) p", p=128), L * KD, c.o_gmix),
                (c.norm_ffn_g.rearrange("l (k p) -> (l k) p", p=128), L * KD, c.o_gffn),
                (c.final_g.rearrange("(k p) -> k p", p=128), KD, c.o_fg),
                (c.conv_b.rearrange("l (k p) -> (l k) p", p=128), L * NFF, c.o_cb)]
        for l in range(L):
            srcs.append((c.conv_w[l].rearrange("i (k p) -> (i k) p", p=128), 3 * NFF, c.o_cw + l * 3 * NFF))
        assert len(srcs) <= 8
        for i, (src, n, off) in enumerate(srcs):
            tr.dma("SP", rows[0:n, i, :], src, writes=[rb])
        fns = []
        for i, (src, n, off) in enumerate(srcs):
            fns.append(lambda e, i=i, n=n, off=off: e.transpose(P[:, off:off + n], rows[0:n, i, :], c.ident[0:n, 0:n]))
        tr.mm_group(fns, reads=[rb, c.b_const], writes=[PB])
        tr.op("DVE", lambda e: e.tensor_copy(out=c.smallv[:, 0:c.o_cw + L * 3 * NFF], in_=P[:, 0:c.o_cw + L * 3 * NFF]),
              reads=[PB], writes=[c.b_const])
    for b in range(c.NB):
        for g in range(c.NG):
            src = c.ones_big[:].rearrange("p (j t) -> p j t", j=3)
            tr.dma("SP", c.QbT[b, :, 67:70, g * 512:(g + 1) * 512], src, reads=[c.b_const])
            tr.dma("SP", c.KbT[b, :, 64:67, g * 512:(g + 1) * 512], src, reads=[c.b_const])


def load_w_cast(c, dst, dstb, src_rows, col0, ncols, kchunks, dcol0=0):
    tr = c.tr
    for k in range(kchunks):
        o = 0
        while o < ncols:
            n = min(2048, ncols - o)
            tr.dma("POOL", dst[:, k, dcol0 + o:dcol0 + o + n], src_rows[k * 128:(k + 1) * 128, col0 + o:col0 + o + n], writes=[dstb], part=True)
            o += n


def phase_inproj(c, l):
    nc, tr = c.nc, c.tr
    NB, S, NG = c.NB, c.S, c.NG
    NW = C_G
    with ExitStack() as st:
        def sb(name, shape, dt):
            return st.enter_context(nc.sbuf_tensor(uq("p1_" + name), list(shape), dt))
        W = sb("w", [128, KD, NW], BF16)
        WB = Buf("p1_w")
        load_w_cast(c, W, WB, c.w_in[l], 0, NW, KD)
        wst = sb("wst", [128, 8, 128], BF16)
        wstb = Buf("p1_wst")
        wraw = sb("wraw", [128, 8, 128], F32)
        wrawb = Buf("p1_wraw")
        tr.dma("SP", wraw[:], c.sgu_w[l].rearrange("g t s -> t g s"), writes=[wrawb])
        for half in range(2):
            P, PB = c.ps[6 + half], c.psb[6 + half]
            tr.mm_group([(lambda e, j=j, P=P, half=half: e.transpose(P[:, j * 128:(j + 1) * 128], wraw[:, half * 4 + j, :], c.ident[:]))
                         for j in range(4)], reads=[wrawb, c.b_const], writes=[PB])
            tr.op("DVE", lambda e, P=P, half=half: e.tensor_copy(out=wst[:, half * 4:(half + 1) * 4, :],
                                                                 in_=P[:].rearrange("p (j t) -> p j t", j=4)),
                  reads=[PB], writes=[wstb])
        tr.op("DVE", lambda e: e.memset(wst[64:128, :, 0:64], 0.0), reads=[wstb], writes=[wstb])
        sgub32 = sb("sgub32", [1, 1024], F32)
        sgub_hi = sb("sgub_hi", [1, 1024], BF16)
        sgub_lo = sb("sgub_lo", [1, 1024], BF16)
        sgubb = Buf("p1_sgub")
        tr.dma("SP", sgub32[:], c.sgu_b[l:l + 1].rearrange("o g t -> o (g t)"), writes=[sgubb])
        tr.op("DVE", lambda e: e.tensor_copy(out=sgub_hi[:], in_=sgub32[:]), reads=[sgubb], writes=[sgubb])
        tr.op("DVE", lambda e: e.tensor_tensor(out=sgub32[:], in0=sgub32[:], in1=sgub_hi[:], op=ALU.subtract), reads=[sgubb], writes=[sgubb])
        tr.op("DVE", lambda e: e.tensor_copy(out=sgub_lo[:], in_=sgub32[:]), reads=[sgubb], writes=[sgubb])
        lng = sb("lng", [128, 512], F32)
        foxb = sb("foxb", [8, 1], F32)
        ones8 = sb("ones8", [8, 512], F32)
        lb = Buf("p1_lng")
        tr.dma("SP", lng[:], c.sgu_ln_g[l].partition_broadcast(128), writes=[lb])
        tr.dma("SP", foxb[:], c.fox_bias[l].rearrange("(h o) -> h o", o=1), writes=[lb])
        tr.op("DVE", lambda e: e.memset(ones8[:], 1.0), reads=[lb], writes=[lb])
        gcol = c.smallv[:, c.o_gmix + l * KD:c.o_gmix + (l + 1) * KD]

        ht = [sb(f"h{i}", [128, KD, 512], F32) for i in range(2)]
        hb = [Buf(f"p1_h{i}") for i in range(2)]
        xn = [sb(f"xn{i}", [128, KD, 512], BF16) for i in range(2)]
        xnb = [Buf(f"p1_xn{i}") for i in range(2)]
        sq = sb("sq", [128, KD, 512], BF16)
        sqb = Buf("p1_sq")
        rstd = sb("rstd", [128, 512], F32)
        rstdb = Buf("p1_rstd")
        fam = {n_: sb("fam_" + n_, [128, 4, 512], BF16) for n_ in ("qa", "ka", "qb", "kb", "va", "vb")}
        famb = {n_: Buf("p1_fam_" + n_) for n_ in fam}
        ut = sb("ut", [128, 4, 512], F32)
        utb = Buf("p1_ut")
        v32 = sb("v32", [128, 4, 512], F32)
        v32b = [Buf(f"p1_v32_{i}") for i in range(4)]
        vn = sb("vn", [128, 2, 512], BF16)
        vnb = [Buf(f"p1_vn{i}") for i in range(2)]
        stt = sb("stt", [128, 4, 16], F32)
        sttb = Buf("p1_stt")
        yc = sb("yc", [128, 4, 512], BF16)
        ycb = Buf("p1_yc")
        fl = sb("fl", [8, 512], F32)
        flb = Buf("p1_fl")
        lc = sb("lc", [8, 512], F32)
        lcx = fl
        carry = sb("carry", [8, 1], F32)
        lcb = Buf("p1_lc")
        pcs = sb("pc", [8, 6, 512], BF16)
        pcsb = Buf("p1_pcs")

        groups = [(b, g) for b in range(NB) for g in range(NG)]

        def load_h(gi):
            b, g = groups[gi]
            tr.dma("SP", ht[gi % 2][:], c.hT[b].rearrange("(k p) t -> p k t", p=128)[:, :, g * 512:(g + 1) * 512], writes=[hb[gi % 2]])

        def norm(gi):
            emit_rmsnorm(c, ht[gi % 2], hb[gi % 2], xn[gi % 2], xnb[gi % 2], gcol, sq, sqb, rstd, rstdb, 0)

        load_h(0)
        norm(0)
        ifm = 0
        itm = 0
        ibank_fm = 0
        ibank_tm = 0
        for gi, (b, g) in enumerate(groups):
            XN, XNB = xn[gi % 2], xnb[gi % 2]
            tsl = slice(g * 512, (g + 1) * 512)
            if gi + 1 < len(groups):
                load_h(gi + 1)
            if g == 0:
                tr.op("DVE", lambda e: e.memset(carry[:], 0.0), reads=[lcb], writes=[lcb])
            tr.dma("SP", c.xnT[b].rearrange("(k p) t -> p k t", p=128)[:, :, tsl], XN[:], reads=[XNB])
            fm_jobs = []
            for j in range(4):
                fm_jobs.append((C_QA + j * 128, "qa", 0.125, j))
            for j in range(4):
                fm_jobs.append((C_KA + j * 128, "ka", 1.0, j))
            for j in range(4):
                fm_jobs.append((C_QB + j * 128, "qb", 0.125, j))
            for j in range(4):
                fm_jobs.append((C_KB + j * 128, "kb", 1.0, j))
            for j in range(4):
                fm_jobs.append((C_U + j * 128, "gelu", 1.0, j))
            for (col, kind, scale, j) in fm_jobs:
                pi = 1 + (ibank_fm % 2)
                ibank_fm += 1
                P, PB = c.ps[pi], c.psb[pi]
                tr.mm_group([(lambda e, k=k, P=P, col=col, XN=XN: e.matmul(
                    P[:], W[:, k, col:col + 128], XN[:, k, :], start=(k == 0), stop=(k == KD - 1))) for k in range(KD)],
                    reads=[WB, XNB], writes=[PB])
                if kind == "gelu":
                    tr.op("ACT", lambda e, P=P, j=j: e.activation(out=ut[:, j, :], in_=P[:], func=AF.Gelu),
                          reads=[PB], writes=[utb])
                    continue
                F, FB = fam[kind], famb[kind]
                if ifm % 2 == 0:
                    tr.op("ACT", lambda e, P=P, F=F, scale=scale, j=j: e.activation(out=F[:, j, :], in_=P[:], func=AF.Copy, scale=scale),
                          reads=[PB], writes=[FB])
                else:
                    tr.op("DVE", lambda e, P=P, F=F, scale=scale, j=j: e.tensor_scalar(
                        out=F[:, j, :], in0=P[:], scalar1=scale, scalar2=None, op0=ALU.mult), reads=[PB], writes=[FB])
                ifm += 1
                if j == 3:
                    if kind in ("qa", "ka"):
                        T = c.QaT if kind == "qa" else c.KaT
                        tr.dma("SP", T[b].rearrange("(j p) t -> p j t", p=128)[:, :, tsl], F[:], reads=[FB])
                    else:
                        T = c.QbT if kind == "qb" else c.KbT
                        Tv = T[b].rearrange("(j hh) r t -> hh j r t", hh=2)
                        for hh in range(2):
                            tr.dma("SP", Tv[hh, :, 0:64, tsl].rearrange("j d t -> d j t"), F[hh * 64:(hh + 1) * 64, :, :], reads=[FB])
            PF, PFB = c.ps[5], c.psb[5]
            tr.mm_group([(lambda e, k=k, XN=XN: e.matmul(PF[0:8, :], W[:, k, C_F:C_F + 8], XN[:, k, :], start=(k == 0), stop=(k == KD - 1)))
                         for k in range(KD)], reads=[WB, XNB], writes=[PFB])
            tr.op("DVE", lambda e: e.tensor_scalar(out=fl[:], in0=PF[0:8, :], scalar1=foxb[:, 0:1], scalar2=None, op0=ALU.add),
                  reads=[PFB, lb], writes=[flb])
            for tb in range(4):
                bsl = slice(tb * 128, (tb + 1) * 128)
                for (col, kind, dstT) in ((C_VA, "va", c.Va), (C_VB, "vb", c.Vb), (C_V, "gelu", None)):
                    pi = 3 + (ibank_tm % 2)
                    ibank_tm += 1
                    P, PB = c.ps[pi], c.psb[pi]
                    tr.mm_group([(lambda e, k=k, P=P, col=col, XN=XN, bsl=bsl: e.matmul(
                        P[:], XN[:, k, bsl], W[:, k, col:col + 512], start=(k == 0), stop=(k == KD - 1))) for k in range(KD)],
                        reads=[WB, XNB], writes=[PB])
                    if kind == "gelu":
                        tr.op("ACT", lambda e, P=P, tb=tb: e.activation(out=v32[:, tb, :], in_=P[:], func=AF.Gelu),
                              reads=[PB], writes=[v32b[tb]])
                        tr.op("DVE", lambda e, tb=tb: e.bn_stats(out=stt[:, tb, 0:6], in_=v32[:, tb, :]), reads=[v32b[tb]], writes=[sttb])
                        tr.op("DVE", lambda e, tb=tb: e.bn_aggr(out=stt[:, tb, 6:8], in_=stt[:, tb, 0:6]), reads=[sttb], writes=[sttb])
                    else:
                        T_, TB_ = fam[kind], famb[kind]
                        if itm % 2 == 0:
                            tr.op("DVE", lambda e, P=P, T_=T_, tb=tb: e.tensor_copy(out=T_[:, tb, :], in_=P[:]), reads=[PB], writes=[TB_])
                        else:
                            tr.op("ACT", lambda e, P=P, T_=T_, tb=tb: e.activation(out=T_[:, tb, :], in_=P[:], func=AF.Copy), reads=[PB], writes=[TB_])
                        itm += 1
                        if tb == 3:
                            tr.dma("SP", dstT[b, g * 512:(g + 1) * 512, :].rearrange("(tb p) c -> p tb c", p=128), T_[:], reads=[TB_])
            if gi + 1 < len(groups):
                norm(gi + 1)
            tr.op("ACT", lambda e: e.activation(out=stt[:, :, 8:9], in_=stt[:, :, 7:8], func=AF.Ln, bias=c.eps_col[:, 0:1]),
                  reads=[sttb, c.b_const], writes=[sttb])
            tr.op("ACT", lambda e: e.activation(out=stt[:, :, 9:10], in_=stt[:, :, 8:9], func=AF.Exp, scale=-0.5),
                  reads=[sttb], writes=[sttb])
            tr.op("DVE", lambda e: e.scalar_tensor_tensor(out=stt[:, :, 10:11], in0=stt[:, :, 6:7], scalar=-1.0, in1=stt[:, :, 9:10],
                                                          op0=ALU.mult, op1=ALU.mult), reads=[sttb], writes=[sttb])
            tr.op("ACT", lambda e: e.activation(out=fl[:], in_=fl[:], func=AF.Exp, scale=-1.0), reads=[flb], writes=[flb])
            tr.op("ACT", lambda e: e.activation(out=fl[:], in_=fl[:], func=AF.Ln, bias=c.one_col[0:8, 0:1]), reads=[flb, c.b_const], writes=[flb])
            for tb in range(4):
                bsl = slice(tb * 128, (tb + 1) * 128)
                tr.op("ACT", lambda e, tb=tb: e.activation(out=v32[:, tb, :], in_=v32[:, tb, :], func=AF.Identity,
                                                          scale=stt[:, tb, 9:10], bias=stt[:, tb, 10:11]),
                      reads=[sttb, v32b[tb]], writes=[v32b[tb]])
                tr.op("DVE", lambda e, tb=tb: e.tensor_tensor(out=vn[:, tb % 2, :], in0=v32[:, tb, :], in1=lng[:], op=ALU.mult),
                      reads=[v32b[tb], lb], writes=[vnb[tb % 2]])
                PM, PMB = c.ps[6 + tb % 2], c.psb[6 + tb % 2]
                fns = []
                for gg in range(8):
                    cc, hh = gg // 2, gg % 2
                    o = PM[hh * 64:(hh + 1) * 64, cc * 128:(cc + 1) * 128]
                    fns.append(lambda e, o=o, gg=gg, tb=tb: e.matmul(o, vn[:, tb % 2, gg * 64:(gg + 1) * 64], wst[:, gg, :], start=True, stop=False))
                    fns.append(lambda e, o=o, gg=gg: e.matmul(o, c.ones_bf[0:1, 0:64], sgub_hi[0:1, gg * 128:(gg + 1) * 128], start=False, stop=False))
                    fns.append(lambda e, o=o, gg=gg: e.matmul(o, c.ones_bf[0:1, 0:64], sgub_lo[0:1, gg * 128:(gg + 1) * 128], start=False, stop=True))
                tr.mm_group(fns, reads=[vnb[tb % 2], wstb, sgubb, c.b_const], writes=[PMB])
                tr.op("DVE", lambda e, PM=PM, bsl=bsl: e.tensor_tensor(
                    out=yc[:, :, bsl], in0=PM[:].rearrange("p (c t) -> p c t", c=4), in1=ut[:, :, bsl], op=ALU.mult),
                    reads=[PMB, utb], writes=[ycb])
            tr.dma("SP", c.YcT[b].rearrange("(c p) t -> p c t", p=128)[:, :, tsl], yc[:], reads=[ycb])
            tr.op("DVE", lambda e: e.tensor_tensor_scan(out=lc[:], data0=ones8[:], data1=fl[:], initial=carry[:, 0:1], op0=ALU.mult, op1=ALU.add),
                  reads=[flb, lcb, lb], writes=[lcb])
            tr.op("DVE", lambda e: e.tensor_copy(out=carry[:], in_=lc[:, 511:512]), reads=[lcb], writes=[lcb])
            tr.op("DVE", lambda e: e.tensor_scalar(out=lcx[:], in0=lc[:], scalar1=-1.0, scalar2=None, op0=ALU.mult), reads=[lcb, flb], writes=[lcb, flb])
            for j in range(3):
                tr.op("DVE", lambda e, j=j: e.tensor_copy(out=pcs[:, j, :], in_=lcx[:]), reads=[lcb, flb, pcsb], writes=[pcsb])
                tr.op("DVE", lambda e, j=j: e.tensor_scalar(out=pcs[:, 3 + j, :], in0=pcs[:, j, :], scalar1=-1.0, scalar2=None, op0=ALU.mult),
                      reads=[pcsb], writes=[pcsb])
                if j < 2:
                    tr.op("DVE", lambda e, j=j: e.tensor_tensor(out=lcx[:], in0=lcx[:], in1=pcs[:, j, :], op=ALU.subtract),
                          reads=[lcb, flb, pcsb], writes=[lcb, flb])
            tr.dma("SP", c.QbT[b, :, 64:67, tsl], pcs[:, 0:3, :], reads=[pcsb])
            tr.dma("SP", c.KbT[b, :, 67:70, tsl], pcs[:, 3:6, :], reads=[pcsb])


def phase_attn(c, l, kind):
    nc, tr = c.nc, c.tr
    NB, S, NG, NBLK = c.NB, c.S, c.NG, c.NBLK
    sb_mode = (kind == "sb")
    KR = 64 if sb_mode else AUG
    with ExitStack() as st:
        def sbt(name, shape, dt):
            return st.enter_context(nc.sbuf_tensor(uq(f"p{kind}_" + name), list(shape), dt))
        Vts = [sbt(f"v{i}", [128, NBLK, 512] if sb_mode else [128, NBLK, NH, 128], BF16) for i in range(2)]
        VBs = [Buf(f"at_v{i}") for i in range(2)]
        KT = [sbt(f"k{i}", [KR, S], BF16) for i in range(2)]
        QT = [sbt(f"q{i}", [KR, S], BF16) for i in range(2)]
        KQB = [Buf(f"at_kq{i}") for i in range(2)]
        NR = 3
        E32 = [sbt(f"e{i}", [128, 512], F32) for i in range(NR)] if sb_mode else None
        LS = [sbt(f"ls{i}", [128, 512], BF16) for i in range(NR)] if sb_mode else None
        LSB = [Buf(f"at_ls{i}") for i in range(NR)]
        SS = [sbt(f"ss{i}", [128, 512], BF16) for i in range(2)] if sb_mode else None
        SSB = [Buf(f"at_ss{i}") for i in range(2)]
        WT = [sbt(f"w{i}", [128, 512], BF16) for i in range(NR)]
        WTB = [Buf(f"at_w{i}") for i in range(NR)]
        YS = [sbt(f"y{i}", [64, 512], BF16) for i in range(2)]
        YSB = [Buf(f"at_y{i}") for i in range(2)]
        RC = [sbt(f"r{i}", [128, 512], F32) for i in range(2)] if not sb_mode else None
        RC0 = [sbt(f"r0_{i}", [64, 512], F32) for i in range(2)] if not sb_mode else None
        RCB = [Buf(f"at_r{i}") for i in range(2)]
        RC0B = [Buf(f"at_r0_{i}") for i in range(2)]
        nbias = c.nb_sb if sb_mode else c.nb_fox
        Ksrc = c.KaT if sb_mode else None
        dstT = c.YaT if sb_mode else c.YbT
        Vsrc = c.Va if sb_mode else c.Vb

        units = []
        ihead = 0
        igroup = 0
        for b in range(NB):
            for h in range(NH):
                for g in range(NG):
                    kbs = list(range(4 * g + 3, -1, -1)) if sb_mode else list(range(0, 4 * g + 4))
                    for j, kb in enumerate(kbs):
                        units.append(dict(b=b, h=h, g=g, kb=kb, j=j, first=(j == 0), last=(j == len(kbs) - 1),
                                          ihead=ihead, igroup=igroup, newhead=(g == 0 and j == 0), newbatch=(h == 0 and g == 0 and j == 0)))
                    igroup += 1
                ihead += 1
        n = len(units)

        def stage_A(i):
            u = units[i]
            b, h, g, kb = u["b"], u["h"], u["g"], u["kb"]
            if u["newbatch"]:
                if sb_mode:
                    tr.dma("SP", Vts[b % 2][:], Vsrc[b].rearrange("(n p) c -> p n c", p=128), writes=[VBs[b % 2]])
                else:
                    tr.op("POOL", lambda e: e.memset(Vts[b % 2][:], 1.0), writes=[VBs[b % 2]])
                    vsrc4 = Vsrc[b].rearrange("(n p) (h d) -> p n h d", p=128, d=64)
                    for n0 in range(NBLK):
                        tr.dma("SP", Vts[b % 2][:, n0, :, 0:64], vsrc4[:, n0, :, :], writes=[VBs[b % 2]], part=(n0 > 0))
            if u["newhead"]:
                hb_ = u["ihead"] % 2
                if sb_mode:
                    tr.dma("SP", KT[hb_][:], c.KaT[b, h * 64:(h + 1) * 64, :], writes=[KQB[hb_]])
                    tr.dma("SP", QT[hb_][:], c.QaT[b, h * 64:(h + 1) * 64, :], writes=[KQB[hb_]])
                else:
                    tr.dma("SP", KT[hb_][:], c.KbT[b, h, :, :], writes=[KQB[hb_]])
                    tr.dma("SP", QT[hb_][:], c.QbT[b, h, :, :], writes=[KQB[hb_]])
            hb_ = u["ihead"] % 2
            c0 = max(0, kb - 4 * g) * 128
            diag = kb >= 4 * g
            u["c0"], u["diag"] = c0, diag
            PZ, PZB = c.ps[i % 4], c.psb[i % 4]
            u["PZ"], u["PZB"] = PZ, PZB
            fns = [lambda e: e.matmul(PZ[:, c0:512], KT[hb_][:, kb * 128:(kb + 1) * 128], QT[hb_][:, g * 512 + c0:(g + 1) * 512],
                                      start=True, stop=True)]
            if diag:
                fns.append(lambda e: e.matmul(PZ[:, c0:c0 + 128], c.ident_bf[:], nbias[:], start=False, stop=True, skip_group_check=True))
            tr.mm_group(fns, reads=[KQB[hb_], c.b_const], writes=[PZB])

        def stage_B(i):
            u = units[i]
            c0, PZ, PZB = u["c0"], u["PZ"], u["PZB"]
            r = i % NR
            if sb_mode:
                tr.op("ACT", lambda e: e.activation(out=E32[r][:, c0:512], in_=PZ[:, c0:512], func=AF.Exp), reads=[PZB], writes=[LSB[r]])
                tr.op("ACT", lambda e: e.activation(out=LS[r][:, c0:512], in_=E32[r][:, c0:512], func=AF.Ln, bias=c.one_col[:, 0:1]),
                      reads=[LSB[r], c.b_const], writes=[LSB[r]])
            else:
                tr.op("ACT", lambda e: e.activation(out=WT[r][:, c0:512], in_=PZ[:, c0:512], func=AF.Exp), reads=[PZB], writes=[WTB[r]])

        def stage_C(i):
            if not sb_mode:
                return
            u = units[i]
            c0, PZ, PZB, j = u["c0"], u["PZ"], u["PZB"], u["j"]
            r = i % NR
            if u["first"]:
                for k2 in range(2):
                    tr.op("POOL", lambda e, k2=k2: e.memset(SS[k2][:], 0.0), writes=[SSB[k2]])
            fns = [lambda e: e.matmul(PZ[:, c0:512], c.negtri_bf[:, 0, :], LS[r][:, c0:512], start=False, stop=True, skip_group_check=True)]
            reads = [LSB[r], c.b_const]
            if j > 0:
                fns.append(lambda e: e.matmul(PZ[:, c0:512], c.negtri_bf[:, 1, :], SS[j % 2][:, c0:512], start=False, stop=True, skip_group_check=True))
                reads.append(SSB[j % 2])
            tr.mm_group(fns, reads=reads, writes=[PZB])
            if not u["last"]:
                tr.op("DVE", lambda e: e.tensor_tensor(out=SS[(j + 1) % 2][:, c0:512], in0=SS[j % 2][:, c0:512], in1=LS[r][:, c0:512], op=ALU.add),
                      reads=[SSB[j % 2], LSB[r]], writes=[SSB[(j + 1) % 2]])
            tr.op("ACT", lambda e: e.activation(out=WT[r][:, c0:512], in_=PZ[:, c0:512], func=AF.Exp), reads=[PZB], writes=[WTB[r]])

        def stage_F(i):
            u = units[i]
            b, h, g, kb, c0 = u["b"], u["h"], u["g"], u["kb"], u["c0"]
            r = i % NR
            gi = u["igroup"] % 2
            PO, POB = c.ps[4 + gi], c.psb[4 + gi]
            Vt, VB = Vts[b % 2], VBs[b % 2]
            if sb_mode:
                fns = [lambda e: e.matmul(PO[0:64, c0:512], Vt[:, kb, h * 64:(h + 1) * 64], WT[r][:, c0:512], start=u["first"], stop=True,
                                          skip_group_check=True)]
            else:
                fns = [lambda e: e.matmul(PO[:, c0:512], Vt[:, kb, h, :], WT[r][:, c0:512], start=u["first"], stop=True, skip_group_check=True)]
            tr.mm_group(fns, reads=[VB, WTB[r], c.b_const], writes=[POB])
            if u["last"]:
                Y, YB = YS[gi], YSB[gi]
                if sb_mode:
                    tr.op("DVE", lambda e: e.tensor_copy(out=Y[:], in_=PO[0:64, :]), reads=[POB], writes=[YB])
                else:
                    tr.op("DVE", lambda e: e.reciprocal(out=RC[gi][64:128, :], in_=PO[64:128, :]), reads=[POB], writes=[RCB[gi]])
                    tr.dma("SP", RC0[gi][:], RC[gi][64:128, :], reads=[RCB[gi]], writes=[RC0B[gi]])
                    tr.op("DVE", lambda e: e.tensor_tensor(out=Y[:], in0=PO[0:64, :], in1=RC0[gi][:], op=ALU.mult),
                          reads=[POB, RC0B[gi]], writes=[YB])
                tr.dma("SP", dstT[b, h * 64:(h + 1) * 64, g * 512:(g + 1) * 512], Y[:], reads=[YB])

        for s_ in range(-1, n + 1):
            if 0 <= s_ + 1 < n:
                stage_A(s_ + 1)
                stage_B(s_ + 1)
            if 0 <= s_ < n:
                stage_C(s_)
            if 0 <= s_ - 1 < n:
                stage_F(s_ - 1) if sb_mode else None
            if not sb_mode and 0 <= s_ < n:
                stage_F(s_)


def phase_merge(c, l):
    nc, tr = c.nc, c.tr
    NB, S, NG = c.NB, c.S, c.NG
    with ExitStack() as st:
        def sb(name, shape, dt):
            return st.enter_context(nc.sbuf_tensor(uq("p5_" + name), list(shape), dt))
        WG = sb("wg", [128, KD, 3 * D], BF16)
        WBR = sb("wbr", [128, 12, D], BF16)
        WO = sb("wo", [128, KD, D], BF16)
        WB = Buf("p5_w")
        load_w_cast(c, WG, WB, c.w_in[l], C_G, 3 * D, KD)
        load_w_cast(c, WBR, WB, c.w_branch[l].rearrange("i k n -> (i k) n"), 0, D, 12)
        load_w_cast(c, WO, WB, c.w_out[l], 0, D, KD)
        xn = [sb(f"xn{i}", [128, KD, 512], BF16) for i in range(2)]
        ys = [sb(f"y{i}", [128, 12, 512], BF16) for i in range(2)]
        inb = [Buf(f"p5_in{i}") for i in range(2)]
        H = sb("h", [128, KD, 512], F32)
        HB = Buf("p5_h")
        MT = sb("mt", [128, KD, 512], BF16)
        MTB = Buf("p5_mt")
        G = [sb(f"g{i}", [128, 512], F32) for i in range(2)]
        GB = [Buf(f"p5_g{i}") for i in range(2)]
        T = [sb(f"t{i}", [128, 512], F32) for i in range(3)]
        TB = [Buf(f"p5_t{i}") for i in range(3)]
        M = [sb(f"m{i}", [128, 512], F32) for i in range(2)]
        MB = [Buf(f"p5_m{i}") for i in range(2)]
        ysrc = (c.YaT, c.YbT, c.YcT)

        def load_inputs(b, g, gi):
            tsl = slice(g * 512, (g + 1) * 512)
            tr.dma("SP", xn[gi % 2][:], c.xnT[b].rearrange("(k p) t -> p k t", p=128)[:, :, tsl], writes=[inb[gi % 2]])
            for i in range(3):
                tr.dma("SP", ys[gi % 2][:, i * 4:(i + 1) * 4, :], ysrc[i][b].rearrange("(k p) t -> p k t", p=128)[:, :, tsl], writes=[inb[gi % 2]])

        groups = [(b, g) for b in range(NB) for g in range(NG)]
        load_inputs(*groups[0], 0)
        ia = 0
        it = 0
        for gi, (b, g) in enumerate(groups):
            tsl = slice(g * 512, (g + 1) * 512)
            if gi + 1 < len(groups):
                load_inputs(*groups[gi + 1], gi + 1)
            XN, YS, INB = xn[gi % 2], ys[gi % 2], inb[gi % 2]
            hT_v = c.hT[b].rearrange("(k p) t -> p k t", p=128)
            tr.dma("SP", H[:], hT_v[:, :, tsl], writes=[HB])
            for cc in range(KD):
                csl = slice(cc * 128, (cc + 1) * 128)
                for i in range(3):
                    PA, PAB = c.ps[ia % 2], c.psb[ia % 2]
                    PBk, PBB = c.ps[2 + ia % 2], c.psb[2 + ia % 2]
                    Gt, GtB = G[ia % 2], GB[ia % 2]
                    tr.mm_group([(lambda e, k=k, PA=PA, i=i, csl=csl, XN=XN: e.matmul(
                        PA[:], WG[:, k, i * D + csl.start:i * D + csl.stop], XN[:, k, :], start=(k == 0), stop=(k == KD - 1))) for k in range(KD)],
                        reads=[WB, INB], writes=[PAB])
                    tr.mm_group([(lambda e, k=k, PBk=PBk, i=i, csl=csl, YS=YS: e.matmul(
                        PBk[:], WBR[:, i * 4 + k, csl], YS[:, i * 4 + k, :], start=(k == 0), stop=(k == 3))) for k in range(4)],
                        reads=[WB, INB], writes=[PBB])
                    tr.op("ACT", lambda e, PA=PA, Gt=Gt: e.activation(out=Gt[:], in_=PA[:], func=AF.Sigmoid), reads=[PAB], writes=[GtB])
                    Tt, TtB = T[it % 3], TB[it % 3]
                    it += 1
                    tr.op("DVE", lambda e, PBk=PBk, Gt=Gt, Tt=Tt: e.tensor_tensor(out=Tt[:], in0=PBk[:], in1=Gt[:], op=ALU.mult),
                          reads=[PBB, GtB], writes=[TtB])
                    if i == 0:
                        T0, T0B = Tt, TtB
                    elif i == 1:
                        Mt, MtB = M[cc % 2], MB[cc % 2]
                        tr.op("POOL", lambda e, Mt=Mt, T0=T0, Tt=Tt: e.tensor_tensor(out=Mt[:], in0=T0[:], in1=Tt[:], op=ALU.add),
                              reads=[T0B, TtB], writes=[MtB])
                    else:
                        tr.op("POOL", lambda e, Mt=Mt, Tt=Tt, cc=cc: e.tensor_tensor(out=MT[:, cc, :], in0=Mt[:], in1=Tt[:], op=ALU.add),
                              reads=[MtB, TtB], writes=[MTB])
                    ia += 1
            for cc in range(KD):
                csl = slice(cc * 128, (cc + 1) * 128)
                PO, POB = c.ps[4 + cc % 2], c.psb[4 + cc % 2]
                tr.mm_group([(lambda e, k=k, PO=PO, csl=csl: e.matmul(PO[:], WO[:, k, csl], MT[:, k, :], start=(k == 0), stop=(k == KD - 1)))
                             for k in range(KD)], reads=[WB, MTB], writes=[POB])
                tr.op("DVE", lambda e, PO=PO, cc=cc: e.tensor_tensor(out=H[:, cc, :], in0=PO[:], in1=H[:, cc, :], op=ALU.add),
                      reads=[POB, HB], writes=[HB])
            tr.dma("SP", hT_v[:, :, tsl], H[:], reads=[HB])


def phase_ffn_up(c, l):
    nc, tr = c.nc, c.tr
    NB, S, NG = c.NB, c.S, c.NG
    with ExitStack() as st:
        def sb(name, shape, dt):
            return st.enter_context(nc.sbuf_tensor(uq("p6_" + name), list(shape), dt))
        WU = sb("wu", [128, KD, 2 * D_FF], BF16)
        WB = Buf("p6_w")
        load_w_cast(c, WU, WB, c.w_up[l], 0, 2 * D_FF, KD)
        gcol = c.smallv[:, c.o_gffn + l * KD:c.o_gffn + (l + 1) * KD]
        ht = [sb(f"h{i}", [128, KD, 512], F32) for i in range(2)]
        hb = [Buf(f"p6_h{i}") for i in range(2)]
        XN = sb("xn", [128, KD, 512], BF16)
        XNB = Buf("p6_xn")
        sq = sb("sq", [128, KD, 512], BF16)
        sqb = Buf("p6_sq")
        rstd = sb("rstd", [128, 512], F32)
        rstdb = Buf("p6_rstd")
        NR = 3
        GT = [sb(f"gt{i}", [128, 514], F32) for i in range(NR)]
        GTB = [Buf(f"p6_gt{i}") for i in range(NR)]
        A0 = [sb(f"a0_{i}", [128, 512], F32) for i in range(NR)]
        A0B = [Buf(f"p6_a0_{i}") for i in range(NR)]
        A1 = [sb(f"a1_{i}", [128, 512], F32) for i in range(NR)]
        A1B = [Buf(f"p6_a1_{i}") for i in range(NR)]
        HM = [sb(f"hm{i}", [128, 512], BF16) for i in range(NR)]
        HMB = [Buf(f"p6_hm{i}") for i in range(NR)]
        HALO = sb("halo", [128, NFF, 2], F32)
        HALOB = [Buf(f"p6_halo{j}") for j in range(NFF)]
        groups = [(b, g) for b in range(NB) for g in range(NG)]
        tr.dma("SP", ht[0][:], c.hT[groups[0][0]].rearrange("(k p) t -> p k t", p=128)[:, :, 0:512], writes=[hb[0]])
        it = 0
        for gi, (b, g) in enumerate(groups):
            tsl = slice(g * 512, (g + 1) * 512)
            if gi + 1 < len(groups):
                b2, g2 = groups[gi + 1]
                tr.dma("SP", ht[(gi + 1) % 2][:], c.hT[b2].rearrange("(k p) t -> p k t", p=128)[:, :, g2 * 512:(g2 + 1) * 512],
                       writes=[hb[(gi + 1) % 2]])
            H, HB = ht[gi % 2], hb[gi % 2]
            emit_rmsnorm(c, H, HB, XN, XNB, gcol, sq, sqb, rstd, rstdb, 0)
            if g == 0:
                tr.op("POOL", lambda e: e.memset(HALO[:], 0.0), writes=HALOB)
            for j in range(NFF):
                r = it % NR
                it += 1
                PG, PGB = c.ps[1 + (j % 2)], c.psb[1 + (j % 2)]
                PV, PVB = c.ps[3 + (j % 2)], c.psb[3 + (j % 2)]
                tr.mm_group([(lambda e, k=k, PG=PG, j=j: e.matmul(PG[:], WU[:, k, j * 128:(j + 1) * 128], XN[:, k, :],
                                                                  start=(k == 0), stop=(k == KD - 1))) for k in range(KD)],
                            reads=[WB, XNB], writes=[PGB])
                tr.mm_group([(lambda e, k=k, PV=PV, j=j: e.matmul(PV[:], WU[:, k, D_FF + j * 128:D_FF + (j + 1) * 128], XN[:, k, :],
                                                                  start=(k == 0), stop=(k == KD - 1))) for k in range(KD)],
                            reads=[WB, XNB], writes=[PVB])
                cw = lambda i_, j=j: c.smallv[:, c.o_cw + l * 3 * NFF + i_ * NFF + j:c.o_cw + l * 3 * NFF + i_ * NFF + j + 1]
                cb = c.smallv[:, c.o_cb + l * NFF + j:c.o_cb + l * NFF + j + 1]
                tr.op("ACT", lambda e, r=r, PG=PG: e.activation(out=GT[r][:, 2:514], in_=PG[:], func=AF.Copy), reads=[PGB], writes=[GTB[r]])
                tr.op("POOL", lambda e, r=r, j=j: e.tensor_copy(out=GT[r][:, 0:2], in_=HALO[:, j, :]), reads=[HALOB[j], GTB[r]], writes=[GTB[r]])
                tr.op("POOL", lambda e, r=r, j=j: e.tensor_copy(out=HALO[:, j, :], in_=GT[r][:, 512:514]), reads=[GTB[r]], writes=[HALOB[j]])
                tr.op("ACT", lambda e, r=r, PG=PG, cw=cw, cb=cb: e.activation(out=A0[r][:], in_=PG[:], func=AF.Identity, scale=cw(2), bias=cb),
                      reads=[PGB, c.b_const], writes=[A0B[r]])
                tr.op("DVE", lambda e, r=r, cw=cw: e.scalar_tensor_tensor(out=A1[r][:], in0=GT[r][:, 1:513], scalar=cw(1), in1=A0[r][:],
                                                                          op0=ALU.mult, op1=ALU.add),
                      reads=[GTB[r], A0B[r], c.b_const], writes=[A1B[r]])
                tr.op("DVE", lambda e, r=r, cw=cw: e.scalar_tensor_tensor(out=A0[r][:], in0=GT[r][:, 0:512], scalar=cw(0), in1=A1[r][:],
                                                                          op0=ALU.mult, op1=ALU.add),
                      reads=[GTB[r], A1B[r], c.b_const], writes=[A0B[r]])
                tr.op("ACT", lambda e, r=r: e.activation(out=A1[r][:], in_=A0[r][:], func=AF.Gelu), reads=[A0B[r]], writes=[A1B[r]])
                tr.op("DVE", lambda e, r=r, PV=PV: e.tensor_tensor(out=HM[r][:], in0=PV[:], in1=A1[r][:], op=ALU.mult),
                      reads=[PVB, A1B[r]], writes=[HMB[r]])
                tr.dma("SP", c.hmid[b, j * 128:(j + 1) * 128, tsl], HM[r][:], reads=[HMB[r]])


def phase_ffn_down(c, l):
    nc, tr = c.nc, c.tr
    NB, S, NG = c.NB, c.S, c.NG
    with ExitStack() as st:
        def sb(name, shape, dt):
            return st.enter_context(nc.sbuf_tensor(uq("p6b_" + name), list(shape), dt))
        WD = sb("wd", [128, NFF, D], BF16)
        WB = Buf("p7_w")
        load_w_cast(c, WD, WB, c.w_down[l], 0, D, NFF)
        hm = [sb(f"hm{i}", [128, NFF, 512], BF16) for i in range(2)]
        hmb = [Buf(f"p7_hm{i}") for i in range(2)]
        ht = [sb(f"h{i}", [128, KD, 512], F32) for i in range(2)]
        hb = [Buf(f"p7_h{i}") for i in range(2)]
        groups = [(b, g) for b in range(NB) for g in range(NG)]

        def load(gi):
            b, g = groups[gi]
            tsl = slice(g * 512, (g + 1) * 512)
            tr.dma("SP", hm[gi % 2][:], c.hmid[b].rearrange("(k p) t -> p k t", p=128)[:, :, tsl], writes=[hmb[gi % 2]])
            tr.dma("SP", ht[gi % 2][:], c.hT[b].rearrange("(k p) t -> p k t", p=128)[:, :, tsl], writes=[hb[gi % 2]])
        load(0)
        for gi, (b, g) in enumerate(groups):
            tsl = slice(g * 512, (g + 1) * 512)
            if gi + 1 < len(groups):
                load(gi + 1)
            HMt, HMtB, H, HB = hm[gi % 2], hmb[gi % 2], ht[gi % 2], hb[gi % 2]
            for cc in range(KD):
                csl = slice(cc * 128, (cc + 1) * 128)
                PO, POB = c.ps[cc % 4], c.psb[cc % 4]
                tr.mm_group([(lambda e, k=k, PO=PO, csl=csl, HMt=HMt: e.matmul(PO[:], WD[:, k, csl], HMt[:, k, :], start=(k == 0), stop=(k == NFF - 1)))
                             for k in range(NFF)], reads=[WB, HMtB], writes=[POB])
                tr.op("DVE", lambda e, PO=PO, cc=cc, H=H: e.tensor_tensor(out=H[:, cc, :], in0=PO[:], in1=H[:, cc, :], op=ALU.add),
                      reads=[POB, HB], writes=[HB])
            tr.dma("SP", c.hT[b].rearrange("(k p) t -> p k t", p=128)[:, :, tsl], H[:], reads=[HB])


def phase_final(c):
    nc, tr = c.nc, c.tr
    with ExitStack() as st:
        def sb(name, shape, dt):
            return st.enter_context(nc.sbuf_tensor(name, list(shape), dt))
        gcol = sb("p7_g", [128, KD], F32)
        gb = Buf("p7_g")
        tr.dma("SP", gcol[:], c.final_g.rearrange("(k p) -> p k", p=128), writes=[c.b_const], allow_slow_non_contiguous=True)
        ht = [sb(f"p7_h{i}", [128, KD, 512], F32) for i in range(2)]
        hb = [Buf(f"p7_h{i}") for i in range(2)]
        xn = [sb(f"p7_xn{i}", [128, KD, 512], F32) for i in range(2)]
        xnb = [Buf(f"p7_xn{i}") for i in range(2)]
        sq = sb("p7_sq", [128, KD, 512], BF16)
        sqb = Buf("p7_sq")
        rstd = sb("p7_rstd", [128, 512], F32)
        rstdb = Buf("p7_rstd")
        ot = [sb(f"p7_o{i}", [128, D], F32) for i in range(2)]
        ob = [Buf(f"p7_o{i}") for i in range(2)]
        it = 0
        gi = 0
        for b in range(c.NB):
            hT_v = c.hT[b].rearrange("(k p) t -> p k t", p=128)
            for g in range(c.NG):
                H, HB = ht[gi % 2], hb[gi % 2]
                XN, XNB = xn[gi % 2], xnb[gi % 2]
                tr.dma("SP", H[:], hT_v[:, :, g * 512:(g + 1) * 512], writes=[HB])
                emit_rmsnorm(c, H, HB, XN, XNB, gcol, sq, sqb, rstd, rstdb, 4)
                for tb in range(4):
                    O, OB = ot[it % 2], ob[it % 2]
                    for half in range(2):
                        pi = (it * 2 + half) % 4
                        P, PB = c.ps[pi], c.psb[pi]
                        tr.mm_group(
                            [(lambda e, j=j, P=P, XN=XN, half=half, tb=tb: e.transpose(
                                P[:, j * 128:(j + 1) * 128], XN[:, half * 4 + j, tb * 128:(tb + 1) * 128], c.ident[:]))
                             for j in range(4)],
                            reads=[XNB, c.b_const], writes=[PB])
                        if half == 0:
                            tr.op("ACT", lambda e, P=P, O=O, half=half: e.activation(
                                out=O[:, half * 512:(half + 1) * 512], in_=P[:], func=AF.Copy), reads=[PB], writes=[OB])
                        else:
                            tr.op("DVE", lambda e, P=P, O=O, half=half: e.tensor_copy(
                                out=O[:, half * 512:(half + 1) * 512], in_=P[:]), reads=[PB], writes=[OB])
                    t0 = g * 512 + tb * 128
                    tr.dma("SP", c.out[b, t0:t0 + 128, :], O[:], reads=[OB])
                    it += 1
                gi += 1


_CACHE = {}


def host_consts():
    ident = np.eye(128, dtype=np.float32)
    s = np.arange(128)[:, None]
    t = np.arange(128)[None, :]
    masks = np.zeros((128, 4, 128), np.float32)
    masks[:, 0, :] = (s < t)
    masks[:, 1, :] = (s <= t)
    masks[:, 2, :] = (s >= t)
    masks[:, 3, :] = 1.0
    return ident, masks


def kernel(**inputs):
    NCORES = 8
    x = np.ascontiguousarray(inputs["x"], dtype=np.float32)
    B, S, _ = x.shape
    NB = B // NCORES
    depth = inputs["w_in"].shape[0]
    key = (NB, S, depth)
    if key not in _CACHE:
        _CACHE[key] = build(NB, S, depth)
    nc = _CACHE[key]
    ident, masks = host_consts()
    shared = {k: np.ascontiguousarray(v, dtype=np.float32) for k, v in inputs.items() if k != "x"}
    shared["k_ident"] = ident
    shared["k_masks"] = masks
    in_maps = []
    for i in range(NCORES):
        m = dict(shared)
        m["x"] = x[i * NB:(i + 1) * NB]
        in_maps.append(m)
    res = run_bass_kernel_spmd(nc, in_maps, core_ids=list(range(NCORES)))
    return np.concatenate([r["out"] for r in res.results], axis=0)
```

```python
import numpy as np
import ml_dtypes
from contextlib import ExitStack
import concourse.bass as bass
import concourse.mybir as mybir
from concourse.bass_utils import run_bass_kernel_spmd

F32 = mybir.dt.float32
BF16 = mybir.dt.bfloat16
AF = mybir.ActivationFunctionType
ALU = mybir.AluOpType

D = 1024
KD = D // 128
HD = 64
NH = 8
IN_COLS = 7176
D_FF = 2816
NFF = D_FF // 128
EPS = 1e-6
AUG = 70
C_QA, C_KA, C_VA, C_QB, C_KB, C_VB, C_F, C_U, C_V, C_G = 0, 512, 1024, 1536, 2048, 2560, 3072, 3080, 3592, 4104


class Buf:
    __slots__ = ("name", "w", "r")

    def __init__(self, name):
        self.name = name
        self.w = {}
        self.r = {}


class EngState:
    def __init__(self, name, eng, sem):
        self.name = name
        self.eng = eng
        self.sem = sem
        self.key = "E_" + name
        self.count = 0
        self.waited = {}


class Tr:
    def __init__(self, nc, ndma=8):
        self.nc = nc
        self.E = {}
        for name, attr in (("PE", "tensor"), ("ACT", "scalar"), ("DVE", "vector"), ("POOL", "gpsimd"), ("SP", "sync")):
            self.E[name] = EngState(name, getattr(nc, attr), nc.alloc_semaphore("s_" + name))
        self.Q = {}
        for q in ("SP", "POOL"):
            sems = [nc.alloc_semaphore(f"d_{q}_{i}") for i in range(ndma)]
            self.Q[q] = dict(sems=sems, cnts=[0] * ndma, nxt=0)
        self.n_inst = 0

    def _wait(self, E, key, sem, val):
        if E.waited.get(key, 0) >= val:
            return
        E.eng.wait_ge(sem, val)
        E.waited[key] = val

    def _deps(self, E, reads, writes, skip_same_raw=False):
        for b in reads:
            for key, (sem, val) in b.w.items():
                if key == E.key and skip_same_raw:
                    continue
                self._wait(E, key, sem, val)
        for b in writes:
            for key, (sem, val) in b.w.items():
                if key == E.key and skip_same_raw:
                    continue
                self._wait(E, key, sem, val)
            for key, (sem, val) in b.r.items():
                if key == E.key:
                    continue
                self._wait(E, key, sem, val)

    def _stamp(self, key, sem, val, reads, writes):
        for b in reads:
            b.r[key] = (sem, val)
        for b in writes:
            b.w = {key: (sem, val)}
            b.r = {}

    def op(self, eng, fn, reads=(), writes=()):
        E = self.E[eng]
        self._deps(E, reads, writes, skip_same_raw=(eng == "PE"))
        ins = fn(E.eng)
        E.count += 1
        ins.then_inc(E.sem, 1)
        self._stamp(E.key, E.sem, E.count, reads, writes)
        self.n_inst += 1
        return ins

    def mm_group(self, fns, reads=(), writes=()):
        E = self.E["PE"]
        self._deps(E, reads, writes, skip_same_raw=True)
        ins = None
        for fn in fns:
            ins = fn(E.eng)
            self.n_inst += 1
        E.count += 1
        ins.then_inc(E.sem, 1)
        self._stamp(E.key, E.sem, E.count, reads, writes)

    def dma(self, q, out, in_, reads=(), writes=(), part=False, **kw):
        Q = self.Q[q]
        E = self.E[q]
        slot = Q["nxt"]
        Q["nxt"] = (slot + 1) % len(Q["sems"])
        sem = Q["sems"][slot]
        key = f"D_{q}_{slot}"
        if Q["cnts"][slot] > 0:
            self._wait(E, key, sem, Q["cnts"][slot])
        if part:
            self._deps(E, reads, ())
            for b in writes:
                for k2, (s2, v2) in b.r.items():
                    self._wait(E, k2, s2, v2)
        else:
            self._deps(E, reads, writes)
        ins = E.eng.dma_start(out=out, in_=in_, **kw)
        Q["cnts"][slot] += 16
        ins.then_inc(sem, 16)
        if part:
            for b in reads:
                b.r[key] = (sem, Q["cnts"][slot])
            for b in writes:
                b.w[key] = (sem, Q["cnts"][slot])
                b.r = {}
        else:
            self._stamp(key, sem, Q["cnts"][slot], reads, writes)
        self.n_inst += 1

    def barrier(self):
        items = []
        for e in self.E.values():
            if e.count > 0:
                items.append((e.key, e.sem, e.count))
        for q, Q in self.Q.items():
            for i, s in enumerate(Q["sems"]):
                if Q["cnts"][i] > 0:
                    items.append((f"D_{q}_{i}", s, Q["cnts"][i]))
        for e in self.E.values():
            for key, sem, val in items:
                if key == e.key:
                    continue
                self._wait(e, key, sem, val)

    def finish(self, eng="SP"):
        e = self.E[eng]
        for q, Q in self.Q.items():
            for i, s in enumerate(Q["sems"]):
                if Q["cnts"][i] > 0:
                    self._wait(e, f"D_{q}_{i}", s, Q["cnts"][i])
        for e2 in self.E.values():
            if e2.count > 0 and e2 is not e:
                self._wait(e, e2.key, e2.sem, e2.count)


class Ctx:
    pass


_UID = [0]


def uq(name):
    _UID[0] += 1
    return f"{name}_{_UID[0]}"


def build(NB, S, depth, dbg=None):
    assert S % 512 == 0
    NG = S // 512
    NBLK = S // 128
    nc = bass.Bass("TRN2", target_bir_lowering=False)
    tr = Tr(nc)
    c = Ctx()
    c.nc, c.tr, c.NB, c.S, c.NG, c.NBLK, c.depth = nc, tr, NB, S, NG, NBLK, depth

    def din(name, shape, dt=F32):
        return nc.dram_tensor(name, list(shape), dt, kind="ExternalInput").ap()

    def dscr(name, shape, dt):
        kind = "ExternalOutput" if (dbg and name in dbg) else "Internal"
        return nc.dram_tensor(name, list(shape), dt, kind=kind).ap()

    L = max(depth, 1)
    c.x = din("x", [NB, S, D])
    c.norm_mix_g = din("norm_mix_g", [L, D])
    c.w_in = din("w_in", [L, D, IN_COLS])
    c.fox_bias = din("fox_bias", [L, NH])
    c.sgu_ln_g = din("sgu_ln_g", [L, 512])
    c.sgu_w = din("sgu_w", [L, 8, 128, 128])
    c.sgu_b = din("sgu_b", [L, 8, 128])
    c.w_branch = din("w_branch", [L, 3, 512, D])
    c.w_out = din("w_out", [L, D, D])
    c.norm_ffn_g = din("norm_ffn_g", [L, D])
    c.w_up = din("w_up", [L, D, 2 * D_FF])
    c.conv_w = din("conv_w", [L, 3, D_FF])
    c.conv_b = din("conv_b", [L, D_FF])
    c.w_down = din("w_down", [L, D_FF, D])
    c.final_g = din("final_g", [D])
    c.k_ident = din("k_ident", [128, 128])
    c.k_masks = din("k_masks", [128, 4, 128])
    c.out = nc.dram_tensor("out", [NB, S, D], F32, kind="ExternalOutput").ap()

    c.hT = dscr("hT", [NB, D, S], F32)
    c.xnT = dscr("xnT", [NB, D, S], BF16)
    c.QaT = dscr("QaT", [NB, 512, S], BF16)
    c.KaT = dscr("KaT", [NB, 512, S], BF16)
    c.Va = dscr("Va", [NB, S, 512], BF16)
    c.QbT = dscr("QbT", [NB, NH, AUG, S], BF16)
    c.KbT = dscr("KbT", [NB, NH, AUG, S], BF16)
    c.Vb = dscr("Vb", [NB, S, 512], BF16)
    c.YaT = dscr("YaT", [NB, 512, S], BF16)
    c.YbT = dscr("YbT", [NB, 512, S], BF16)
    c.YcT = dscr("YcT", [NB, 512, S], BF16)
    c.hmid = dscr("hmid", [NB, D_FF, S], BF16)

    with ExitStack() as gs:
        def sb(name, shape, dt):
            return gs.enter_context(nc.sbuf_tensor(name, list(shape), dt))

        c.ident = sb("ident", [128, 128], F32)
        c.ones_bf = sb("ones_bf", [128, 128], BF16)
        c.b_const = Buf("const")
        tr.dma("SP", c.ident[:], c.k_ident[:, :], writes=[c.b_const])
        tr.op("DVE", lambda e: e.memset(c.ones_bf[:], 1.0), writes=[c.b_const])
        c.eps_col = sb("eps_col", [128, 1], F32)
        c.one_col = sb("one_col", [128, 1], F32)
        tr.op("DVE", lambda e: e.memset(c.one_col[:], 1.0), writes=[c.b_const])
        tr.op("DVE", lambda e: e.memset(c.eps_col[:], EPS), writes=[c.b_const])
        c.ps = [gs.enter_context(nc.psum_tensor(f"ps{i}", [128, 512], F32)) for i in range(8)]
        c.psb = [Buf(f"ps{i}") for i in range(8)]

        c.ones_big = sb("ones_big", [8, 3 * 512], BF16)
        tr.op("DVE", lambda e: e.memset(c.ones_big[:], 1.0), writes=[c.b_const])
        c.masks = sb("masks", [128, 4, 128], F32)
        tr.dma("SP", c.masks[:], c.k_masks[:, :, :], writes=[c.b_const])
        c.masks_bf = sb("masks_bf", [128, 4, 128], BF16)
        tr.op("DVE", lambda e: e.tensor_copy(out=c.masks_bf[:], in_=c.masks[:]), reads=[c.b_const], writes=[c.b_const])
        c.negtri_bf = sb("negtri_bf", [128, 2, 128], BF16)
        tr.op("DVE", lambda e: e.tensor_scalar(out=c.negtri_bf[:], in0=c.masks[:, 2:4, :], scalar1=-1.0, scalar2=None, op0=ALU.mult),
              reads=[c.b_const], writes=[c.b_const])
        c.ident_bf = sb("ident_bf", [128, 128], BF16)
        tr.op("DVE", lambda e: e.tensor_copy(out=c.ident_bf[:], in_=c.ident[:]), reads=[c.b_const], writes=[c.b_const])
        c.nb_sb = sb("nb_sb", [128, 128], BF16)
        c.nb_fox = sb("nb_fox", [128, 128], BF16)
        tr.op("DVE", lambda e: e.tensor_scalar(out=c.nb_sb[:], in0=c.masks[:, 0, :], scalar1=-1.0, scalar2=30000.0, op0=ALU.add, op1=ALU.mult),
              reads=[c.b_const], writes=[c.b_const])
        tr.op("DVE", lambda e: e.tensor_scalar(out=c.nb_fox[:], in0=c.masks[:, 1, :], scalar1=-1.0, scalar2=30000.0, op0=ALU.add, op1=ALU.mult),
              reads=[c.b_const], writes=[c.b_const])
        init_small_vectors(c, gs)
        phase_in_transpose(c)
        tr.barrier()
        stop_after = (dbg or {}).get("stop_after") if isinstance(dbg, dict) else None
        for l in range(depth):
            phase_inproj(c, l)
            tr.barrier()
            if stop_after == "inproj":
                break
            phase_attn(c, l, "sb")
            tr.barrier()
            phase_attn(c, l, "fox")
            tr.barrier()
            if stop_after == "attn":
                break
            phase_merge(c, l)
            tr.barrier()
            if stop_after == "merge":
                break
            phase_ffn_up(c, l)
            tr.barrier()
            phase_ffn_down(c, l)
            tr.barrier()
        phase_final(c)
        tr.finish("SP")
    return nc


def phase_in_transpose(c):
    nc, tr = c.nc, c.tr
    with ExitStack() as st:
        xt = [st.enter_context(nc.sbuf_tensor(f"p0_x{i}", [128, D], F32)) for i in range(2)]
        xb = [Buf(f"p0_x{i}") for i in range(2)]
        ht = [st.enter_context(nc.sbuf_tensor(f"p0_h{i}", [128, KD, 512], F32)) for i in range(2)]
        hb = [Buf(f"p0_h{i}") for i in range(2)]
        it = 0
        for b in range(c.NB):
            hT_v = c.hT[b].rearrange("(k p) t -> p k t", p=128)
            for g in range(c.NG):
                H, HB = ht[g % 2], hb[g % 2]
                for tb in range(4):
                    X, XB = xt[it % 2], xb[it % 2]
                    t0 = g * 512 + tb * 128
                    tr.dma("SP", X[:], c.x[b, t0:t0 + 128, :], writes=[XB])
                    for half in range(2):
                        pi = (it * 2 + half) % 4
                        P, PB = c.ps[pi], c.psb[pi]
                        tr.mm_group(
                            [(lambda e, j=j, P=P, X=X, half=half: e.transpose(
                                P[:, j * 128:(j + 1) * 128], X[:, (half * 4 + j) * 128:(half * 4 + j + 1) * 128], c.ident[:]))
                             for j in range(4)],
                            reads=[XB, c.b_const], writes=[PB])
                        eng = "ACT" if half == 0 else "DVE"
                        if eng == "ACT":
                            tr.op("ACT", lambda e, P=P, H=H, half=half, tb=tb: e.activation(
                                out=H[:, half * 4:(half + 1) * 4, tb * 128:(tb + 1) * 128],
                                in_=P[:].rearrange("p (j t) -> p j t", j=4), func=AF.Copy),
                                reads=[PB], writes=[HB])
                        else:
                            tr.op("DVE", lambda e, P=P, H=H, half=half, tb=tb: e.tensor_copy(
                                out=H[:, half * 4:(half + 1) * 4, tb * 128:(tb + 1) * 128],
                                in_=P[:].rearrange("p (j t) -> p j t", j=4)),
                                reads=[PB], writes=[HB])
                    it += 1
                tr.dma("SP", hT_v[:, :, g * 512:(g + 1) * 512], H[:], reads=[HB])


def emit_rmsnorm(c, H, HB, XN, XNB, g_col, sq, sqb, rstd, rstdb, pbank, out_dt_note=None):
    nc, tr = c.nc, c.tr
    P, PB = c.ps[pbank], c.psb[pbank]
    tr.op("ACT", lambda e: e.activation(out=sq[:], in_=H[:], func=AF.Square), reads=[HB], writes=[sqb])
    tr.mm_group([(lambda e, k=k: e.matmul(P[:], c.ones_bf[:], sq[:, k, :], start=(k == 0), stop=(k == KD - 1)))
                 for k in range(KD)], reads=[sqb, c.b_const], writes=[PB])
    tr.op("ACT", lambda e: e.activation(out=rstd[:], in_=P[:], func=AF.Ln, scale=1.0 / D, bias=c.eps_col[:, 0:1]),
          reads=[PB, c.b_const], writes=[rstdb])
    tr.op("ACT", lambda e: e.activation(out=rstd[:], in_=rstd[:], func=AF.Exp, scale=-0.5),
          reads=[rstdb], writes=[rstdb])
    for k in range(KD):
        tr.op("DVE", lambda e, k=k: e.scalar_tensor_tensor(
            out=XN[:, k, :], in0=H[:, k, :], scalar=g_col[:, k:k + 1], in1=rstd[:], op0=ALU.mult, op1=ALU.mult),
            reads=[HB, rstdb, c.b_const], writes=[XNB])


def init_small_vectors(c, gs):
    nc, tr = c.nc, c.tr
    L = max(c.depth, 1)
    c.smallv = gs.enter_context(nc.sbuf_tensor("smallv", [128, 512], F32))
    c.o_gmix, c.o_gffn, c.o_fg, c.o_cb, c.o_cw = 0, L * KD, 2 * L * KD, 2 * L * KD + KD, 2 * L * KD + KD + L * NFF
    assert c.o_cw + L * 3 * NFF <= 512
    P, PB = c.ps[0], c.psb[0]
    with ExitStack() as st:
        rows = st.enter_context(nc.sbuf_tensor(uq("sv_rows"), [128, 8, 128], F32))
        rb = Buf("sv_rows")
        srcs = [(c.norm_mix_g.rearrange("l (k p) -> (l k) p", p=128), L * KD, c.o_gmix),
                (c.norm_ffn_g.rearrange("l (k p) -> (l k) p", p=128), L * KD, c.o_gffn),
                (c.final_g.rearrange("(k p) -> k p", p=128), KD, c.o_fg),
                (c.conv_b.rearrange("l (k p) -> (l k) p", p=128), L * NFF, c.o_cb)]
        for l in range(L):
            srcs.append((c.conv_w[l].rearrange("i (k p) -> (i k) p", p=128), 3 * NFF, c.o_cw + l * 3 * NFF))
        assert len(srcs) <= 8
        for i, (src, n, off) in enumerate(srcs):
            tr.dma("SP", rows[0:n, i, :], src, writes=[rb])
        fns = []
        for i, (src, n, off) in enumerate(srcs):
            fns.append(lambda e, i=i, n=n, off=off: e.transpose(P[:, off:off + n], rows[0:n, i, :], c.ident[0:n, 0:n]))
        tr.mm_group(fns, reads=[rb, c.b_const], writes=[PB])
        tr.op("DVE", lambda e: e.tensor_copy(out=c.smallv[:, 0:c.o_cw + L * 3 * NFF], in_=P[:, 0:c.o_cw + L * 3 * NFF]),
              reads=[PB], writes=[c.b_const])
    for b in range(c.NB):
        for g in range(c.NG):
            src = c.ones_big[:].rearrange("p (j t) -> p j t", j=3)
            tr.dma("SP", c.QbT[b, :, 67:70, g * 512:(g + 1) * 512], src, reads=[c.b_const])
            tr.dma("SP", c.KbT[b, :, 64:67, g * 512:(g + 1) * 512], src, reads=[c.b_const])


def load_w_cast(c, dst, dstb, src_rows, col0, ncols, kchunks, dcol0=0):
    tr = c.tr
    for k in range(kchunks):
        o = 0
        while o < ncols:
            n = min(2048, ncols - o)
            tr.dma("POOL", dst[:, k, dcol0 + o:dcol0 + o + n], src_rows[k * 128:(k + 1) * 128, col0 + o:col0 + o + n], writes=[dstb], part=True)
            o += n


def phase_inproj(c, l):
    nc, tr = c.nc, c.tr
    NB, S, NG = c.NB, c.S, c.NG
    NW = C_G
    with ExitStack() as st:
        def sb(name, shape, dt):
            return st.enter_context(nc.sbuf_tensor(uq("p1_" + name), list(shape), dt))
        W = sb("w", [128, KD, NW], BF16)
        WB = Buf("p1_w")
        load_w_cast(c, W, WB, c.w_in[l], 0, NW, KD)
        wst = sb("wst", [128, 8, 128], BF16)
        wstb = Buf("p1_wst")
        wraw = sb("wraw", [128, 8, 128], F32)
        wrawb = Buf("p1_wraw")
        tr.dma("SP", wraw[:], c.sgu_w[l].rearrange("g t s -> t g s"), writes=[wrawb])
        for half in range(2):
            P, PB = c.ps[6 + half], c.psb[6 + half]
            tr.mm_group([(lambda e, j=j, P=P, half=half: e.transpose(P[:, j * 128:(j + 1) * 128], wraw[:, half * 4 + j, :], c.ident[:]))
                         for j in range(4)], reads=[wrawb, c.b_const], writes=[PB])
            tr.op("DVE", lambda e, P=P, half=half: e.tensor_copy(out=wst[:, half * 4:(half + 1) * 4, :],
                                                                 in_=P[:].rearrange("p (j t) -> p j t", j=4)),
                  reads=[PB], writes=[wstb])
        tr.op("DVE", lambda e: e.memset(wst[64:128, :, 0:64], 0.0), reads=[wstb], writes=[wstb])
        sgub32 = sb("sgub32", [1, 1024], F32)
        sgub_hi = sb("sgub_hi", [1, 1024], BF16)
        sgub_lo = sb("sgub_lo", [1, 1024], BF16)
        sgubb = Buf("p1_sgub")
        tr.dma("SP", sgub32[:], c.sgu_b[l:l + 1].rearrange("o g t -> o (g t)"), writes=[sgubb])
        tr.op("DVE", lambda e: e.tensor_copy(out=sgub_hi[:], in_=sgub32[:]), reads=[sgubb], writes=[sgubb])
        tr.op("DVE", lambda e: e.tensor_tensor(out=sgub32[:], in0=sgub32[:], in1=sgub_hi[:], op=ALU.subtract), reads=[sgubb], writes=[sgubb])
        tr.op("DVE", lambda e: e.tensor_copy(out=sgub_lo[:], in_=sgub32[:]), reads=[sgubb], writes=[sgubb])
        lng = sb("lng", [128, 512], F32)
        foxb = sb("foxb", [8, 1], F32)
        ones8 = sb("ones8", [8, 512], F32)
        lb = Buf("p1_lng")
        tr.dma("SP", lng[:], c.sgu_ln_g[l].partition_broadcast(128), writes=[lb])
        tr.dma("SP", foxb[:], c.fox_bias[l].rearrange("(h o) -> h o", o=1), writes=[lb])
        tr.op("DVE", lambda e: e.memset(ones8[:], 1.0), reads=[lb], writes=[lb])
        gcol = c.smallv[:, c.o_gmix + l * KD:c.o_gmix + (l + 1) * KD]

        ht = [sb(f"h{i}", [128, KD, 512], F32) for i in range(2)]
        hb = [Buf(f"p1_h{i}") for i in range(2)]
        xn = [sb(f"xn{i}", [128, KD, 512], BF16) for i in range(2)]
        xnb = [Buf(f"p1_xn{i}") for i in range(2)]
        sq = sb("sq", [128, KD, 512], BF16)
        sqb = Buf("p1_sq")
        rstd = sb("rstd", [128, 512], F32)
        rstdb = Buf("p1_rstd")
        fam = {n_: sb("fam_" + n_, [128, 4, 512], BF16) for n_ in ("qa", "ka", "qb", "kb", "va", "vb")}
        famb = {n_: Buf("p1_fam_" + n_) for n_ in fam}
        ut = sb("ut", [128, 4, 512], F32)
        utb = Buf("p1_ut")
        v32 = sb("v32", [128, 4, 512], F32)
        v32b = [Buf(f"p1_v32_{i}") for i in range(4)]
        vn = sb("vn", [128, 2, 512], BF16)
        vnb = [Buf(f"p1_vn{i}") for i in range(2)]
        stt = sb("stt", [128, 4, 16], F32)
        sttb = Buf("p1_stt")
        yc = sb("yc", [128, 4, 512], BF16)
        ycb = Buf("p1_yc")
        fl = sb("fl", [8, 512], F32)
        flb = Buf("p1_fl")
        lc = sb("lc", [8, 512], F32)
        lcx = fl
        carry = sb("carry", [8, 1], F32)
        lcb = Buf("p1_lc")
        pcs = sb("pc", [8, 6, 512], BF16)
        pcsb = Buf("p1_pcs")

        groups = [(b, g) for b in range(NB) for g in range(NG)]

        def load_h(gi):
            b, g = groups[gi]
            tr.dma("SP", ht[gi % 2][:], c.hT[b].rearrange("(k p) t -> p k t", p=128)[:, :, g * 512:(g + 1) * 512], writes=[hb[gi % 2]])

        def norm(gi):
            emit_rmsnorm(c, ht[gi % 2], hb[gi % 2], xn[gi % 2], xnb[gi % 2], gcol, sq, sqb, rstd, rstdb, 0)

        load_h(0)
        norm(0)
        ifm = 0
        itm = 0
        ibank_fm = 0
        ibank_tm = 0
        for gi, (b, g) in enumerate(groups):
            XN, XNB = xn[gi % 2], xnb[gi % 2]
            tsl = slice(g * 512, (g + 1) * 512)
            if gi + 1 < len(groups):
                load_h(gi + 1)
            if g == 0:
                tr.op("DVE", lambda e: e.memset(carry[:], 0.0), reads=[lcb], writes=[lcb])
            tr.dma("SP", c.xnT[b].rearrange("(k p) t -> p k t", p=128)[:, :, tsl], XN[:], reads=[XNB])
            fm_jobs = []
            for j in range(4):
                fm_jobs.append((C_QA + j * 128, "qa", 0.125, j))
            for j in range(4):
                fm_jobs.append((C_KA + j * 128, "ka", 1.0, j))
            for j in range(4):
                fm_jobs.append((C_QB + j * 128, "qb", 0.125, j))
            for j in range(4):
                fm_jobs.append((C_KB + j * 128, "kb", 1.0, j))
            for j in range(4):
                fm_jobs.append((C_U + j * 128, "gelu", 1.0, j))
            for (col, kind, scale, j) in fm_jobs:
                pi = 1 + (ibank_fm % 2)
                ibank_fm += 1
                P, PB = c.ps[pi], c.psb[pi]
                tr.mm_group([(lambda e, k=k, P=P, col=col, XN=XN: e.matmul(
                    P[:], W[:, k, col:col + 128], XN[:, k, :], start=(k == 0), stop=(k == KD - 1))) for k in range(KD)],
                    reads=[WB, XNB], writes=[PB])
                if kind == "gelu":
                    tr.op("ACT", lambda e, P=P, j=j: e.activation(out=ut[:, j, :], in_=P[:], func=AF.Gelu),
                          reads=[PB], writes=[utb])
                    continue
                F, FB = fam[kind], famb[kind]
                if ifm % 2 == 0:
                    tr.op("ACT", lambda e, P=P, F=F, scale=scale, j=j: e.activation(out=F[:, j, :], in_=P[:], func=AF.Copy, scale=scale),
                          reads=[PB], writes=[FB])
                else:
                    tr.op("DVE", lambda e, P=P, F=F, scale=scale, j=j: e.tensor_scalar(
                        out=F[:, j, :], in0=P[:], scalar1=scale, scalar2=None, op0=ALU.mult), reads=[PB], writes=[FB])
                ifm += 1
                if j == 3:
                    if kind in ("qa", "ka"):
                        T = c.QaT if kind == "qa" else c.KaT
                        tr.dma("SP", T[b].rearrange("(j p) t -> p j t", p=128)[:, :, tsl], F[:], reads=[FB])
                    else:
                        T = c.QbT if kind == "qb" else c.KbT
                        Tv = T[b].rearrange("(j hh) r t -> hh j r t", hh=2)
                        for hh in range(2):
                            tr.dma("SP", Tv[hh, :, 0:64, tsl].rearrange("j d t -> d j t"), F[hh * 64:(hh + 1) * 64, :, :], reads=[FB])
            PF, PFB = c.ps[5], c.psb[5]
            tr.mm_group([(lambda e, k=k, XN=XN: e.matmul(PF[0:8, :], W[:, k, C_F:C_F + 8], XN[:, k, :], start=(k == 0), stop=(k == KD - 1)))
                         for k in range(KD)], reads=[WB, XNB], writes=[PFB])
            tr.op("DVE", lambda e: e.tensor_scalar(out=fl[:], in0=PF[0:8, :], scalar1=foxb[:, 0:1], scalar2=None, op0=ALU.add),
                  reads=[PFB, lb], writes=[flb])
            for tb in range(4):
                bsl = slice(tb * 128, (tb + 1) * 128)
                for (col, kind, dstT) in ((C_VA, "va", c.Va), (C_VB, "vb", c.Vb), (C_V, "gelu", None)):
                    pi = 3 + (ibank_tm % 2)
                    ibank_tm += 1
                    P, PB = c.ps[pi], c.psb[pi]
                    tr.mm_group([(lambda e, k=k, P=P, col=col, XN=XN, bsl=bsl: e.matmul(
                        P[:], XN[:, k, bsl], W[:, k, col:col + 512], start=(k == 0), stop=(k == KD - 1))) for k in range(KD)],
                        reads=[WB, XNB], writes=[PB])
                    if kind == "gelu":
                        tr.op("ACT", lambda e, P=P, tb=tb: e.activation(out=v32[:, tb, :], in_=P[:], func=AF.Gelu),
                              reads=[PB], writes=[v32b[tb]])
                        tr.op("DVE", lambda e, tb=tb: e.bn_stats(out=stt[:, tb, 0:6], in_=v32[:, tb, :]), reads=[v32b[tb]], writes=[sttb])
                        tr.op("DVE", lambda e, tb=tb: e.bn_aggr(out=stt[:, tb, 6:8], in_=stt[:, tb, 0:6]), reads=[sttb], writes=[sttb])
                    else:
                        T_, TB_ = fam[kind], famb[kind]
                        if itm % 2 == 0:
                            tr.op("DVE", lambda e, P=P, T_=T_, tb=tb: e.tensor_copy(out=T_[:, tb, :], in_=P[:]), reads=[PB], writes=[TB_])
                        else:
                            tr.op("ACT", lambda e, P=P, T_=T_, tb=tb: e.activation(out=T_[:, tb, :], in_=P[:], func=AF.Copy), reads=[PB], writes=[TB_])
                        itm += 1
                        if tb == 3:
                            tr.dma("SP", dstT[b, g * 512:(g + 1) * 512, :].rearrange("(tb p) c -> p tb c", p=128), T_[:], reads=[TB_])
            if gi + 1 < len(groups):
                norm(gi + 1)
            tr.op("ACT", lambda e: e.activation(out=stt[:, :, 8:9], in_=stt[:, :, 7:8], func=AF.Ln, bias=c.eps_col[:, 0:1]),
                  reads=[sttb, c.b_const], writes=[sttb])
            tr.op("ACT", lambda e: e.activation(out=stt[:, :, 9:10], in_=stt[:, :, 8:9], func=AF.Exp, scale=-0.5),
                  reads=[sttb], writes=[sttb])
            tr.op("DVE", lambda e: e.scalar_tensor_tensor(out=stt[:, :, 10:11], in0=stt[:, :, 6:7], scalar=-1.0, in1=stt[:, :, 9:10],
                                                          op0=ALU.mult, op1=ALU.mult), reads=[sttb], writes=[sttb])
            tr.op("ACT", lambda e: e.activation(out=fl[:], in_=fl[:], func=AF.Exp, scale=-1.0), reads=[flb], writes=[flb])
            tr.op("ACT", lambda e: e.activation(out=fl[:], in_=fl[:], func=AF.Ln, bias=c.one_col[0:8, 0:1]), reads=[flb, c.b_const], writes=[flb])
            for tb in range(4):
                bsl = slice(tb * 128, (tb + 1) * 128)
                tr.op("ACT", lambda e, tb=tb: e.activation(out=v32[:, tb, :], in_=v32[:, tb, :], func=AF.Identity,
                                                          scale=stt[:, tb, 9:10], bias=stt[:, tb, 10:11]),
                      reads=[sttb, v32b[tb]], writes=[v32b[tb]])
                tr.op("DVE", lambda e, tb=tb: e.tensor_tensor(out=vn[:, tb % 2, :], in0=v32[:, tb, :], in1=lng[:], op=ALU.mult),
                      reads=[v32b[tb], lb], writes=[vnb[tb % 2]])
                PM, PMB = c.ps[6 + tb % 2], c.psb[6 + tb % 2]
                fns = []
                for gg in range(8):
                    cc, hh = gg // 2, gg % 2
                    o = PM[hh * 64:(hh + 1) * 64, cc * 128:(cc + 1) * 128]
                    fns.append(lambda e, o=o, gg=gg, tb=tb: e.matmul(o, vn[:, tb % 2, gg * 64:(gg + 1) * 64], wst[:, gg, :], start=True, stop=False))
                    fns.append(lambda e, o=o, gg=gg: e.matmul(o, c.ones_bf[0:1, 0:64], sgub_hi[0:1, gg * 128:(gg + 1) * 128], start=False, stop=False))
                    fns.append(lambda e, o=o, gg=gg: e.matmul(o, c.ones_bf[0:1, 0:64], sgub_lo[0:1, gg * 128:(gg + 1) * 128], start=False, stop=True))
                tr.mm_group(fns, reads=[vnb[tb % 2], wstb, sgubb, c.b_const], writes=[PMB])
                tr.op("DVE", lambda e, PM=PM, bsl=bsl: e.tensor_tensor(
                    out=yc[:, :, bsl], in0=PM[:].rearrange("p (c t) -> p c t", c=4), in1=ut[:, :, bsl], op=ALU.mult),
                    reads=[PMB, utb], writes=[ycb])
            tr.dma("SP", c.YcT[b].rearrange("(c p) t -> p c t", p=128)[:, :, tsl], yc[:], reads=[ycb])
            tr.op("DVE", lambda e: e.tensor_tensor_scan(out=lc[:], data0=ones8[:], data1=fl[:], initial=carry[:, 0:1], op0=ALU.mult, op1=ALU.add),
                  reads=[flb, lcb, lb], writes=[lcb])
            tr.op("DVE", lambda e: e.tensor_copy(out=carry[:], in_=lc[:, 511:512]), reads=[lcb], writes=[lcb])
            tr.op("DVE", lambda e: e.tensor_scalar(out=lcx[:], in0=lc[:], scalar1=-1.0, scalar2=None, op0=ALU.mult), reads=[lcb, flb], writes=[lcb, flb])
            for j in range(3):
                tr.op("DVE", lambda e, j=j: e.tensor_copy(out=pcs[:, j, :], in_=lcx[:]), reads=[lcb, flb, pcsb], writes=[pcsb])
                tr.op("DVE", lambda e, j=j: e.tensor_scalar(out=pcs[:, 3 + j, :], in0=pcs[:, j, :], scalar1=-1.0, scalar2=None, op0=ALU.mult),
                      reads=[pcsb], writes=[pcsb])
                if j < 2:
                    tr.op("DVE", lambda e, j=j: e.tensor_tensor(out=lcx[:], in0=lcx[:], in1=pcs[:, j, :], op=ALU.subtract),
                          reads=[lcb, flb, pcsb], writes=[lcb, flb])
            tr.dma("SP", c.QbT[b, :, 64:67, tsl], pcs[:, 0:3, :], reads=[pcsb])
            tr.dma("SP", c.KbT[b, :, 67:70, tsl], pcs[:, 3:6, :], reads=[pcsb])


def phase_attn(c, l, kind):
    nc, tr = c.nc, c.tr
    NB, S, NG, NBLK = c.NB, c.S, c.NG, c.NBLK
    sb_mode = (kind == "sb")
    KR = 64 if sb_mode else AUG
    with ExitStack() as st:
        def sbt(name, shape, dt):
            return st.enter_context(nc.sbuf_tensor(uq(f"p{kind}_" + name), list(shape), dt))
        Vts = [sbt(f"v{i}", [128, NBLK, 512] if sb_mode else [128, NBLK, NH, 128], BF16) for i in range(2)]
        VBs = [Buf(f"at_v{i}") for i in range(2)]
        KT = [sbt(f"k{i}", [KR, S], BF16) for i in range(2)]
        QT = [sbt(f"q{i}", [KR, S], BF16) for i in range(2)]
        KQB = [Buf(f"at_kq{i}") for i in range(2)]
        NR = 3
        E32 = [sbt(f"e{i}", [128, 512], F32) for i in range(NR)] if sb_mode else None
        LS = [sbt(f"ls{i}", [128, 512], BF16) for i in range(NR)] if sb_mode else None
        LSB = [Buf(f"at_ls{i}") for i in range(NR)]
        SS = [sbt(f"ss{i}", [128, 512], BF16) for i in range(2)] if sb_mode else None
        SSB = [Buf(f"at_ss{i}") for i in range(2)]
        WT = [sbt(f"w{i}", [128, 512], BF16) for i in range(NR)]
        WTB = [Buf(f"at_w{i}") for i in range(NR)]
        YS = [sbt(f"y{i}", [64, 512], BF16) for i in range(2)]
        YSB = [Buf(f"at_y{i}") for i in range(2)]
        RC = [sbt(f"r{i}", [128, 512], F32) for i in range(2)] if not sb_mode else None
        RC0 = [sbt(f"r0_{i}", [64, 512], F32) for i in range(2)] if not sb_mode else None
        RCB = [Buf(f"at_r{i}") for i in range(2)]
        RC0B = [Buf(f"at_r0_{i}") for i in range(2)]
        nbias = c.nb_sb if sb_mode else c.nb_fox
        Ksrc = c.KaT if sb_mode else None
        dstT = c.YaT if sb_mode else c.YbT
        Vsrc = c.Va if sb_mode else c.Vb

        units = []
        ihead = 0
        igroup = 0
        for b in range(NB):
            for h in range(NH):
                for g in range(NG):
                    kbs = list(range(4 * g + 3, -1, -1)) if sb_mode else list(range(0, 4 * g + 4))
                    for j, kb in enumerate(kbs):
                        units.append(dict(b=b, h=h, g=g, kb=kb, j=j, first=(j == 0), last=(j == len(kbs) - 1),
                                          ihead=ihead, igroup=igroup, newhead=(g == 0 and j == 0), newbatch=(h == 0 and g == 0 and j == 0)))
                    igroup += 1
                ihead += 1
        n = len(units)

        def stage_A(i):
            u = units[i]
            b, h, g, kb = u["b"], u["h"], u["g"], u["kb"]
            if u["newbatch"]:
                if sb_mode:
                    tr.dma("SP", Vts[b % 2][:], Vsrc[b].rearrange("(n p) c -> p n c", p=128), writes=[VBs[b % 2]])
                else:
                    tr.op("POOL", lambda e: e.memset(Vts[b % 2][:], 1.0), writes=[VBs[b % 2]])
                    vsrc4 = Vsrc[b].rearrange("(n p) (h d) -> p n h d", p=128, d=64)
                    for n0 in range(NBLK):
                        tr.dma("SP", Vts[b % 2][:, n0, :, 0:64], vsrc4[:, n0, :, :], writes=[VBs[b % 2]], part=(n0 > 0))
            if u["newhead"]:
                hb_ = u["ihead"] % 2
                if sb_mode:
                    tr.dma("SP", KT[hb_][:], c.KaT[b, h * 64:(h + 1) * 64, :], writes=[KQB[hb_]])
                    tr.dma("SP", QT[hb_][:], c.QaT[b, h * 64:(h + 1) * 64, :], writes=[KQB[hb_]])
                else:
                    tr.dma("SP", KT[hb_][:], c.KbT[b, h, :, :], writes=[KQB[hb_]])
                    tr.dma("SP", QT[hb_][:], c.QbT[b, h, :, :], writes=[KQB[hb_]])
            hb_ = u["ihead"] % 2
            c0 = max(0, kb - 4 * g) * 128
            diag = kb >= 4 * g
            u["c0"], u["diag"] = c0, diag
            PZ, PZB = c.ps[i % 4], c.psb[i % 4]
            u["PZ"], u["PZB"] = PZ, PZB
            fns = [lambda e: e.matmul(PZ[:, c0:512], KT[hb_][:, kb * 128:(kb + 1) * 128], QT[hb_][:, g * 512 + c0:(g + 1) * 512],
                                      start=True, stop=True)]
            if diag:
                fns.append(lambda e: e.matmul(PZ[:, c0:c0 + 128], c.ident_bf[:], nbias[:], start=False, stop=True, skip_group_check=True))
            tr.mm_group(fns, reads=[KQB[hb_], c.b_const], writes=[PZB])

        def stage_B(i):
            u = units[i]
            c0, PZ, PZB = u["c0"], u["PZ"], u["PZB"]
            r = i % NR
            if sb_mode:
                tr.op("ACT", lambda e: e.activation(out=E32[r][:, c0:512], in_=PZ[:, c0:512], func=AF.Exp), reads=[PZB], writes=[LSB[r]])
                tr.op("ACT", lambda e: e.activation(out=LS[r][:, c0:512], in_=E32[r][:, c0:512], func=AF.Ln, bias=c.one_col[:, 0:1]),
                      reads=[LSB[r], c.b_const], writes=[LSB[r]])
            else:
                tr.op("ACT", lambda e: e.activation(out=WT[r][:, c0:512], in_=PZ[:, c0:512], func=AF.Exp), reads=[PZB], writes=[WTB[r]])

        def stage_C(i):
            if not sb_mode:
                return
            u = units[i]
            c0, PZ, PZB, j = u["c0"], u["PZ"], u["PZB"], u["j"]
            r = i % NR
            if u["first"]:
                for k2 in range(2):
                    tr.op("POOL", lambda e, k2=k2: e.memset(SS[k2][:], 0.0), writes=[SSB[k2]])
            fns = [lambda e: e.matmul(PZ[:, c0:512], c.negtri_bf[:, 0, :], LS[r][:, c0:512], start=False, stop=True, skip_group_check=True)]
            reads = [LSB[r], c.b_const]
            if j > 0:
                fns.append(lambda e: e.matmul(PZ[:, c0:512], c.negtri_bf[:, 1, :], SS[j % 2][:, c0:512], start=False, stop=True, skip_group_check=True))
                reads.append(SSB[j % 2])
            tr.mm_group(fns, reads=reads, writes=[PZB])
            if not u["last"]:
                tr.op("DVE", lambda e: e.tensor_tensor(out=SS[(j + 1) % 2][:, c0:512], in0=SS[j % 2][:, c0:512], in1=LS[r][:, c0:512], op=ALU.add),
                      reads=[SSB[j % 2], LSB[r]], writes=[SSB[(j + 1) % 2]])
            tr.op("ACT", lambda e: e.activation(out=WT[r][:, c0:512], in_=PZ[:, c0:512], func=AF.Exp), reads=[PZB], writes=[WTB[r]])

        def stage_F(i):
            u = units[i]
            b, h, g, kb, c0 = u["b"], u["h"], u["g"], u["kb"], u["c0"]
            r = i % NR
            gi = u["igroup"] % 2
            PO, POB = c.ps[4 + gi], c.psb[4 + gi]
            Vt, VB = Vts[b % 2], VBs[b % 2]
            if sb_mode:
                fns = [lambda e: e.matmul(PO[0:64, c0:512], Vt[:, kb, h * 64:(h + 1) * 64], WT[r][:, c0:512], start=u["first"], stop=True,
                                          skip_group_check=True)]
            else:
                fns = [lambda e: e.matmul(PO[:, c0:512], Vt[:, kb, h, :], WT[r][:, c0:512], start=u["first"], stop=True, skip_group_check=True)]
            tr.mm_group(fns, reads=[VB, WTB[r], c.b_const], writes=[POB])
            if u["last"]:
                Y, YB = YS[gi], YSB[gi]
                if sb_mode:
                    tr.op("DVE", lambda e: e.tensor_copy(out=Y[:], in_=PO[0:64, :]), reads=[POB], writes=[YB])
                else:
                    tr.op("DVE", lambda e: e.reciprocal(out=RC[gi][64:128, :], in_=PO[64:128, :]), reads=[POB], writes=[RCB[gi]])
                    tr.dma("SP", RC0[gi][:], RC[gi][64:128, :], reads=[RCB[gi]], writes=[RC0B[gi]])
                    tr.op("DVE", lambda e: e.tensor_tensor(out=Y[:], in0=PO[0:64, :], in1=RC0[gi][:], op=ALU.mult),
                          reads=[POB, RC0B[gi]], writes=[YB])
                tr.dma("SP", dstT[b, h * 64:(h + 1) * 64, g * 512:(g + 1) * 512], Y[:], reads=[YB])

        if sb_mode:
            for s_ in range(-2, n + 1):
                if 0 <= s_ + 2 < n:
                    stage_A(s_ + 2)
                if 0 <= s_ + 1 < n:
                    stage_B(s_ + 1)
                if 0 <= s_ < n:
                    stage_C(s_)
                if 0 <= s_ - 1 < n:
                    stage_F(s_ - 1)
        else:
            for s_ in range(-2, n):
                if 0 <= s_ + 2 < n:
                    stage_A(s_ + 2)
                    stage_B(s_ + 2)
                if 0 <= s_ < n:
                    stage_F(s_)


def phase_merge(c, l):
    nc, tr = c.nc, c.tr
    NB, S, NG = c.NB, c.S, c.NG
    with ExitStack() as st:
        def sb(name, shape, dt):
            return st.enter_context(nc.sbuf_tensor(uq("p5_" + name), list(shape), dt))
        WG = sb("wg", [128, KD, 3 * D], BF16)
        WBR = sb("wbr", [128, 12, D], BF16)
        WO = sb("wo", [128, KD, D], BF16)
        WB = Buf("p5_w")
        load_w_cast(c, WG, WB, c.w_in[l], C_G, 3 * D, KD)
        load_w_cast(c, WBR, WB, c.w_branch[l].rearrange("i k n -> (i k) n"), 0, D, 12)
        load_w_cast(c, WO, WB, c.w_out[l], 0, D, KD)
        xn = [sb(f"xn{i}", [128, KD, 512], BF16) for i in range(2)]
        ys = [sb(f"y{i}", [128, 12, 512], BF16) for i in range(2)]
        inb = [Buf(f"p5_in{i}") for i in range(2)]
        H = sb("h", [128, KD, 512], F32)
        HB = Buf("p5_h")
        MT = sb("mt", [128, KD, 512], BF16)
        MTB = Buf("p5_mt")
        G = [sb(f"g{i}", [128, 512], F32) for i in range(2)]
        GB = [Buf(f"p5_g{i}") for i in range(2)]
        T = [sb(f"t{i}", [128, 512], F32) for i in range(3)]
        TB = [Buf(f"p5_t{i}") for i in range(3)]
        M = [sb(f"m{i}", [128, 512], F32) for i in range(2)]
        MB = [Buf(f"p5_m{i}") for i in range(2)]
        ysrc = (c.YaT, c.YbT, c.YcT)

        def load_inputs(b, g, gi):
            tsl = slice(g * 512, (g + 1) * 512)
            tr.dma("SP", xn[gi % 2][:], c.xnT[b].rearrange("(k p) t -> p k t", p=128)[:, :, tsl], writes=[inb[gi % 2]])
            for i in range(3):
                tr.dma("SP", ys[gi % 2][:, i * 4:(i + 1) * 4, :], ysrc[i][b].rearrange("(k p) t -> p k t", p=128)[:, :, tsl], writes=[inb[gi % 2]])

        groups = [(b, g) for b in range(NB) for g in range(NG)]
        load_inputs(*groups[0], 0)
        ia = 0
        it = 0
        for gi, (b, g) in enumerate(groups):
            tsl = slice(g * 512, (g + 1) * 512)
            if gi + 1 < len(groups):
                load_inputs(*groups[gi + 1], gi + 1)
            XN, YS, INB = xn[gi % 2], ys[gi % 2], inb[gi % 2]
            hT_v = c.hT[b].rearrange("(k p) t -> p k t", p=128)
            tr.dma("SP", H[:], hT_v[:, :, tsl], writes=[HB])
            for cc in range(KD):
                csl = slice(cc * 128, (cc + 1) * 128)
                for i in range(3):
                    PA, PAB = c.ps[ia % 2], c.psb[ia % 2]
                    PBk, PBB = c.ps[2 + ia % 2], c.psb[2 + ia % 2]
                    Gt, GtB = G[ia % 2], GB[ia % 2]
                    tr.mm_group([(lambda e, k=k, PA=PA, i=i, csl=csl, XN=XN: e.matmul(
                        PA[:], WG[:, k, i * D + csl.start:i * D + csl.stop], XN[:, k, :], start=(k == 0), stop=(k == KD - 1))) for k in range(KD)],
                        reads=[WB, INB], writes=[PAB])
                    tr.mm_group([(lambda e, k=k, PBk=PBk, i=i, csl=csl, YS=YS: e.matmul(
                        PBk[:], WBR[:, i * 4 + k, csl], YS[:, i * 4 + k, :], start=(k == 0), stop=(k == 3))) for k in range(4)],
                        reads=[WB, INB], writes=[PBB])
                    tr.op("ACT", lambda e, PA=PA, Gt=Gt: e.activation(out=Gt[:], in_=PA[:], func=AF.Sigmoid), reads=[PAB], writes=[GtB])
                    Tt, TtB = T[it % 3], TB[it % 3]
                    it += 1
                    tr.op("DVE", lambda e, PBk=PBk, Gt=Gt, Tt=Tt: e.tensor_tensor(out=Tt[:], in0=PBk[:], in1=Gt[:], op=ALU.mult),
                          reads=[PBB, GtB], writes=[TtB])
                    if i == 0:
                        T0, T0B = Tt, TtB
                    elif i == 1:
                        Mt, MtB = M[cc % 2], MB[cc % 2]
                        tr.op("POOL", lambda e, Mt=Mt, T0=T0, Tt=Tt: e.tensor_tensor(out=Mt[:], in0=T0[:], in1=Tt[:], op=ALU.add),
                              reads=[T0B, TtB], writes=[MtB])
                    else:
                        tr.op("POOL", lambda e, Mt=Mt, Tt=Tt, cc=cc: e.tensor_tensor(out=MT[:, cc, :], in0=Mt[:], in1=Tt[:], op=ALU.add),
                              reads=[MtB, TtB], writes=[MTB])
                    ia += 1
            for cc in range(KD):
                csl = slice(cc * 128, (cc + 1) * 128)
                PO, POB = c.ps[4 + cc % 2], c.psb[4 + cc % 2]
                tr.mm_group([(lambda e, k=k, PO=PO, csl=csl: e.matmul(PO[:], WO[:, k, csl], MT[:, k, :], start=(k == 0), stop=(k == KD - 1)))
                             for k in range(KD)], reads=[WB, MTB], writes=[POB])
                tr.op("DVE", lambda e, PO=PO, cc=cc: e.tensor_tensor(out=H[:, cc, :], in0=PO[:], in1=H[:, cc, :], op=ALU.add),
                      reads=[POB, HB], writes=[HB])
            tr.dma("SP", hT_v[:, :, tsl], H[:], reads=[HB])


def phase_ffn_up(c, l):
    nc, tr = c.nc, c.tr
    NB, S, NG = c.NB, c.S, c.NG
    with ExitStack() as st:
        def sb(name, shape, dt):
            return st.enter_context(nc.sbuf_tensor(uq("p6_" + name), list(shape), dt))
        WU = sb("wu", [128, KD, 2 * D_FF], BF16)
        WB = Buf("p6_w")
        load_w_cast(c, WU, WB, c.w_up[l], 0, 2 * D_FF, KD)
        gcol = c.smallv[:, c.o_gffn + l * KD:c.o_gffn + (l + 1) * KD]
        ht = [sb(f"h{i}", [128, KD, 512], F32) for i in range(2)]
        hb = [Buf(f"p6_h{i}") for i in range(2)]
        XN = sb("xn", [128, KD, 512], BF16)
        XNB = Buf("p6_xn")
        sq = sb("sq", [128, KD, 512], BF16)
        sqb = Buf("p6_sq")
        rstd = sb("rstd", [128, 512], F32)
        rstdb = Buf("p6_rstd")
        NR = 3
        GT = [sb(f"gt{i}", [128, 514], F32) for i in range(NR)]
        GTB = [Buf(f"p6_gt{i}") for i in range(NR)]
        A0 = [sb(f"a0_{i}", [128, 512], F32) for i in range(NR)]
        A0B = [Buf(f"p6_a0_{i}") for i in range(NR)]
        A1 = [sb(f"a1_{i}", [128, 512], F32) for i in range(NR)]
        A1B = [Buf(f"p6_a1_{i}") for i in range(NR)]
        HM = [sb(f"hm{i}", [128, 512], BF16) for i in range(NR)]
        HMB = [Buf(f"p6_hm{i}") for i in range(NR)]
        HALO = sb("halo", [128, NFF, 2], F32)
        HALOB = [Buf(f"p6_halo{j}") for j in range(NFF)]
        groups = [(b, g) for b in range(NB) for g in range(NG)]
        tr.dma("SP", ht[0][:], c.hT[groups[0][0]].rearrange("(k p) t -> p k t", p=128)[:, :, 0:512], writes=[hb[0]])
        it = 0
        for gi, (b, g) in enumerate(groups):
            tsl = slice(g * 512, (g + 1) * 512)
            if gi + 1 < len(groups):
                b2, g2 = groups[gi + 1]
                tr.dma("SP", ht[(gi + 1) % 2][:], c.hT[b2].rearrange("(k p) t -> p k t", p=128)[:, :, g2 * 512:(g2 + 1) * 512],
                       writes=[hb[(gi + 1) % 2]])
            H, HB = ht[gi % 2], hb[gi % 2]
            emit_rmsnorm(c, H, HB, XN, XNB, gcol, sq, sqb, rstd, rstdb, 0)
            if g == 0:
                tr.op("POOL", lambda e: e.memset(HALO[:], 0.0), writes=HALOB)
            for j in range(NFF):
                r = it % NR
                it += 1
                PG, PGB = c.ps[1 + (j % 2)], c.psb[1 + (j % 2)]
                PV, PVB = c.ps[3 + (j % 2)], c.psb[3 + (j % 2)]
                tr.mm_group([(lambda e, k=k, PG=PG, j=j: e.matmul(PG[:], WU[:, k, j * 128:(j + 1) * 128], XN[:, k, :],
                                                                  start=(k == 0), stop=(k == KD - 1))) for k in range(KD)],
                            reads=[WB, XNB], writes=[PGB])
                tr.mm_group([(lambda e, k=k, PV=PV, j=j: e.matmul(PV[:], WU[:, k, D_FF + j * 128:D_FF + (j + 1) * 128], XN[:, k, :],
                                                                  start=(k == 0), stop=(k == KD - 1))) for k in range(KD)],
                            reads=[WB, XNB], writes=[PVB])
                cw = lambda i_, j=j: c.smallv[:, c.o_cw + l * 3 * NFF + i_ * NFF + j:c.o_cw + l * 3 * NFF + i_ * NFF + j + 1]
                cb = c.smallv[:, c.o_cb + l * NFF + j:c.o_cb + l * NFF + j + 1]
                tr.op("ACT", lambda e, r=r, PG=PG: e.activation(out=GT[r][:, 2:514], in_=PG[:], func=AF.Copy), reads=[PGB], writes=[GTB[r]])
                tr.op("POOL", lambda e, r=r, j=j: e.tensor_copy(out=GT[r][:, 0:2], in_=HALO[:, j, :]), reads=[HALOB[j], GTB[r]], writes=[GTB[r]])
                tr.op("POOL", lambda e, r=r, j=j: e.tensor_copy(out=HALO[:, j, :], in_=GT[r][:, 512:514]), reads=[GTB[r]], writes=[HALOB[j]])
                tr.op("ACT", lambda e, r=r, PG=PG, cw=cw, cb=cb: e.activation(out=A0[r][:], in_=PG[:], func=AF.Identity, scale=cw(2), bias=cb),
                      reads=[PGB, c.b_const], writes=[A0B[r]])
                tr.op("DVE", lambda e, r=r, cw=cw: e.scalar_tensor_tensor(out=A1[r][:], in0=GT[r][:, 1:513], scalar=cw(1), in1=A0[r][:],
                                                                          op0=ALU.mult, op1=ALU.add),
                      reads=[GTB[r], A0B[r], c.b_const], writes=[A1B[r]])
                tr.op("DVE", lambda e, r=r, cw=cw: e.scalar_tensor_tensor(out=A0[r][:], in0=GT[r][:, 0:512], scalar=cw(0), in1=A1[r][:],
                                                                          op0=ALU.mult, op1=ALU.add),
                      reads=[GTB[r], A1B[r], c.b_const], writes=[A0B[r]])
                tr.op("ACT", lambda e, r=r: e.activation(out=A1[r][:], in_=A0[r][:], func=AF.Gelu), reads=[A0B[r]], writes=[A1B[r]])
                tr.op("DVE", lambda e, r=r, PV=PV: e.tensor_tensor(out=HM[r][:], in0=PV[:], in1=A1[r][:], op=ALU.mult),
                      reads=[PVB, A1B[r]], writes=[HMB[r]])
                tr.dma("SP", c.hmid[b, j * 128:(j + 1) * 128, tsl], HM[r][:], reads=[HMB[r]])


def phase_ffn_down(c, l):
    nc, tr = c.nc, c.tr
    NB, S, NG = c.NB, c.S, c.NG
    with ExitStack() as st:
        def sb(name, shape, dt):
            return st.enter_context(nc.sbuf_tensor(uq("p6b_" + name), list(shape), dt))
        WD = sb("wd", [128, NFF, D], BF16)
        WB = Buf("p7_w")
        load_w_cast(c, WD, WB, c.w_down[l], 0, D, NFF)
        hm = [sb(f"hm{i}", [128, NFF, 512], BF16) for i in range(2)]
        hmb = [Buf(f"p7_hm{i}") for i in range(2)]
        ht = [sb(f"h{i}", [128, KD, 512], F32) for i in range(2)]
        hb = [Buf(f"p7_h{i}") for i in range(2)]
        groups = [(b, g) for b in range(NB) for g in range(NG)]

        def load(gi):
            b, g = groups[gi]
            tsl = slice(g * 512, (g + 1) * 512)
            tr.dma("SP", hm[gi % 2][:], c.hmid[b].rearrange("(k p) t -> p k t", p=128)[:, :, tsl], writes=[hmb[gi % 2]])
            tr.dma("SP", ht[gi % 2][:], c.hT[b].rearrange("(k p) t -> p k t", p=128)[:, :, tsl], writes=[hb[gi % 2]])
        load(0)
        for gi, (b, g) in enumerate(groups):
            tsl = slice(g * 512, (g + 1) * 512)
            if gi + 1 < len(groups):
                load(gi + 1)
            HMt, HMtB, H, HB = hm[gi % 2], hmb[gi % 2], ht[gi % 2], hb[gi % 2]
            for cc in range(KD):
                csl = slice(cc * 128, (cc + 1) * 128)
                PO, POB = c.ps[cc % 4], c.psb[cc % 4]
                tr.mm_group([(lambda e, k=k, PO=PO, csl=csl, HMt=HMt: e.matmul(PO[:], WD[:, k, csl], HMt[:, k, :], start=(k == 0), stop=(k == NFF - 1)))
                             for k in range(NFF)], reads=[WB, HMtB], writes=[POB])
                tr.op("DVE", lambda e, PO=PO, cc=cc, H=H: e.tensor_tensor(out=H[:, cc, :], in0=PO[:], in1=H[:, cc, :], op=ALU.add),
                      reads=[POB, HB], writes=[HB])
            tr.dma("SP", c.hT[b].rearrange("(k p) t -> p k t", p=128)[:, :, tsl], H[:], reads=[HB])


def phase_final(c):
    nc, tr = c.nc, c.tr
    with ExitStack() as st:
        def sb(name, shape, dt):
            return st.enter_context(nc.sbuf_tensor(name, list(shape), dt))
        gcol = sb("p7_g", [128, KD], F32)
        gb = Buf("p7_g")
        tr.dma("SP", gcol[:], c.final_g.rearrange("(k p) -> p k", p=128), writes=[c.b_const], allow_slow_non_contiguous=True)
        ht = [sb(f"p7_h{i}", [128, KD, 512], F32) for i in range(2)]
        hb = [Buf(f"p7_h{i}") for i in range(2)]
        xn = [sb(f"p7_xn{i}", [128, KD, 512], F32) for i in range(2)]
        xnb = [Buf(f"p7_xn{i}") for i in range(2)]
        sq = sb("p7_sq", [128, KD, 512], BF16)
        sqb = Buf("p7_sq")
        rstd = sb("p7_rstd", [128, 512], F32)
        rstdb = Buf("p7_rstd")
        ot = [sb(f"p7_o{i}", [128, D], F32) for i in range(2)]
        ob = [Buf(f"p7_o{i}") for i in range(2)]
        it = 0
        gi = 0
        for b in range(c.NB):
            hT_v = c.hT[b].rearrange("(k p) t -> p k t", p=128)
            for g in range(c.NG):
                H, HB = ht[gi % 2], hb[gi % 2]
                XN, XNB = xn[gi % 2], xnb[gi % 2]
                tr.dma("SP", H[:], hT_v[:, :, g * 512:(g + 1) * 512], writes=[HB])
                emit_rmsnorm(c, H, HB, XN, XNB, gcol, sq, sqb, rstd, rstdb, 4)
                for tb in range(4):
                    O, OB = ot[it % 2], ob[it % 2]
                    for half in range(2):
                        pi = (it * 2 + half) % 4
                        P, PB = c.ps[pi], c.psb[pi]
                        tr.mm_group(
                            [(lambda e, j=j, P=P, XN=XN, half=half, tb=tb: e.transpose(
                                P[:, j * 128:(j + 1) * 128], XN[:, half * 4 + j, tb * 128:(tb + 1) * 128], c.ident[:]))
                             for j in range(4)],
                            reads=[XNB, c.b_const], writes=[PB])
                        if half == 0:
                            tr.op("ACT", lambda e, P=P, O=O, half=half: e.activation(
                                out=O[:, half * 512:(half + 1) * 512], in_=P[:], func=AF.Copy), reads=[PB], writes=[OB])
                        else:
                            tr.op("DVE", lambda e, P=P, O=O, half=half: e.tensor_copy(
                                out=O[:, half * 512:(half + 1) * 512], in_=P[:]), reads=[PB], writes=[OB])
                    t0 = g * 512 + tb * 128
                    tr.dma("SP", c.out[b, t0:t0 + 128, :], O[:], reads=[OB])
                    it += 1
                gi += 1


_CACHE = {}


def host_consts():
    ident = np.eye(128, dtype=np.float32)
    s = np.arange(128)[:, None]
    t = np.arange(128)[None, :]
    masks = np.zeros((128, 4, 128), np.float32)
    masks[:, 0, :] = (s < t)
    masks[:, 1, :] = (s <= t)
    masks[:, 2, :] = (s >= t)
    masks[:, 3, :] = 1.0
    return ident, masks


def kernel(**inputs):
    NCORES = 8
    x = np.ascontiguousarray(inputs["x"], dtype=np.float32)
    B, S, _ = x.shape
    NB = B // NCORES
    depth = inputs["w_in"].shape[0]
    key = (NB, S, depth)
    if key not in _CACHE:
        _CACHE[key] = build(NB, S, depth)
    nc = _CACHE[key]
    ident, masks = host_consts()
    shared = {k: np.ascontiguousarray(v, dtype=np.float32) for k, v in inputs.items() if k != "x"}
    shared["k_ident"] = ident
    shared["k_masks"] = masks
    in_maps = []
    for i in range(NCORES):
        m = dict(shared)
        m["x"] = x[i * NB:(i + 1) * NB]
        in_maps.append(m)
    res = run_bass_kernel_spmd(nc, in_maps, core_ids=list(range(NCORES)))
    return np.concatenate([r["out"] for r in res.results], axis=0)
```

```python
import numpy as np
import ml_dtypes
from contextlib import ExitStack
import concourse.bass as bass
import concourse.mybir as mybir
from concourse.bass_utils import run_bass_kernel_spmd

F32 = mybir.dt.float32
BF16 = mybir.dt.bfloat16
AF = mybir.ActivationFunctionType
ALU = mybir.AluOpType

D = 1024
KD = D // 128
HD = 64
NH = 8
IN_COLS = 7176
D_FF = 2816
NFF = D_FF // 128
EPS = 1e-6
AUG = 70
C_QA, C_KA, C_VA, C_QB, C_KB, C_VB, C_F, C_U, C_V, C_G = 0, 512, 1024, 1536, 2048, 2560, 3072, 3080, 3592, 4104


class Buf:
    __slots__ = ("name", "w", "r")

    def __init__(self, name):
        self.name = name
        self.w = {}
        self.r = {}


class EngState:
    def __init__(self, name, eng, sem):
        self.name = name
        self.eng = eng
        self.sem = sem
        self.key = "E_" + name
        self.count = 0
        self.waited = {}


class Tr:
    def __init__(self, nc, ndma=8):
        self.nc = nc
        self.E = {}
        for name, attr in (("PE", "tensor"), ("ACT", "scalar"), ("DVE", "vector"), ("POOL", "gpsimd"), ("SP", "sync")):
            self.E[name] = EngState(name, getattr(nc, attr), nc.alloc_semaphore("s_" + name))
        self.Q = {}
        for q in ("SP", "POOL"):
            sems = [nc.alloc_semaphore(f"d_{q}_{i}") for i in range(ndma)]
            self.Q[q] = dict(sems=sems, cnts=[0] * ndma, nxt=0)
        self.n_inst = 0

    def _wait(self, E, key, sem, val):
        if E.waited.get(key, 0) >= val:
            return
        E.eng.wait_ge(sem, val)
        E.waited[key] = val

    def _deps(self, E, reads, writes, skip_same_raw=False):
        for b in reads:
            for key, (sem, val) in b.w.items():
                if key == E.key and skip_same_raw:
                    continue
                self._wait(E, key, sem, val)
        for b in writes:
            for key, (sem, val) in b.w.items():
                if key == E.key and skip_same_raw:
                    continue
                self._wait(E, key, sem, val)
            for key, (sem, val) in b.r.items():
                if key == E.key:
                    continue
                self._wait(E, key, sem, val)

    def _stamp(self, key, sem, val, reads, writes):
        for b in reads:
            b.r[key] = (sem, val)
        for b in writes:
            b.w = {key: (sem, val)}
            b.r = {}

    def op(self, eng, fn, reads=(), writes=()):
        E = self.E[eng]
        self._deps(E, reads, writes, skip_same_raw=(eng == "PE"))
        ins = fn(E.eng)
        E.count += 1
        ins.then_inc(E.sem, 1)
        self._stamp(E.key, E.sem, E.count, reads, writes)
        self.n_inst += 1
        return ins

    def mm_group(self, fns, reads=(), writes=()):
        E = self.E["PE"]
        self._deps(E, reads, writes, skip_same_raw=True)
        ins = None
        for fn in fns:
            ins = fn(E.eng)
            self.n_inst += 1
        E.count += 1
        ins.then_inc(E.sem, 1)
        self._stamp(E.key, E.sem, E.count, reads, writes)

    def dma(self, q, out, in_, reads=(), writes=(), part=False, **kw):
        Q = self.Q[q]
        E = self.E[q]
        slot = Q["nxt"]
        Q["nxt"] = (slot + 1) % len(Q["sems"])
        sem = Q["sems"][slot]
        key = f"D_{q}_{slot}"
        if Q["cnts"][slot] > 0:
            self._wait(E, key, sem, Q["cnts"][slot])
        if part:
            self._deps(E, reads, ())
            for b in writes:
                for k2, (s2, v2) in b.r.items():
                    self._wait(E, k2, s2, v2)
        else:
            self._deps(E, reads, writes)
        ins = E.eng.dma_start(out=out, in_=in_, **kw)
        Q["cnts"][slot] += 16
        ins.then_inc(sem, 16)
        if part:
            for b in reads:
                b.r[key] = (sem, Q["cnts"][slot])
            for b in writes:
                b.w[key] = (sem, Q["cnts"][slot])
                b.r = {}
        else:
            self._stamp(key, sem, Q["cnts"][slot], reads, writes)
        self.n_inst += 1

    def barrier(self):
        items = []
        for e in self.E.values():
            if e.count > 0:
                items.append((e.key, e.sem, e.count))
        for q, Q in self.Q.items():
            for i, s in enumerate(Q["sems"]):
                if Q["cnts"][i] > 0:
                    items.append((f"D_{q}_{i}", s, Q["cnts"][i]))
        for e in self.E.values():
            for key, sem, val in items:
                if key == e.key:
                    continue
                self._wait(e, key, sem, val)

    def finish(self, eng="SP"):
        e = self.E[eng]
        for q, Q in self.Q.items():
            for i, s in enumerate(Q["sems"]):
                if Q["cnts"][i] > 0:
                    self._wait(e, f"D_{q}_{i}", s, Q["cnts"][i])
        for e2 in self.E.values():
            if e2.count > 0 and e2 is not e:
                self._wait(e, e2.key, e2.sem, e2.count)


class Ctx:
    pass


_UID = [0]


def uq(name):
    _UID[0] += 1
    return f"{name}_{_UID[0]}"


def build(NB, S, depth, dbg=None):
    assert S % 512 == 0
    NG = S // 512
    NBLK = S // 128
    nc = bass.Bass("TRN2", target_bir_lowering=False)
    tr = Tr(nc)
    c = Ctx()
    c.nc, c.tr, c.NB, c.S, c.NG, c.NBLK, c.depth = nc, tr, NB, S, NG, NBLK, depth

    def din(name, shape, dt=F32):
        return nc.dram_tensor(name, list(shape), dt, kind="ExternalInput").ap()

    def dscr(name, shape, dt):
        kind = "ExternalOutput" if (dbg and name in dbg) else "Internal"
        return nc.dram_tensor(name, list(shape), dt, kind=kind).ap()

    L = max(depth, 1)
    c.x = din("x", [NB, S, D])
    c.norm_mix_g = din("norm_mix_g", [L, D])
    c.w_in = din("w_in", [L, D, IN_COLS])
    c.fox_bias = din("fox_bias", [L, NH])
    c.sgu_ln_g = din("sgu_ln_g", [L, 512])
    c.sgu_w = din("sgu_w", [L, 8, 128, 128])
    c.sgu_b = din("sgu_b", [L, 8, 128])
    c.w_branch = din("w_branch", [L, 3, 512, D])
    c.w_out = din("w_out", [L, D, D])
    c.norm_ffn_g = din("norm_ffn_g", [L, D])
    c.w_up = din("w_up", [L, D, 2 * D_FF])
    c.conv_w = din("conv_w", [L, 3, D_FF])
    c.conv_b = din("conv_b", [L, D_FF])
    c.w_down = din("w_down", [L, D_FF, D])
    c.final_g = din("final_g", [D])
    c.k_ident = din("k_ident", [128, 128])
    c.k_masks = din("k_masks", [128, 4, 128])
    c.out = nc.dram_tensor("out", [NB, S, D], F32, kind="ExternalOutput").ap()

    c.hT = dscr("hT", [NB, D, S], F32)
    c.xnT = dscr("xnT", [NB, D, S], BF16)
    c.QaT = dscr("QaT", [NB, 512, S], BF16)
    c.KaT = dscr("KaT", [NB, 512, S], BF16)
    c.Va = dscr("Va", [NB, S, 512], BF16)
    c.QbT = dscr("QbT", [NB, NH, AUG, S], BF16)
    c.KbT = dscr("KbT", [NB, NH, AUG, S], BF16)
    c.Vb = dscr("Vb", [NB, S, 512], BF16)
    c.YaT = dscr("YaT", [NB, 512, S], BF16)
    c.YbT = dscr("YbT", [NB, 512, S], BF16)
    c.YcT = dscr("YcT", [NB, 512, S], BF16)
    c.hmid = dscr("hmid", [NB, D_FF, S], BF16)

    with ExitStack() as gs:
        def sb(name, shape, dt):
            return gs.enter_context(nc.sbuf_tensor(name, list(shape), dt))

        c.ident = sb("ident", [128, 128], F32)
        c.ones_bf = sb("ones_bf", [128, 128], BF16)
        c.b_const = Buf("const")
        tr.dma("SP", c.ident[:], c.k_ident[:, :], writes=[c.b_const])
        tr.op("DVE", lambda e: e.memset(c.ones_bf[:], 1.0), writes=[c.b_const])
        c.eps_col = sb("eps_col", [128, 1], F32)
        c.one_col = sb("one_col", [128, 1], F32)
        tr.op("DVE", lambda e: e.memset(c.one_col[:], 1.0), writes=[c.b_const])
        tr.op("DVE", lambda e: e.memset(c.eps_col[:], EPS), writes=[c.b_const])
        c.ps2 = [gs.enter_context(nc.psum_tensor(f"ps{i}", [128, 1024], F32)) for i in range(4)]
        c.ps = [c.ps2[i // 2][:, (i % 2) * 512:(i % 2 + 1) * 512] for i in range(8)]
        c.psb = [Buf(f"ps{i}") for i in range(8)]

        c.ones_big = sb("ones_big", [8, 3 * 512], BF16)
        tr.op("DVE", lambda e: e.memset(c.ones_big[:], 1.0), writes=[c.b_const])
        c.masks = sb("masks", [128, 4, 128], F32)
        tr.dma("SP", c.masks[:], c.k_masks[:, :, :], writes=[c.b_const])
        c.masks_bf = sb("masks_bf", [128, 4, 128], BF16)
        tr.op("DVE", lambda e: e.tensor_copy(out=c.masks_bf[:], in_=c.masks[:]), reads=[c.b_const], writes=[c.b_const])
        c.negtri_bf = sb("negtri_bf", [128, 2, 128], BF16)
        tr.op("DVE", lambda e: e.tensor_scalar(out=c.negtri_bf[:], in0=c.masks[:, 2:4, :], scalar1=-1.0, scalar2=None, op0=ALU.mult),
              reads=[c.b_const], writes=[c.b_const])
        c.ident_bf = sb("ident_bf", [128, 128], BF16)
        tr.op("DVE", lambda e: e.tensor_copy(out=c.ident_bf[:], in_=c.ident[:]), reads=[c.b_const], writes=[c.b_const])
        c.nb_sb = sb("nb_sb", [128, 128], BF16)
        c.nb_fox = sb("nb_fox", [128, 128], BF16)
        tr.op("DVE", lambda e: e.tensor_scalar(out=c.nb_sb[:], in0=c.masks[:, 0, :], scalar1=-1.0, scalar2=30000.0, op0=ALU.add, op1=ALU.mult),
              reads=[c.b_const], writes=[c.b_const])
        tr.op("DVE", lambda e: e.tensor_scalar(out=c.nb_fox[:], in0=c.masks[:, 1, :], scalar1=-1.0, scalar2=30000.0, op0=ALU.add, op1=ALU.mult),
              reads=[c.b_const], writes=[c.b_const])
        init_small_vectors(c, gs)
        phase_in_transpose(c)
        tr.barrier()
        stop_after = (dbg or {}).get("stop_after") if isinstance(dbg, dict) else None
        for l in range(depth):
            phase_inproj(c, l)
            tr.barrier()
            if stop_after == "inproj":
                break
            phase_attn(c, l, "sb")
            tr.barrier()
            phase_attn(c, l, "fox")
            tr.barrier()
            if stop_after == "attn":
                break
            phase_merge(c, l)
            tr.barrier()
            if stop_after == "merge":
                break
            phase_ffn_up(c, l)
            tr.barrier()
            phase_ffn_down(c, l)
            tr.barrier()
        phase_final(c)
        tr.finish("SP")
    return nc


def phase_in_transpose(c):
    nc, tr = c.nc, c.tr
    with ExitStack() as st:
        xt = [st.enter_context(nc.sbuf_tensor(f"p0_x{i}", [128, D], F32)) for i in range(2)]
        xb = [Buf(f"p0_x{i}") for i in range(2)]
        ht = [st.enter_context(nc.sbuf_tensor(f"p0_h{i}", [128, KD, 512], F32)) for i in range(2)]
        hb = [Buf(f"p0_h{i}") for i in range(2)]
        it = 0
        for b in range(c.NB):
            hT_v = c.hT[b].rearrange("(k p) t -> p k t", p=128)
            for g in range(c.NG):
                H, HB = ht[g % 2], hb[g % 2]
                for tb in range(4):
                    X, XB = xt[it % 2], xb[it % 2]
                    t0 = g * 512 + tb * 128
                    tr.dma("SP", X[:], c.x[b, t0:t0 + 128, :], writes=[XB])
                    for half in range(2):
                        pi = (it * 2 + half) % 4
                        P, PB = c.ps[pi], c.psb[pi]
                        tr.mm_group(
                            [(lambda e, j=j, P=P, X=X, half=half: e.transpose(
                                P[:, j * 128:(j + 1) * 128], X[:, (half * 4 + j) * 128:(half * 4 + j + 1) * 128], c.ident[:]))
                             for j in range(4)],
                            reads=[XB, c.b_const], writes=[PB])
                        eng = "ACT" if half == 0 else "DVE"
                        if eng == "ACT":
                            tr.op("ACT", lambda e, P=P, H=H, half=half, tb=tb: e.activation(
                                out=H[:, half * 4:(half + 1) * 4, tb * 128:(tb + 1) * 128],
                                in_=P[:].rearrange("p (j t) -> p j t", j=4), func=AF.Copy),
                                reads=[PB], writes=[HB])
                        else:
                            tr.op("DVE", lambda e, P=P, H=H, half=half, tb=tb: e.tensor_copy(
                                out=H[:, half * 4:(half + 1) * 4, tb * 128:(tb + 1) * 128],
                                in_=P[:].rearrange("p (j t) -> p j t", j=4)),
                                reads=[PB], writes=[HB])
                    it += 1
                tr.dma("SP", hT_v[:, :, g * 512:(g + 1) * 512], H[:], reads=[HB])


def emit_rmsnorm(c, H, HB, XN, XNB, g_col, sq, sqb, rstd, rstdb, pbank, out_dt_note=None):
    nc, tr = c.nc, c.tr
    P, PB = c.ps[pbank], c.psb[pbank]
    tr.op("ACT", lambda e: e.activation(out=sq[:], in_=H[:], func=AF.Square), reads=[HB], writes=[sqb])
    tr.mm_group([(lambda e, k=k: e.matmul(P[:], c.ones_bf[:], sq[:, k, :], start=(k == 0), stop=(k == KD - 1)))
                 for k in range(KD)], reads=[sqb, c.b_const], writes=[PB])
    tr.op("ACT", lambda e: e.activation(out=rstd[:], in_=P[:], func=AF.Ln, scale=1.0 / D, bias=c.eps_col[:, 0:1]),
          reads=[PB, c.b_const], writes=[rstdb])
    tr.op("ACT", lambda e: e.activation(out=rstd[:], in_=rstd[:], func=AF.Exp, scale=-0.5),
          reads=[rstdb], writes=[rstdb])
    for k in range(KD):
        tr.op("DVE", lambda e, k=k: e.scalar_tensor_tensor(
            out=XN[:, k, :], in0=H[:, k, :], scalar=g_col[:, k:k + 1], in1=rstd[:], op0=ALU.mult, op1=ALU.mult),
            reads=[HB, rstdb, c.b_const], writes=[XNB])


def init_small_vectors(c, gs):
    nc, tr = c.nc, c.tr
    L = max(c.depth, 1)
    c.smallv = gs.enter_context(nc.sbuf_tensor("smallv", [128, 512], F32))
    c.o_gmix, c.o_gffn, c.o_fg, c.o_cb, c.o_cw = 0, L * KD, 2 * L * KD, 2 * L * KD + KD, 2 * L * KD + KD + L * NFF
    assert c.o_cw + L * 3 * NFF <= 512
    P, PB = c.ps[0], c.psb[0]
    with ExitStack() as st:
        rows = st.enter_context(nc.sbuf_tensor(uq("sv_rows"), [128, 8, 128], F32))
        rb = Buf("sv_rows")
        srcs = [(c.norm_mix_g.rearrange("l (k p) -> (l k) p", p=128), L * KD, c.o_gmix),
                (c.norm_ffn_g.rearrange("l (k p) -> (l k) p", p=128), L * KD, c.o_gffn),
                (c.final_g.rearrange("(k p) -> k p", p=128), KD, c.o_fg),
                (c.conv_b.rearrange("l (k p) -> (l k) p", p=128), L * NFF, c.o_cb)]
        for l in range(L):
            srcs.append((c.conv_w[l].rearrange("i (k p) -> (i k) p", p=128), 3 * NFF, c.o_cw + l * 3 * NFF))
        assert len(srcs) <= 8
        for i, (src, n, off) in enumerate(srcs):
            tr.dma("SP", rows[0:n, i, :], src, writes=[rb])
        fns = []
        for i, (src, n, off) in enumerate(srcs):
            fns.append(lambda e, i=i, n=n, off=off: e.transpose(P[:, off:off + n], rows[0:n, i, :], c.ident[0:n, 0:n]))
        tr.mm_group(fns, reads=[rb, c.b_const], writes=[PB])
        tr.op("DVE", lambda e: e.tensor_copy(out=c.smallv[:, 0:c.o_cw + L * 3 * NFF], in_=P[:, 0:c.o_cw + L * 3 * NFF]),
              reads=[PB], writes=[c.b_const])
    for b in range(c.NB):
        for g in range(c.NG):
            src = c.ones_big[:].rearrange("p (j t) -> p j t", j=3)
            tr.dma("SP", c.QbT[b, :, 67:70, g * 512:(g + 1) * 512], src, reads=[c.b_const])
            tr.dma("SP", c.KbT[b, :, 64:67, g * 512:(g + 1) * 512], src, reads=[c.b_const])


def load_w_cast(c, dst, dstb, src_rows, col0, ncols, kchunks, dcol0=0):
    tr = c.tr
    for k in range(kchunks):
        o = 0
        while o < ncols:
            n = min(2048, ncols - o)
            tr.dma("POOL", dst[:, k, dcol0 + o:dcol0 + o + n], src_rows[k * 128:(k + 1) * 128, col0 + o:col0 + o + n], writes=[dstb], part=True)
            o += n


def phase_inproj(c, l):
    nc, tr = c.nc, c.tr
    NB, S, NG = c.NB, c.S, c.NG
    NW = C_G
    with ExitStack() as st:
        def sb(name, shape, dt):
            return st.enter_context(nc.sbuf_tensor(uq("p1_" + name), list(shape), dt))
        W = sb("w", [128, KD, NW], BF16)
        WB = Buf("p1_w")
        load_w_cast(c, W, WB, c.w_in[l], 0, NW, KD)
        wst = sb("wst", [128, 8, 128], BF16)
        wstb = Buf("p1_wst")
        wraw = sb("wraw", [128, 8, 128], F32)
        wrawb = Buf("p1_wraw")
        tr.dma("SP", wraw[:], c.sgu_w[l].rearrange("g t s -> t g s"), writes=[wrawb])
        for half in range(2):
            P, PB = c.ps[6 + half], c.psb[6 + half]
            tr.mm_group([(lambda e, j=j, P=P, half=half: e.transpose(P[:, j * 128:(j + 1) * 128], wraw[:, half * 4 + j, :], c.ident[:]))
                         for j in range(4)], reads=[wrawb, c.b_const], writes=[PB])
            tr.op("DVE", lambda e, P=P, half=half: e.tensor_copy(out=wst[:, half * 4:(half + 1) * 4, :],
                                                                 in_=P[:].rearrange("p (j t) -> p j t", j=4)),
                  reads=[PB], writes=[wstb])
        tr.op("DVE", lambda e: e.memset(wst[64:128, :, 0:64], 0.0), reads=[wstb], writes=[wstb])
        sgub32 = sb("sgub32", [1, 1024], F32)
        sgub_hi = sb("sgub_hi", [1, 1024], BF16)
        sgub_lo = sb("sgub_lo", [1, 1024], BF16)
        sgubb = Buf("p1_sgub")
        tr.dma("SP", sgub32[:], c.sgu_b[l:l + 1].rearrange("o g t -> o (g t)"), writes=[sgubb])
        tr.op("DVE", lambda e: e.tensor_copy(out=sgub_hi[:], in_=sgub32[:]), reads=[sgubb], writes=[sgubb])
        tr.op("DVE", lambda e: e.tensor_tensor(out=sgub32[:], in0=sgub32[:], in1=sgub_hi[:], op=ALU.subtract), reads=[sgubb], writes=[sgubb])
        tr.op("DVE", lambda e: e.tensor_copy(out=sgub_lo[:], in_=sgub32[:]), reads=[sgubb], writes=[sgubb])
        lng = sb("lng", [128, 512], F32)
        foxb = sb("foxb", [8, 1], F32)
        ones8 = sb("ones8", [8, 512], F32)
        lb = Buf("p1_lng")
        tr.dma("SP", lng[:], c.sgu_ln_g[l].partition_broadcast(128), writes=[lb])
        tr.dma("SP", foxb[:], c.fox_bias[l].rearrange("(h o) -> h o", o=1), writes=[lb])
        tr.op("DVE", lambda e: e.memset(ones8[:], 1.0), reads=[lb], writes=[lb])
        gcol = c.smallv[:, c.o_gmix + l * KD:c.o_gmix + (l + 1) * KD]

        ht = [sb(f"h{i}", [128, KD, 512], F32) for i in range(2)]
        hb = [Buf(f"p1_h{i}") for i in range(2)]
        xn = [sb(f"xn{i}", [128, KD, 512], BF16) for i in range(2)]
        xnb = [Buf(f"p1_xn{i}") for i in range(2)]
        sq = sb("sq", [128, KD, 512], BF16)
        sqb = Buf("p1_sq")
        rstd = sb("rstd", [128, 512], F32)
        rstdb = Buf("p1_rstd")
        fam = {n_: sb("fam_" + n_, [128, 4, 512], BF16) for n_ in ("qa", "ka", "qb", "kb", "va", "vb")}
        famb = {n_: Buf("p1_fam_" + n_) for n_ in fam}
        ut = sb("ut", [128, 4, 512], F32)
        utb = Buf("p1_ut")
        v32 = sb("v32", [128, 4, 512], F32)
        v32b = [Buf(f"p1_v32_{i}") for i in range(4)]
        vn = sb("vn", [128, 2, 512], BF16)
        vnb = [Buf(f"p1_vn{i}") for i in range(2)]
        stt = sb("stt", [128, 4, 16], F32)
        sttb = Buf("p1_stt")
        yc = sb("yc", [128, 4, 512], BF16)
        ycb = Buf("p1_yc")
        fl = sb("fl", [8, 512], F32)
        flb = Buf("p1_fl")
        lc = sb("lc", [8, 512], F32)
        lcx = fl
        carry = sb("carry", [8, 1], F32)
        lcb = Buf("p1_lc")
        pcs = sb("pc", [8, 6, 512], BF16)
        pcsb = Buf("p1_pcs")

        groups = [(b, g) for b in range(NB) for g in range(NG)]

        def load_h(gi):
            b, g = groups[gi]
            tr.dma("SP", ht[gi % 2][:], c.hT[b].rearrange("(k p) t -> p k t", p=128)[:, :, g * 512:(g + 1) * 512], writes=[hb[gi % 2]])

        def norm(gi):
            emit_rmsnorm(c, ht[gi % 2], hb[gi % 2], xn[gi % 2], xnb[gi % 2], gcol, sq, sqb, rstd, rstdb, 0)

        load_h(0)
        norm(0)
        ifm = 0
        itm = 0
        ibank_fm = 0
        ibank_tm = 0
        for gi, (b, g) in enumerate(groups):
            XN, XNB = xn[gi % 2], xnb[gi % 2]
            tsl = slice(g * 512, (g + 1) * 512)
            if gi + 1 < len(groups):
                load_h(gi + 1)
            if g == 0:
                tr.op("DVE", lambda e: e.memset(carry[:], 0.0), reads=[lcb], writes=[lcb])
            tr.dma("SP", c.xnT[b].rearrange("(k p) t -> p k t", p=128)[:, :, tsl], XN[:], reads=[XNB])
            fm_jobs = []
            for j in range(4):
                fm_jobs.append((C_QA + j * 128, "qa", 0.125, j))
            for j in range(4):
                fm_jobs.append((C_KA + j * 128, "ka", 1.0, j))
            for j in range(4):
                fm_jobs.append((C_QB + j * 128, "qb", 0.125, j))
            for j in range(4):
                fm_jobs.append((C_KB + j * 128, "kb", 1.0, j))
            for j in range(4):
                fm_jobs.append((C_U + j * 128, "gelu", 1.0, j))
            for (col, kind, scale, j) in fm_jobs:
                pi = 1 + (ibank_fm % 2)
                ibank_fm += 1
                P, PB = c.ps[pi], c.psb[pi]
                tr.mm_group([(lambda e, k=k, P=P, col=col, XN=XN: e.matmul(
                    P[:], W[:, k, col:col + 128], XN[:, k, :], start=(k == 0), stop=(k == KD - 1))) for k in range(KD)],
                    reads=[WB, XNB], writes=[PB])
                if kind == "gelu":
                    tr.op("ACT", lambda e, P=P, j=j: e.activation(out=ut[:, j, :], in_=P[:], func=AF.Gelu),
                          reads=[PB], writes=[utb])
                    continue
                F, FB = fam[kind], famb[kind]
                if ifm % 2 == 0:
                    tr.op("ACT", lambda e, P=P, F=F, scale=scale, j=j: e.activation(out=F[:, j, :], in_=P[:], func=AF.Copy, scale=scale),
                          reads=[PB], writes=[FB])
                else:
                    tr.op("DVE", lambda e, P=P, F=F, scale=scale, j=j: e.tensor_scalar(
                        out=F[:, j, :], in0=P[:], scalar1=scale, scalar2=None, op0=ALU.mult), reads=[PB], writes=[FB])
                ifm += 1
                if j == 3:
                    if kind in ("qa", "ka"):
                        T = c.QaT if kind == "qa" else c.KaT
                        tr.dma("SP", T[b].rearrange("(j p) t -> p j t", p=128)[:, :, tsl], F[:], reads=[FB])
                    else:
                        T = c.QbT if kind == "qb" else c.KbT
                        Tv = T[b].rearrange("(j hh) r t -> hh j r t", hh=2)
                        for hh in range(2):
                            tr.dma("SP", Tv[hh, :, 0:64, tsl].rearrange("j d t -> d j t"), F[hh * 64:(hh + 1) * 64, :, :], reads=[FB])
            PF, PFB = c.ps[5], c.psb[5]
            tr.mm_group([(lambda e, k=k, XN=XN: e.matmul(PF[0:8, :], W[:, k, C_F:C_F + 8], XN[:, k, :], start=(k == 0), stop=(k == KD - 1)))
                         for k in range(KD)], reads=[WB, XNB], writes=[PFB])
            tr.op("DVE", lambda e: e.tensor_scalar(out=fl[:], in0=PF[0:8, :], scalar1=foxb[:, 0:1], scalar2=None, op0=ALU.add),
                  reads=[PFB, lb], writes=[flb])
            for tb in range(4):
                bsl = slice(tb * 128, (tb + 1) * 128)
                for (col, kind, dstT) in ((C_VA, "va", c.Va), (C_VB, "vb", c.Vb), (C_V, "gelu", None)):
                    pi = 3 + (ibank_tm % 2)
                    ibank_tm += 1
                    P, PB = c.ps[pi], c.psb[pi]
                    tr.mm_group([(lambda e, k=k, P=P, col=col, XN=XN, bsl=bsl: e.matmul(
                        P[:], XN[:, k, bsl], W[:, k, col:col + 512], start=(k == 0), stop=(k == KD - 1))) for k in range(KD)],
                        reads=[WB, XNB], writes=[PB])
                    if kind == "gelu":
                        tr.op("ACT", lambda e, P=P, tb=tb: e.activation(out=v32[:, tb, :], in_=P[:], func=AF.Gelu),
                              reads=[PB], writes=[v32b[tb]])
                        tr.op("DVE", lambda e, tb=tb: e.bn_stats(out=stt[:, tb, 0:6], in_=v32[:, tb, :]), reads=[v32b[tb]], writes=[sttb])
                        tr.op("DVE", lambda e, tb=tb: e.bn_aggr(out=stt[:, tb, 6:8], in_=stt[:, tb, 0:6]), reads=[sttb], writes=[sttb])
                    else:
                        T_, TB_ = fam[kind], famb[kind]
                        if itm % 2 == 0:
                            tr.op("DVE", lambda e, P=P, T_=T_, tb=tb: e.tensor_copy(out=T_[:, tb, :], in_=P[:]), reads=[PB], writes=[TB_])
                        else:
                            tr.op("ACT", lambda e, P=P, T_=T_, tb=tb: e.activation(out=T_[:, tb, :], in_=P[:], func=AF.Copy), reads=[PB], writes=[TB_])
                        itm += 1
                        if tb == 3:
                            tr.dma("SP", dstT[b, g * 512:(g + 1) * 512, :].rearrange("(tb p) c -> p tb c", p=128), T_[:], reads=[TB_])
            if gi + 1 < len(groups):
                norm(gi + 1)
            tr.op("ACT", lambda e: e.activation(out=stt[:, :, 8:9], in_=stt[:, :, 7:8], func=AF.Ln, bias=c.eps_col[:, 0:1]),
                  reads=[sttb, c.b_const], writes=[sttb])
            tr.op("ACT", lambda e: e.activation(out=stt[:, :, 9:10], in_=stt[:, :, 8:9], func=AF.Exp, scale=-0.5),
                  reads=[sttb], writes=[sttb])
            tr.op("DVE", lambda e: e.scalar_tensor_tensor(out=stt[:, :, 10:11], in0=stt[:, :, 6:7], scalar=-1.0, in1=stt[:, :, 9:10],
                                                          op0=ALU.mult, op1=ALU.mult), reads=[sttb], writes=[sttb])
            tr.op("ACT", lambda e: e.activation(out=fl[:], in_=fl[:], func=AF.Exp, scale=-1.0), reads=[flb], writes=[flb])
            tr.op("ACT", lambda e: e.activation(out=fl[:], in_=fl[:], func=AF.Ln, bias=c.one_col[0:8, 0:1]), reads=[flb, c.b_const], writes=[flb])
            for tb in range(4):
                bsl = slice(tb * 128, (tb + 1) * 128)
                tr.op("ACT", lambda e, tb=tb: e.activation(out=v32[:, tb, :], in_=v32[:, tb, :], func=AF.Identity,
                                                          scale=stt[:, tb, 9:10], bias=stt[:, tb, 10:11]),
                      reads=[sttb, v32b[tb]], writes=[v32b[tb]])
                tr.op("DVE", lambda e, tb=tb: e.tensor_tensor(out=vn[:, tb % 2, :], in0=v32[:, tb, :], in1=lng[:], op=ALU.mult),
                      reads=[v32b[tb], lb], writes=[vnb[tb % 2]])
                PM, PMB = c.ps[6 + tb % 2], c.psb[6 + tb % 2]
                fns = []
                for gg in range(8):
                    cc, hh = gg // 2, gg % 2
                    o = PM[hh * 64:(hh + 1) * 64, cc * 128:(cc + 1) * 128]
                    fns.append(lambda e, o=o, gg=gg, tb=tb: e.matmul(o, vn[:, tb % 2, gg * 64:(gg + 1) * 64], wst[:, gg, :], start=True, stop=False))
                    fns.append(lambda e, o=o, gg=gg: e.matmul(o, c.ones_bf[0:1, 0:64], sgub_hi[0:1, gg * 128:(gg + 1) * 128], start=False, stop=False))
                    fns.append(lambda e, o=o, gg=gg: e.matmul(o, c.ones_bf[0:1, 0:64], sgub_lo[0:1, gg * 128:(gg + 1) * 128], start=False, stop=True))
                tr.mm_group(fns, reads=[vnb[tb % 2], wstb, sgubb, c.b_const], writes=[PMB])
                tr.op("DVE", lambda e, PM=PM, bsl=bsl: e.tensor_tensor(
                    out=yc[:, :, bsl], in0=PM[:].rearrange("p (c t) -> p c t", c=4), in1=ut[:, :, bsl], op=ALU.mult),
                    reads=[PMB, utb], writes=[ycb])
            tr.dma("SP", c.YcT[b].rearrange("(c p) t -> p c t", p=128)[:, :, tsl], yc[:], reads=[ycb])
            tr.op("DVE", lambda e: e.tensor_tensor_scan(out=lc[:], data0=ones8[:], data1=fl[:], initial=carry[:, 0:1], op0=ALU.mult, op1=ALU.add),
                  reads=[flb, lcb, lb], writes=[lcb])
            tr.op("DVE", lambda e: e.tensor_copy(out=carry[:], in_=lc[:, 511:512]), reads=[lcb], writes=[lcb])
            tr.op("DVE", lambda e: e.tensor_scalar(out=lcx[:], in0=lc[:], scalar1=-1.0, scalar2=None, op0=ALU.mult), reads=[lcb, flb], writes=[lcb, flb])
            for j in range(3):
                tr.op("DVE", lambda e, j=j: e.tensor_copy(out=pcs[:, j, :], in_=lcx[:]), reads=[lcb, flb, pcsb], writes=[pcsb])
                tr.op("DVE", lambda e, j=j: e.tensor_scalar(out=pcs[:, 3 + j, :], in0=pcs[:, j, :], scalar1=-1.0, scalar2=None, op0=ALU.mult),
                      reads=[pcsb], writes=[pcsb])
                if j < 2:
                    tr.op("DVE", lambda e, j=j: e.tensor_tensor(out=lcx[:], in0=lcx[:], in1=pcs[:, j, :], op=ALU.subtract),
                          reads=[lcb, flb, pcsb], writes=[lcb, flb])
            tr.dma("SP", c.QbT[b, :, 64:67, tsl], pcs[:, 0:3, :], reads=[pcsb])
            tr.dma("SP", c.KbT[b, :, 67:70, tsl], pcs[:, 3:6, :], reads=[pcsb])


def phase_attn(c, l, kind):
    nc, tr = c.nc, c.tr
    NB, S, NG, NBLK = c.NB, c.S, c.NG, c.NBLK
    sb_mode = (kind == "sb")
    KR = 128 if sb_mode else AUG
    NHU = NH // 2 if sb_mode else NH
    with ExitStack() as st:
        def sbt(name, shape, dt):
            return st.enter_context(nc.sbuf_tensor(uq(f"p{kind}_" + name), list(shape), dt))
        Vts = [sbt(f"v{i}", [128, NBLK, 512] if sb_mode else [128, NBLK, NH, 128], BF16) for i in range(2)]
        VBs = [Buf(f"at_v{i}") for i in range(2)]
        KT = [sbt(f"k{i}", [KR, S], BF16) for i in range(2)]
        QT = [sbt(f"q{i}", [KR, S], BF16) for i in range(2)]
        QTB2 = [sbt(f"qb{i}", [KR, S], BF16) for i in range(2)] if sb_mode else None
        KQB = [Buf(f"at_kq{i}") for i in range(2)]
        NR = 3
        E32 = [sbt(f"e{i}", [128, 2, 512], F32) for i in range(NR)] if sb_mode else None
        LS = [sbt(f"ls{i}", [128, 2, 512], BF16) for i in range(NR)] if sb_mode else None
        LSB = [Buf(f"at_ls{i}") for i in range(NR)]
        SS = [sbt(f"ss{i}", [128, 2, 512], BF16) for i in range(2)] if sb_mode else None
        SSB = [Buf(f"at_ss{i}") for i in range(2)]
        WT = [sbt(f"w{i}", [128, 2, 512] if sb_mode else [128, 512], BF16) for i in range(NR)]
        WTB = [Buf(f"at_w{i}") for i in range(NR)]
        YS = [sbt(f"y{i}", [128 if sb_mode else 64, 512], BF16) for i in range(2)]
        YSB = [Buf(f"at_y{i}") for i in range(2)]
        RC = [sbt(f"r{i}", [128, 512], F32) for i in range(2)] if not sb_mode else None
        RC0 = [sbt(f"r0_{i}", [64, 512], F32) for i in range(2)] if not sb_mode else None
        RCB = [Buf(f"at_r{i}") for i in range(2)]
        RC0B = [Buf(f"at_r0_{i}") for i in range(2)]
        nbias = c.nb_sb if sb_mode else c.nb_fox
        Ksrc = c.KaT if sb_mode else None
        dstT = c.YaT if sb_mode else c.YbT
        Vsrc = c.Va if sb_mode else c.Vb

        if sb_mode:
            for i_ in range(2):
                tr.op("POOL", lambda e, i_=i_: e.memset(QT[i_][64:128, :], 0.0), writes=[KQB[i_]])
                tr.op("POOL", lambda e, i_=i_: e.memset(QTB2[i_][0:64, :], 0.0), writes=[KQB[i_]])
        units = []
        ihead = 0
        igroup = 0
        for b in range(NB):
            for h in range(NHU):
                for g in range(NG):
                    kbs = list(range(4 * g + 3, -1, -1)) if sb_mode else list(range(0, 4 * g + 4))
                    for j, kb in enumerate(kbs):
                        units.append(dict(b=b, h=h, g=g, kb=kb, j=j, first=(j == 0), last=(j == len(kbs) - 1),
                                          ihead=ihead, igroup=igroup, newhead=(g == 0 and j == 0), newbatch=(h == 0 and g == 0 and j == 0)))
                    igroup += 1
                ihead += 1
        n = len(units)

        def stage_A(i):
            u = units[i]
            b, h, g, kb = u["b"], u["h"], u["g"], u["kb"]
            if u["newbatch"]:
                if sb_mode:
                    tr.dma("SP", Vts[b % 2][:], Vsrc[b].rearrange("(n p) c -> p n c", p=128), writes=[VBs[b % 2]])
                else:
                    tr.op("POOL", lambda e: e.memset(Vts[b % 2][:], 1.0), writes=[VBs[b % 2]])
                    vsrc4 = Vsrc[b].rearrange("(n p) (h d) -> p n h d", p=128, d=64)
                    for n0 in range(NBLK):
                        tr.dma("SP", Vts[b % 2][:, n0, :, 0:64], vsrc4[:, n0, :, :], writes=[VBs[b % 2]], part=(n0 > 0))
            if u["newhead"]:
                hb_ = u["ihead"] % 2
                if sb_mode:
                    tr.dma("SP", KT[hb_][:], c.KaT[b, h * 128:(h + 1) * 128, :], writes=[KQB[hb_]])
                    tr.dma("SP", QT[hb_][0:64, :], c.QaT[b, h * 128:h * 128 + 64, :], writes=[KQB[hb_]])
                    tr.dma("SP", QTB2[hb_][64:128, :], c.QaT[b, h * 128 + 64:(h + 1) * 128, :], writes=[KQB[hb_]])
                else:
                    tr.dma("SP", KT[hb_][:], c.KbT[b, h, :, :], writes=[KQB[hb_]])
                    tr.dma("SP", QT[hb_][:], c.QbT[b, h, :, :], writes=[KQB[hb_]])
            hb_ = u["ihead"] % 2
            c0 = max(0, kb - 4 * g) * 128
            diag = kb >= 4 * g
            u["c0"], u["diag"] = c0, diag
            if sb_mode:
                PZ2 = c.ps2[i % 3]
                PZBs = [c.psb[2 * (i % 3)], c.psb[2 * (i % 3) + 1]]
                u["PZ2"], u["PZBs"] = PZ2, PZBs
                fns = []
                for hh in range(2):
                    Qz = QT[hb_] if hh == 0 else QTB2[hb_]
                    fns.append(lambda e, hh=hh, Qz=Qz: e.matmul(PZ2[:, hh * 512 + c0:(hh + 1) * 512], KT[hb_][:, kb * 128:(kb + 1) * 128],
                                                                Qz[:, g * 512 + c0:(g + 1) * 512], start=True, stop=True))
                    if diag:
                        fns.append(lambda e, hh=hh: e.matmul(PZ2[:, hh * 512 + c0:hh * 512 + c0 + 128], c.ident_bf[:], nbias[:],
                                                             start=False, stop=True, skip_group_check=True))
                tr.mm_group(fns, reads=[KQB[hb_], c.b_const], writes=PZBs)
                return
            PZ, PZB = c.ps[i % 4], c.psb[i % 4]
            u["PZ"], u["PZB"] = PZ, PZB
            fns = [lambda e: e.matmul(PZ[:, c0:512], KT[hb_][:, kb * 128:(kb + 1) * 128], QT[hb_][:, g * 512 + c0:(g + 1) * 512],
                                      start=True, stop=True)]
            if diag:
                fns.append(lambda e: e.matmul(PZ[:, c0:c0 + 128], c.ident_bf[:], nbias[:], start=False, stop=True, skip_group_check=True))
            tr.mm_group(fns, reads=[KQB[hb_], c.b_const], writes=[PZB])

        def stage_B(i):
            u = units[i]
            c0 = u["c0"]
            r = i % NR
            if sb_mode:
                PZv = u["PZ2"][:].rearrange("p (h t) -> p h t", h=2)[:, :, c0:512]
                tr.op("ACT", lambda e: e.activation(out=E32[r][:, :, c0:512], in_=PZv, func=AF.Exp), reads=u["PZBs"], writes=[LSB[r]])
                tr.op("ACT", lambda e: e.activation(out=LS[r][:, :, c0:512], in_=E32[r][:, :, c0:512], func=AF.Ln, bias=c.one_col[:, 0:1]),
                      reads=[LSB[r], c.b_const], writes=[LSB[r]])
            else:
                PZ, PZB = u["PZ"], u["PZB"]
                tr.op("ACT", lambda e: e.activation(out=WT[r][:, c0:512], in_=PZ[:, c0:512], func=AF.Exp), reads=[PZB], writes=[WTB[r]])

        def stage_C(i):
            if not sb_mode:
                return
            u = units[i]
            c0, PZ2, PZBs, j = u["c0"], u["PZ2"], u["PZBs"], u["j"]
            r = i % NR
            if u["first"]:
                for k2 in range(2):
                    tr.op("POOL", lambda e, k2=k2: e.memset(SS[k2][:], 0.0), writes=[SSB[k2]])
            fns = []
            reads = [LSB[r], c.b_const]
            for hh in range(2):
                o = PZ2[:, hh * 512 + c0:(hh + 1) * 512]
                fns.append(lambda e, o=o, hh=hh: e.matmul(o, c.negtri_bf[:, 0, :], LS[r][:, hh, c0:512], start=False, stop=True, skip_group_check=True))
                if j > 0:
                    fns.append(lambda e, o=o, hh=hh: e.matmul(o, c.negtri_bf[:, 1, :], SS[j % 2][:, hh, c0:512], start=False, stop=True,
                                                              skip_group_check=True))
            if j > 0:
                reads.append(SSB[j % 2])
            tr.mm_group(fns, reads=reads, writes=PZBs)
            if not u["last"]:
                tr.op("DVE", lambda e: e.tensor_tensor(out=SS[(j + 1) % 2][:, :, c0:512], in0=SS[j % 2][:, :, c0:512], in1=LS[r][:, :, c0:512], op=ALU.add),
                      reads=[SSB[j % 2], LSB[r]], writes=[SSB[(j + 1) % 2]])
            PZv = PZ2[:].rearrange("p (h t) -> p h t", h=2)[:, :, c0:512]
            tr.op("ACT", lambda e: e.activation(out=WT[r][:, :, c0:512], in_=PZv, func=AF.Exp), reads=PZBs, writes=[WTB[r]])

        def stage_F(i):
            u = units[i]
            b, h, g, kb, c0 = u["b"], u["h"], u["g"], u["kb"], u["c0"]
            r = i % NR
            gi = u["igroup"] % 2
            PO, POB = (c.ps[6 + gi], c.psb[6 + gi]) if sb_mode else (c.ps[4 + gi], c.psb[4 + gi])
            Vt, VB = Vts[b % 2], VBs[b % 2]
            if sb_mode:
                fns = [(lambda e, hh=hh: e.matmul(PO[hh * 64:(hh + 1) * 64, c0:512], Vt[:, kb, (2 * h + hh) * 64:(2 * h + hh + 1) * 64],
                                                  WT[r][:, hh, c0:512], start=u["first"], stop=True, skip_group_check=True)) for hh in range(2)]
            else:
                fns = [lambda e: e.matmul(PO[:, c0:512], Vt[:, kb, h, :], WT[r][:, c0:512], start=u["first"], stop=True, skip_group_check=True)]
            tr.mm_group(fns, reads=[VB, WTB[r], c.b_const], writes=[POB])
            if u["last"]:
                Y, YB = YS[gi], YSB[gi]
                if sb_mode:
                    tr.op("DVE", lambda e: e.tensor_copy(out=Y[:], in_=PO[:, :]), reads=[POB], writes=[YB])
                    tr.dma("SP", dstT[b, h * 128:(h + 1) * 128, g * 512:(g + 1) * 512], Y[:], reads=[YB])
                    return
                else:
                    tr.op("DVE", lambda e: e.reciprocal(out=RC[gi][64:128, :], in_=PO[64:128, :]), reads=[POB], writes=[RCB[gi]])
                    tr.dma("SP", RC0[gi][:], RC[gi][64:128, :], reads=[RCB[gi]], writes=[RC0B[gi]])
                    tr.op("DVE", lambda e: e.tensor_tensor(out=Y[:], in0=PO[0:64, :], in1=RC0[gi][:], op=ALU.mult),
                          reads=[POB, RC0B[gi]], writes=[YB])
                tr.dma("SP", dstT[b, h * 64:(h + 1) * 64, g * 512:(g + 1) * 512], Y[:], reads=[YB])

        if sb_mode:
            for s_ in range(-2, n + 1):
                if 0 <= s_ + 2 < n:
                    stage_A(s_ + 2)
                if 0 <= s_ + 1 < n:
                    stage_B(s_ + 1)
                if 0 <= s_ < n:
                    stage_C(s_)
                if 0 <= s_ - 1 < n:
                    stage_F(s_ - 1)
        else:
            for s_ in range(-2, n):
                if 0 <= s_ + 2 < n:
                    stage_A(s_ + 2)
                    stage_B(s_ + 2)
                if 0 <= s_ < n:
                    stage_F(s_)


def phase_merge(c, l):
    nc, tr = c.nc, c.tr
    NB, S, NG = c.NB, c.S, c.NG
    with ExitStack() as st:
        def sb(name, shape, dt):
            return st.enter_context(nc.sbuf_tensor(uq("p5_" + name), list(shape), dt))
        WG = sb("wg", [128, KD, 3 * D], BF16)
        WBR = sb("wbr", [128, 12, D], BF16)
        WO = sb("wo", [128, KD, D], BF16)
        WB = Buf("p5_w")
        load_w_cast(c, WG, WB, c.w_in[l], C_G, 3 * D, KD)
        load_w_cast(c, WBR, WB, c.w_branch[l].rearrange("i k n -> (i k) n"), 0, D, 12)
        load_w_cast(c, WO, WB, c.w_out[l], 0, D, KD)
        xn = [sb(f"xn{i}", [128, KD, 512], BF16) for i in range(2)]
        ys = [sb(f"y{i}", [128, 12, 512], BF16) for i in range(2)]
        inb = [Buf(f"p5_in{i}") for i in range(2)]
        H = sb("h", [128, KD, 512], F32)
        HB = Buf("p5_h")
        MT = sb("mt", [128, KD, 512], BF16)
        MTB = Buf("p5_mt")
        G = [sb(f"g{i}", [128, 512], F32) for i in range(2)]
        GB = [Buf(f"p5_g{i}") for i in range(2)]
        T = [sb(f"t{i}", [128, 512], F32) for i in range(3)]
        TB = [Buf(f"p5_t{i}") for i in range(3)]
        M = [sb(f"m{i}", [128, 512], F32) for i in range(2)]
        MB = [Buf(f"p5_m{i}") for i in range(2)]
        ysrc = (c.YaT, c.YbT, c.YcT)

        def load_inputs(b, g, gi):
            tsl = slice(g * 512, (g + 1) * 512)
            tr.dma("SP", xn[gi % 2][:], c.xnT[b].rearrange("(k p) t -> p k t", p=128)[:, :, tsl], writes=[inb[gi % 2]])
            for i in range(3):
                tr.dma("SP", ys[gi % 2][:, i * 4:(i + 1) * 4, :], ysrc[i][b].rearrange("(k p) t -> p k t", p=128)[:, :, tsl], writes=[inb[gi % 2]])

        groups = [(b, g) for b in range(NB) for g in range(NG)]
        load_inputs(*groups[0], 0)
        ia = 0
        it = 0
        for gi, (b, g) in enumerate(groups):
            tsl = slice(g * 512, (g + 1) * 512)
            if gi + 1 < len(groups):
                load_inputs(*groups[gi + 1], gi + 1)
            XN, YS, INB = xn[gi % 2], ys[gi % 2], inb[gi % 2]
            hT_v = c.hT[b].rearrange("(k p) t -> p k t", p=128)
            tr.dma("SP", H[:], hT_v[:, :, tsl], writes=[HB])
            for cc in range(KD):
                csl = slice(cc * 128, (cc + 1) * 128)
                for i in range(3):
                    PA, PAB = c.ps[ia % 2], c.psb[ia % 2]
                    PBk, PBB = c.ps[2 + ia % 2], c.psb[2 + ia % 2]
                    Gt, GtB = G[ia % 2], GB[ia % 2]
                    tr.mm_group([(lambda e, k=k, PA=PA, i=i, csl=csl, XN=XN: e.matmul(
                        PA[:], WG[:, k, i * D + csl.start:i * D + csl.stop], XN[:, k, :], start=(k == 0), stop=(k == KD - 1))) for k in range(KD)],
                        reads=[WB, INB], writes=[PAB])
                    tr.mm_group([(lambda e, k=k, PBk=PBk, i=i, csl=csl, YS=YS: e.matmul(
                        PBk[:], WBR[:, i * 4 + k, csl], YS[:, i * 4 + k, :], start=(k == 0), stop=(k == 3))) for k in range(4)],
                        reads=[WB, INB], writes=[PBB])
                    tr.op("ACT", lambda e, PA=PA, Gt=Gt: e.activation(out=Gt[:], in_=PA[:], func=AF.Sigmoid), reads=[PAB], writes=[GtB])
                    Tt, TtB = T[it % 3], TB[it % 3]
                    it += 1
                    tr.op("DVE", lambda e, PBk=PBk, Gt=Gt, Tt=Tt: e.tensor_tensor(out=Tt[:], in0=PBk[:], in1=Gt[:], op=ALU.mult),
                          reads=[PBB, GtB], writes=[TtB])
                    if i == 0:
                        T0, T0B = Tt, TtB
                    elif i == 1:
                        Mt, MtB = M[cc % 2], MB[cc % 2]
                        tr.op("POOL", lambda e, Mt=Mt, T0=T0, Tt=Tt: e.tensor_tensor(out=Mt[:], in0=T0[:], in1=Tt[:], op=ALU.add),
                              reads=[T0B, TtB], writes=[MtB])
                    else:
                        tr.op("POOL", lambda e, Mt=Mt, Tt=Tt, cc=cc: e.tensor_tensor(out=MT[:, cc, :], in0=Mt[:], in1=Tt[:], op=ALU.add),
                              reads=[MtB, TtB], writes=[MTB])
                    ia += 1
            for cc in range(KD):
                csl = slice(cc * 128, (cc + 1) * 128)
                PO, POB = c.ps[4 + cc % 2], c.psb[4 + cc % 2]
                tr.mm_group([(lambda e, k=k, PO=PO, csl=csl: e.matmul(PO[:], WO[:, k, csl], MT[:, k, :], start=(k == 0), stop=(k == KD - 1)))
                             for k in range(KD)], reads=[WB, MTB], writes=[POB])
                tr.op("DVE", lambda e, PO=PO, cc=cc: e.tensor_tensor(out=H[:, cc, :], in0=PO[:], in1=H[:, cc, :], op=ALU.add),
                      reads=[POB, HB], writes=[HB])
            tr.dma("SP", hT_v[:, :, tsl], H[:], reads=[HB])


def phase_ffn_up(c, l):
    nc, tr = c.nc, c.tr
    NB, S, NG = c.NB, c.S, c.NG
    with ExitStack() as st:
        def sb(name, shape, dt):
            return st.enter_context(nc.sbuf_tensor(uq("p6_" + name), list(shape), dt))
        WU = sb("wu", [128, KD, 2 * D_FF], BF16)
        WB = Buf("p6_w")
        load_w_cast(c, WU, WB, c.w_up[l], 0, 2 * D_FF, KD)
        gcol = c.smallv[:, c.o_gffn + l * KD:c.o_gffn + (l + 1) * KD]
        ht = [sb(f"h{i}", [128, KD, 512], F32) for i in range(2)]
        hb = [Buf(f"p6_h{i}") for i in range(2)]
        XN = sb("xn", [128, KD, 512], BF16)
        XNB = Buf("p6_xn")
        sq = sb("sq", [128, KD, 512], BF16)
        sqb = Buf("p6_sq")
        rstd = sb("rstd", [128, 512], F32)
        rstdb = Buf("p6_rstd")
        NR = 3
        GT = [sb(f"gt{i}", [128, 514], F32) for i in range(NR)]
        GTB = [Buf(f"p6_gt{i}") for i in range(NR)]
        A0 = [sb(f"a0_{i}", [128, 512], F32) for i in range(NR)]
        A0B = [Buf(f"p6_a0_{i}") for i in range(NR)]
        A1 = [sb(f"a1_{i}", [128, 512], F32) for i in range(NR)]
        A1B = [Buf(f"p6_a1_{i}") for i in range(NR)]
        HM = [sb(f"hm{i}", [128, 512], BF16) for i in range(NR)]
        HMB = [Buf(f"p6_hm{i}") for i in range(NR)]
        HALO = sb("halo", [128, NFF, 2], F32)
        HALOB = [Buf(f"p6_halo{j}") for j in range(NFF)]
        groups = [(b, g) for b in range(NB) for g in range(NG)]
        tr.dma("SP", ht[0][:], c.hT[groups[0][0]].rearrange("(k p) t -> p k t", p=128)[:, :, 0:512], writes=[hb[0]])
        it = 0
        for gi, (b, g) in enumerate(groups):
            tsl = slice(g * 512, (g + 1) * 512)
            if gi + 1 < len(groups):
                b2, g2 = groups[gi + 1]
                tr.dma("SP", ht[(gi + 1) % 2][:], c.hT[b2].rearrange("(k p) t -> p k t", p=128)[:, :, g2 * 512:(g2 + 1) * 512],
                       writes=[hb[(gi + 1) % 2]])
            H, HB = ht[gi % 2], hb[gi % 2]
            emit_rmsnorm(c, H, HB, XN, XNB, gcol, sq, sqb, rstd, rstdb, 0)
            if g == 0:
                tr.op("POOL", lambda e: e.memset(HALO[:], 0.0), writes=HALOB)
            for j in range(NFF):
                r = it % NR
                it += 1
                PG, PGB = c.ps[1 + (j % 2)], c.psb[1 + (j % 2)]
                PV, PVB = c.ps[3 + (j % 2)], c.psb[3 + (j % 2)]
                tr.mm_group([(lambda e, k=k, PG=PG, j=j: e.matmul(PG[:], WU[:, k, j * 128:(j + 1) * 128], XN[:, k, :],
                                                                  start=(k == 0), stop=(k == KD - 1))) for k in range(KD)],
                            reads=[WB, XNB], writes=[PGB])
                tr.mm_group([(lambda e, k=k, PV=PV, j=j: e.matmul(PV[:], WU[:, k, D_FF + j * 128:D_FF + (j + 1) * 128], XN[:, k, :],
                                                                  start=(k == 0), stop=(k == KD - 1))) for k in range(KD)],
                            reads=[WB, XNB], writes=[PVB])
                cw = lambda i_, j=j: c.smallv[:, c.o_cw + l * 3 * NFF + i_ * NFF + j:c.o_cw + l * 3 * NFF + i_ * NFF + j + 1]
                cb = c.smallv[:, c.o_cb + l * NFF + j:c.o_cb + l * NFF + j + 1]
                tr.op("ACT", lambda e, r=r, PG=PG: e.activation(out=GT[r][:, 2:514], in_=PG[:], func=AF.Copy), reads=[PGB], writes=[GTB[r]])
                tr.op("POOL", lambda e, r=r, j=j: e.tensor_copy(out=GT[r][:, 0:2], in_=HALO[:, j, :]), reads=[HALOB[j], GTB[r]], writes=[GTB[r]])
                tr.op("POOL", lambda e, r=r, j=j: e.tensor_copy(out=HALO[:, j, :], in_=GT[r][:, 512:514]), reads=[GTB[r]], writes=[HALOB[j]])
                tr.op("ACT", lambda e, r=r, PG=PG, cw=cw, cb=cb: e.activation(out=A0[r][:], in_=PG[:], func=AF.Identity, scale=cw(2), bias=cb),
                      reads=[PGB, c.b_const], writes=[A0B[r]])
                tr.op("DVE", lambda e, r=r, cw=cw: e.scalar_tensor_tensor(out=A1[r][:], in0=GT[r][:, 1:513], scalar=cw(1), in1=A0[r][:],
                                                                          op0=ALU.mult, op1=ALU.add),
                      reads=[GTB[r], A0B[r], c.b_const], writes=[A1B[r]])
                tr.op("DVE", lambda e, r=r, cw=cw: e.scalar_tensor_tensor(out=A0[r][:], in0=GT[r][:, 0:512], scalar=cw(0), in1=A1[r][:],
                                                                          op0=ALU.mult, op1=ALU.add),
                      reads=[GTB[r], A1B[r], c.b_const], writes=[A0B[r]])
                tr.op("ACT", lambda e, r=r: e.activation(out=A1[r][:], in_=A0[r][:], func=AF.Gelu), reads=[A0B[r]], writes=[A1B[r]])
                tr.op("DVE", lambda e, r=r, PV=PV: e.tensor_tensor(out=HM[r][:], in0=PV[:], in1=A1[r][:], op=ALU.mult),
                      reads=[PVB, A1B[r]], writes=[HMB[r]])
                tr.dma("SP", c.hmid[b, j * 128:(j + 1) * 128, tsl], HM[r][:], reads=[HMB[r]])


def phase_ffn_down(c, l):
    nc, tr = c.nc, c.tr
    NB, S, NG = c.NB, c.S, c.NG
    with ExitStack() as st:
        def sb(name, shape, dt):
            return st.enter_context(nc.sbuf_tensor(uq("p6b_" + name), list(shape), dt))
        WD = sb("wd", [128, NFF, D], BF16)
        WB = Buf("p7_w")
        load_w_cast(c, WD, WB, c.w_down[l], 0, D, NFF)
        hm = [sb(f"hm{i}", [128, NFF, 512], BF16) for i in range(2)]
        hmb = [Buf(f"p7_hm{i}") for i in range(2)]
        ht = [sb(f"h{i}", [128, KD, 512], F32) for i in range(2)]
        hb = [Buf(f"p7_h{i}") for i in range(2)]
        groups = [(b, g) for b in range(NB) for g in range(NG)]

        def load(gi):
            b, g = groups[gi]
            tsl = slice(g * 512, (g + 1) * 512)
            tr.dma("SP", hm[gi % 2][:], c.hmid[b].rearrange("(k p) t -> p k t", p=128)[:, :, tsl], writes=[hmb[gi % 2]])
            tr.dma("SP", ht[gi % 2][:], c.hT[b].rearrange("(k p) t -> p k t", p=128)[:, :, tsl], writes=[hb[gi % 2]])
        load(0)
        for gi, (b, g) in enumerate(groups):
            tsl = slice(g * 512, (g + 1) * 512)
            if gi + 1 < len(groups):
                load(gi + 1)
            HMt, HMtB, H, HB = hm[gi % 2], hmb[gi % 2], ht[gi % 2], hb[gi % 2]
            for cc in range(KD):
                csl = slice(cc * 128, (cc + 1) * 128)
                PO, POB = c.ps[cc % 4], c.psb[cc % 4]
                tr.mm_group([(lambda e, k=k, PO=PO, csl=csl, HMt=HMt: e.matmul(PO[:], WD[:, k, csl], HMt[:, k, :], start=(k == 0), stop=(k == NFF - 1)))
                             for k in range(NFF)], reads=[WB, HMtB], writes=[POB])
                tr.op("DVE", lambda e, PO=PO, cc=cc, H=H: e.tensor_tensor(out=H[:, cc, :], in0=PO[:], in1=H[:, cc, :], op=ALU.add),
                      reads=[POB, HB], writes=[HB])
            tr.dma("SP", c.hT[b].rearrange("(k p) t -> p k t", p=128)[:, :, tsl], H[:], reads=[HB])


def phase_final(c):
    nc, tr = c.nc, c.tr
    with ExitStack() as st:
        def sb(name, shape, dt):
            return st.enter_context(nc.sbuf_tensor(name, list(shape), dt))
        gcol = sb("p7_g", [128, KD], F32)
        gb = Buf("p7_g")
        tr.dma("SP", gcol[:], c.final_g.rearrange("(k p) -> p k", p=128), writes=[c.b_const], allow_slow_non_contiguous=True)
        ht = [sb(f"p7_h{i}", [128, KD, 512], F32) for i in range(2)]
        hb = [Buf(f"p7_h{i}") for i in range(2)]
        xn = [sb(f"p7_xn{i}", [128, KD, 512], F32) for i in range(2)]
        xnb = [Buf(f"p7_xn{i}") for i in range(2)]
        sq = sb("p7_sq", [128, KD, 512], BF16)
        sqb = Buf("p7_sq")
        rstd = sb("p7_rstd", [128, 512], F32)
        rstdb = Buf("p7_rstd")
        ot = [sb(f"p7_o{i}", [128, D], F32) for i in range(2)]
        ob = [Buf(f"p7_o{i}") for i in range(2)]
        it = 0
        gi = 0
        for b in range(c.NB):
            hT_v = c.hT[b].rearrange("(k p) t -> p k t", p=128)
            for g in range(c.NG):
                H, HB = ht[gi % 2], hb[gi % 2]
                XN, XNB = xn[gi % 2], xnb[gi % 2]
                tr.dma("SP", H[:], hT_v[:, :, g * 512:(g + 1) * 512], writes=[HB])
                emit_rmsnorm(c, H, HB, XN, XNB, gcol, sq, sqb, rstd, rstdb, 4)
                for tb in range(4):
                    O, OB = ot[it % 2], ob[it % 2]
                    for half in range(2):
                        pi = (it * 2 + half) % 4
                        P, PB = c.ps[pi], c.psb[pi]
                        tr.mm_group(
                            [(lambda e, j=j, P=P, XN=XN, half=half, tb=tb: e.transpose(
                                P[:, j * 128:(j + 1) * 128], XN[:, half * 4 + j, tb * 128:(tb + 1) * 128], c.ident[:]))
                             for j in range(4)],
                            reads=[XNB, c.b_const], writes=[PB])
                        if half == 0:
                            tr.op("ACT", lambda e, P=P, O=O, half=half: e.activation(
                                out=O[:, half * 512:(half + 1) * 512], in_=P[:], func=AF.Copy), reads=[PB], writes=[OB])
                        else:
                            tr.op("DVE", lambda e, P=P, O=O, half=half: e.tensor_copy(
                                out=O[:, half * 512:(half + 1) * 512], in_=P[:]), reads=[PB], writes=[OB])
                    t0 = g * 512 + tb * 128
                    tr.dma("SP", c.out[b, t0:t0 + 128, :], O[:], reads=[OB])
                    it += 1
                gi += 1


_CACHE = {}


def host_consts():
    ident = np.eye(128, dtype=np.float32)
    s = np.arange(128)[:, None]
    t = np.arange(128)[None, :]
    masks = np.zeros((128, 4, 128), np.float32)
    masks[:, 0, :] = (s < t)
    masks[:, 1, :] = (s <= t)
    masks[:, 2, :] = (s >= t)
    masks[:, 3, :] = 1.0
    return ident, masks


def kernel(**inputs):
    NCORES = 8
    x = np.ascontiguousarray(inputs["x"], dtype=np.float32)
    B, S, _ = x.shape
    NB = B // NCORES
    depth = inputs["w_in"].shape[0]
    key = (NB, S, depth)
    if key not in _CACHE:
        _CACHE[key] = build(NB, S, depth)
    nc = _CACHE[key]
    ident, masks = host_consts()
    shared = {k: np.ascontiguousarray(v, dtype=np.float32) for k, v in inputs.items() if k != "x"}
    shared["k_ident"] = ident
    shared["k_masks"] = masks
    in_maps = []
    for i in range(NCORES):
        m = dict(shared)
        m["x"] = x[i * NB:(i + 1) * NB]
        in_maps.append(m)
    res = run_bass_kernel_spmd(nc, in_maps, core_ids=list(range(NCORES)))
    return np.concatenate([r["out"] for r in res.results], axis=0)
```

```python
import numpy as np
import ml_dtypes
from contextlib import ExitStack
import concourse.bass as bass
import concourse.mybir as mybir
from concourse.bass_utils import run_bass_kernel_spmd

F32 = mybir.dt.float32
BF16 = mybir.dt.bfloat16
AF = mybir.ActivationFunctionType
ALU = mybir.AluOpType

D = 1024
KD = D // 128
HD = 64
NH = 8
IN_COLS = 7176
D_FF = 2816
NFF = D_FF // 128
EPS = 1e-6
AUG = 70
C_QA, C_KA, C_VA, C_QB, C_KB, C_VB, C_F, C_U, C_V, C_G = 0, 512, 1024, 1536, 2048, 2560, 3072, 3080, 3592, 4104


class Buf:
    __slots__ = ("name", "w", "r")

    def __init__(self, name):
        self.name = name
        self.w = {}
        self.r = {}


class EngState:
    def __init__(self, name, eng, sem):
        self.name = name
        self.eng = eng
        self.sem = sem
        self.key = "E_" + name
        self.count = 0
        self.waited = {}


class Tr:
    def __init__(self, nc, ndma=8):
        self.nc = nc
        self.E = {}
        for name, attr in (("PE", "tensor"), ("ACT", "scalar"), ("DVE", "vector"), ("POOL", "gpsimd"), ("SP", "sync")):
            self.E[name] = EngState(name, getattr(nc, attr), nc.alloc_semaphore("s_" + name))
        self.Q = {}
        for q in ("SP", "POOL"):
            sems = [nc.alloc_semaphore(f"d_{q}_{i}") for i in range(ndma)]
            self.Q[q] = dict(sems=sems, cnts=[0] * ndma, nxt=0)
        self.n_inst = 0

    def _wait(self, E, key, sem, val):
        if E.waited.get(key, 0) >= val:
            return
        E.eng.wait_ge(sem, val)
        E.waited[key] = val

    def _deps(self, E, reads, writes, skip_same_raw=False):
        for b in reads:
            for key, (sem, val) in b.w.items():
                if key == E.key and skip_same_raw:
                    continue
                self._wait(E, key, sem, val)
        for b in writes:
            for key, (sem, val) in b.w.items():
                if key == E.key and skip_same_raw:
                    continue
                self._wait(E, key, sem, val)
            for key, (sem, val) in b.r.items():
                if key == E.key:
                    continue
                self._wait(E, key, sem, val)

    def _stamp(self, key, sem, val, reads, writes):
        for b in reads:
            b.r[key] = (sem, val)
        for b in writes:
            b.w = {key: (sem, val)}
            b.r = {}

    def op(self, eng, fn, reads=(), writes=()):
        E = self.E[eng]
        self._deps(E, reads, writes, skip_same_raw=(eng == "PE"))
        ins = fn(E.eng)
        E.count += 1
        ins.then_inc(E.sem, 1)
        self._stamp(E.key, E.sem, E.count, reads, writes)
        self.n_inst += 1
        return ins

    def mm_group(self, fns, reads=(), writes=()):
        E = self.E["PE"]
        self._deps(E, reads, writes, skip_same_raw=True)
        ins = None
        for fn in fns:
            ins = fn(E.eng)
            self.n_inst += 1
        E.count += 1
        ins.then_inc(E.sem, 1)
        self._stamp(E.key, E.sem, E.count, reads, writes)

    def dma(self, q, out, in_, reads=(), writes=(), part=False, **kw):
        Q = self.Q[q]
        E = self.E[q]
        slot = Q["nxt"]
        Q["nxt"] = (slot + 1) % len(Q["sems"])
        sem = Q["sems"][slot]
        key = f"D_{q}_{slot}"
        if Q["cnts"][slot] > 0:
            self._wait(E, key, sem, Q["cnts"][slot])
        if part:
            self._deps(E, reads, ())
            for b in writes:
                for k2, (s2, v2) in b.r.items():
                    self._wait(E, k2, s2, v2)
        else:
            self._deps(E, reads, writes)
        ins = E.eng.dma_start(out=out, in_=in_, **kw)
        Q["cnts"][slot] += 16
        ins.then_inc(sem, 16)
        if part:
            for b in reads:
                b.r[key] = (sem, Q["cnts"][slot])
            for b in writes:
                b.w[key] = (sem, Q["cnts"][slot])
                b.r = {}
        else:
            self._stamp(key, sem, Q["cnts"][slot], reads, writes)
        self.n_inst += 1

    def barrier(self):
        items = []
        for e in self.E.values():
            if e.count > 0:
                items.append((e.key, e.sem, e.count))
        for q, Q in self.Q.items():
            for i, s in enumerate(Q["sems"]):
                if Q["cnts"][i] > 0:
                    items.append((f"D_{q}_{i}", s, Q["cnts"][i]))
        for e in self.E.values():
            for key, sem, val in items:
                if key == e.key:
                    continue
                self._wait(e, key, sem, val)

    def finish(self, eng="SP"):
        e = self.E[eng]
        for q, Q in self.Q.items():
            for i, s in enumerate(Q["sems"]):
                if Q["cnts"][i] > 0:
                    self._wait(e, f"D_{q}_{i}", s, Q["cnts"][i])
        for e2 in self.E.values():
            if e2.count > 0 and e2 is not e:
                self._wait(e, e2.key, e2.sem, e2.count)


class Ctx:
    pass


_UID = [0]


def uq(name):
    _UID[0] += 1
    return f"{name}_{_UID[0]}"


def build(NB, S, depth, dbg=None):
    assert S % 512 == 0
    NG = S // 512
    NBLK = S // 128
    nc = bass.Bass("TRN2", target_bir_lowering=False)
    tr = Tr(nc)
    c = Ctx()
    c.nc, c.tr, c.NB, c.S, c.NG, c.NBLK, c.depth = nc, tr, NB, S, NG, NBLK, depth

    def din(name, shape, dt=F32):
        return nc.dram_tensor(name, list(shape), dt, kind="ExternalInput").ap()

    def dscr(name, shape, dt):
        kind = "ExternalOutput" if (dbg and name in dbg) else "Internal"
        return nc.dram_tensor(name, list(shape), dt, kind=kind).ap()

    L = max(depth, 1)
    c.x = din("x", [NB, S, D])
    c.norm_mix_g = din("norm_mix_g", [L, D])
    c.w_in = din("w_in", [L, D, IN_COLS])
    c.fox_bias = din("fox_bias", [L, NH])
    c.sgu_ln_g = din("sgu_ln_g", [L, 512])
    c.sgu_w = din("sgu_w", [L, 8, 128, 128])
    c.sgu_b = din("sgu_b", [L, 8, 128])
    c.w_branch = din("w_branch", [L, 3, 512, D])
    c.w_out = din("w_out", [L, D, D])
    c.norm_ffn_g = din("norm_ffn_g", [L, D])
    c.w_up = din("w_up", [L, D, 2 * D_FF])
    c.conv_w = din("conv_w", [L, 3, D_FF])
    c.conv_b = din("conv_b", [L, D_FF])
    c.w_down = din("w_down", [L, D_FF, D])
    c.final_g = din("final_g", [D])
    c.k_ident = din("k_ident", [128, 128])
    c.k_masks = din("k_masks", [128, 4, 128])
    c.out = nc.dram_tensor("out", [NB, S, D], F32, kind="ExternalOutput").ap()

    c.hT = dscr("hT", [NB, D, S], F32)
    c.xnT = dscr("xnT", [NB, D, S], BF16)
    c.QaT = dscr("QaT", [NB, 512, S], BF16)
    c.KaT = dscr("KaT", [NB, 512, S], BF16)
    c.Va = dscr("Va", [NB, S, 512], BF16)
    c.QbT = dscr("QbT", [NB, NH, AUG, S], BF16)
    c.KbT = dscr("KbT", [NB, NH, AUG, S], BF16)
    c.Vb = dscr("Vb", [NB, S, 512], BF16)
    c.YaT = dscr("YaT", [NB, 512, S], BF16)
    c.YbT = dscr("YbT", [NB, 512, S], BF16)
    c.YcT = dscr("YcT", [NB, 512, S], BF16)
    c.hmid = dscr("hmid", [NB, D_FF, S], BF16)

    with ExitStack() as gs:
        def sb(name, shape, dt):
            return gs.enter_context(nc.sbuf_tensor(name, list(shape), dt))

        c.ident = sb("ident", [128, 128], F32)
        c.ones_bf = sb("ones_bf", [128, 128], BF16)
        c.b_const = Buf("const")
        tr.dma("SP", c.ident[:], c.k_ident[:, :], writes=[c.b_const])
        tr.op("DVE", lambda e: e.memset(c.ones_bf[:], 1.0), writes=[c.b_const])
        c.eps_col = sb("eps_col", [128, 1], F32)
        c.one_col = sb("one_col", [128, 1], F32)
        tr.op("DVE", lambda e: e.memset(c.one_col[:], 1.0), writes=[c.b_const])
        tr.op("DVE", lambda e: e.memset(c.eps_col[:], EPS), writes=[c.b_const])
        c.ps2 = [gs.enter_context(nc.psum_tensor(f"ps{i}", [128, 1024], F32)) for i in range(4)]
        c.ps = [c.ps2[i // 2][:, (i % 2) * 512:(i % 2 + 1) * 512] for i in range(8)]
        c.psb = [Buf(f"ps{i}") for i in range(8)]

        c.ones_big = sb("ones_big", [8, 3 * 512], BF16)
        tr.op("DVE", lambda e: e.memset(c.ones_big[:], 1.0), writes=[c.b_const])
        c.masks = sb("masks", [128, 4, 128], F32)
        tr.dma("SP", c.masks[:], c.k_masks[:, :, :], writes=[c.b_const])
        c.masks_bf = sb("masks_bf", [128, 4, 128], BF16)
        tr.op("DVE", lambda e: e.tensor_copy(out=c.masks_bf[:], in_=c.masks[:]), reads=[c.b_const], writes=[c.b_const])
        c.negtri_bf = sb("negtri_bf", [128, 2, 128], BF16)
        tr.op("DVE", lambda e: e.tensor_scalar(out=c.negtri_bf[:], in0=c.masks[:, 2:4, :], scalar1=-1.0, scalar2=None, op0=ALU.mult),
              reads=[c.b_const], writes=[c.b_const])
        c.ident_bf = sb("ident_bf", [128, 128], BF16)
        tr.op("DVE", lambda e: e.tensor_copy(out=c.ident_bf[:], in_=c.ident[:]), reads=[c.b_const], writes=[c.b_const])
        c.nb_sb = sb("nb_sb", [128, 128], BF16)
        c.nb_fox = sb("nb_fox", [128, 128], BF16)
        tr.op("DVE", lambda e: e.tensor_scalar(out=c.nb_sb[:], in0=c.masks[:, 0, :], scalar1=-1.0, scalar2=30000.0, op0=ALU.add, op1=ALU.mult),
              reads=[c.b_const], writes=[c.b_const])
        tr.op("DVE", lambda e: e.tensor_scalar(out=c.nb_fox[:], in0=c.masks[:, 1, :], scalar1=-1.0, scalar2=30000.0, op0=ALU.add, op1=ALU.mult),
              reads=[c.b_const], writes=[c.b_const])
        init_small_vectors(c, gs)
        phase_in_transpose(c)
        tr.barrier()
        stop_after = (dbg or {}).get("stop_after") if isinstance(dbg, dict) else None
        for l in range(depth):
            phase_inproj(c, l)
            tr.barrier()
            if stop_after == "inproj":
                break
            phase_attn(c, l, "sb")
            tr.barrier()
            phase_attn(c, l, "fox")
            tr.barrier()
            if stop_after == "attn":
                break
            phase_merge(c, l)
            tr.barrier()
            if stop_after == "merge":
                break
            phase_ffn_up(c, l)
            tr.barrier()
            phase_ffn_down(c, l)
            tr.barrier()
        phase_final(c)
        tr.finish("SP")
    return nc


def phase_in_transpose(c):
    nc, tr = c.nc, c.tr
    with ExitStack() as st:
        xt = [st.enter_context(nc.sbuf_tensor(f"p0_x{i}", [128, D], F32)) for i in range(2)]
        xb = [Buf(f"p0_x{i}") for i in range(2)]
        ht = [st.enter_context(nc.sbuf_tensor(f"p0_h{i}", [128, KD, 512], F32)) for i in range(2)]
        hb = [Buf(f"p0_h{i}") for i in range(2)]
        it = 0
        for b in range(c.NB):
            hT_v = c.hT[b].rearrange("(k p) t -> p k t", p=128)
            for g in range(c.NG):
                H, HB = ht[g % 2], hb[g % 2]
                for tb in range(4):
                    X, XB = xt[it % 2], xb[it % 2]
                    t0 = g * 512 + tb * 128
                    tr.dma("SP", X[:], c.x[b, t0:t0 + 128, :], writes=[XB])
                    for half in range(2):
                        pi = (it * 2 + half) % 4
                        P, PB = c.ps[pi], c.psb[pi]
                        tr.mm_group(
                            [(lambda e, j=j, P=P, X=X, half=half: e.transpose(
                                P[:, j * 128:(j + 1) * 128], X[:, (half * 4 + j) * 128:(half * 4 + j + 1) * 128], c.ident[:]))
                             for j in range(4)],
                            reads=[XB, c.b_const], writes=[PB])
                        eng = "ACT" if half == 0 else "DVE"
                        if eng == "ACT":
                            tr.op("ACT", lambda e, P=P, H=H, half=half, tb=tb: e.activation(
                                out=H[:, half * 4:(half + 1) * 4, tb * 128:(tb + 1) * 128],
                                in_=P[:].rearrange("p (j t) -> p j t", j=4), func=AF.Copy),
                                reads=[PB], writes=[HB])
                        else:
                            tr.op("DVE", lambda e, P=P, H=H, half=half, tb=tb: e.tensor_copy(
                                out=H[:, half * 4:(half + 1) * 4, tb * 128:(tb + 1) * 128],
                                in_=P[:].rearrange("p (j t) -> p j t", j=4)),
                                reads=[PB], writes=[HB])
                    it += 1
                tr.dma("SP", hT_v[:, :, g * 512:(g + 1) * 512], H[:], reads=[HB])


def emit_rmsnorm(c, H, HB, XN, XNB, g_col, sq, sqb, rstd, rstdb, pbank, out_dt_note=None):
    nc, tr = c.nc, c.tr
    P, PB = c.ps[pbank], c.psb[pbank]
    tr.op("ACT", lambda e: e.activation(out=sq[:], in_=H[:], func=AF.Square), reads=[HB], writes=[sqb])
    tr.mm_group([(lambda e, k=k: e.matmul(P[:], c.ones_bf[:], sq[:, k, :], start=(k == 0), stop=(k == KD - 1)))
                 for k in range(KD)], reads=[sqb, c.b_const], writes=[PB])
    tr.op("ACT", lambda e: e.activation(out=rstd[:], in_=P[:], func=AF.Ln, scale=1.0 / D, bias=c.eps_col[:, 0:1]),
          reads=[PB, c.b_const], writes=[rstdb])
    tr.op("ACT", lambda e: e.activation(out=rstd[:], in_=rstd[:], func=AF.Exp, scale=-0.5),
          reads=[rstdb], writes=[rstdb])
    for k in range(KD):
        tr.op("DVE", lambda e, k=k: e.scalar_tensor_tensor(
            out=XN[:, k, :], in0=H[:, k, :], scalar=g_col[:, k:k + 1], in1=rstd[:], op0=ALU.mult, op1=ALU.mult),
            reads=[HB, rstdb, c.b_const], writes=[XNB])


def init_small_vectors(c, gs):
    nc, tr = c.nc, c.tr
    L = max(c.depth, 1)
    c.smallv = gs.enter_context(nc.sbuf_tensor("smallv", [128, 512], F32))
    c.o_gmix, c.o_gffn, c.o_fg, c.o_cb, c.o_cw = 0, L * KD, 2 * L * KD, 2 * L * KD + KD, 2 * L * KD + KD + L * NFF
    assert c.o_cw + L * 3 * NFF <= 512
    P, PB = c.ps[0], c.psb[0]
    with ExitStack() as st:
        rows = st.enter_context(nc.sbuf_tensor(uq("sv_rows"), [128, 8, 128], F32))
        rb = Buf("sv_rows")
        srcs = [(c.norm_mix_g.rearrange("l (k p) -> (l k) p", p=128), L * KD, c.o_gmix),
                (c.norm_ffn_g.rearrange("l (k p) -> (l k) p", p=128), L * KD, c.o_gffn),
                (c.final_g.rearrange("(k p) -> k p", p=128), KD, c.o_fg),
                (c.conv_b.rearrange("l (k p) -> (l k) p", p=128), L * NFF, c.o_cb)]
        for l in range(L):
            srcs.append((c.conv_w[l].rearrange("i (k p) -> (i k) p", p=128), 3 * NFF, c.o_cw + l * 3 * NFF))
        assert len(srcs) <= 8
        for i, (src, n, off) in enumerate(srcs):
            tr.dma("SP", rows[0:n, i, :], src, writes=[rb])
        fns = []
        for i, (src, n, off) in enumerate(srcs):
            fns.append(lambda e, i=i, n=n, off=off: e.transpose(P[:, off:off + n], rows[0:n, i, :], c.ident[0:n, 0:n]))
        tr.mm_group(fns, reads=[rb, c.b_const], writes=[PB])
        tr.op("DVE", lambda e: e.tensor_copy(out=c.smallv[:, 0:c.o_cw + L * 3 * NFF], in_=P[:, 0:c.o_cw + L * 3 * NFF]),
              reads=[PB], writes=[c.b_const])
    for b in range(c.NB):
        for g in range(c.NG):
            src = c.ones_big[:].rearrange("p (j t) -> p j t", j=3)
            tr.dma("SP", c.QbT[b, :, 67:70, g * 512:(g + 1) * 512], src, reads=[c.b_const])
            tr.dma("SP", c.KbT[b, :, 64:67, g * 512:(g + 1) * 512], src, reads=[c.b_const])


def load_w_cast(c, dst, dstb, src_rows, col0, ncols, kchunks, dcol0=0):
    tr = c.tr
    for k in range(kchunks):
        o = 0
        while o < ncols:
            n = min(2048, ncols - o)
            tr.dma("POOL", dst[:, k, dcol0 + o:dcol0 + o + n], src_rows[k * 128:(k + 1) * 128, col0 + o:col0 + o + n], writes=[dstb], part=True)
            o += n


def load_w_segments(c, dst, src_rows, segs, kchunks, name):
    tr = c.tr
    bufs = []
    for (o, n) in segs:
        B = Buf(f"{name}_{o}")
        for k in range(kchunks):
            tr.dma("POOL", dst[:, k, o:o + n], src_rows[k * 128:(k + 1) * 128, o:o + n], writes=[B], part=True)
        bufs.append((o, o + n, B))

    def find(a, b):
        return [B for (s_, e_, B) in bufs if s_ < b and e_ > a]
    return find


def phase_inproj(c, l):
    nc, tr = c.nc, c.tr
    NB, S, NG = c.NB, c.S, c.NG
    NW = C_G
    with ExitStack() as st:
        def sb(name, shape, dt):
            return st.enter_context(nc.sbuf_tensor(uq("p1_" + name), list(shape), dt))
        W = sb("w", [128, KD, NW], BF16)
        wfind = load_w_segments(c, W, c.w_in[l], [(o, min(512, NW - o)) for o in range(0, NW, 512)], KD, "p1_w")
        wst = sb("wst", [128, 8, 128], BF16)
        wstb = Buf("p1_wst")
        wraw = sb("wraw", [128, 8, 128], F32)
        wrawb = Buf("p1_wraw")
        tr.dma("SP", wraw[:], c.sgu_w[l].rearrange("g t s -> t g s"), writes=[wrawb])
        for half in range(2):
            P, PB = c.ps[6 + half], c.psb[6 + half]
            tr.mm_group([(lambda e, j=j, P=P, half=half: e.transpose(P[:, j * 128:(j + 1) * 128], wraw[:, half * 4 + j, :], c.ident[:]))
                         for j in range(4)], reads=[wrawb, c.b_const], writes=[PB])
            tr.op("DVE", lambda e, P=P, half=half: e.tensor_copy(out=wst[:, half * 4:(half + 1) * 4, :],
                                                                 in_=P[:].rearrange("p (j t) -> p j t", j=4)),
                  reads=[PB], writes=[wstb])
        tr.op("DVE", lambda e: e.memset(wst[64:128, :, 0:64], 0.0), reads=[wstb], writes=[wstb])
        sgub32 = sb("sgub32", [1, 1024], F32)
        sgub_hi = sb("sgub_hi", [1, 1024], BF16)
        sgub_lo = sb("sgub_lo", [1, 1024], BF16)
        sgubb = Buf("p1_sgub")
        tr.dma("SP", sgub32[:], c.sgu_b[l:l + 1].rearrange("o g t -> o (g t)"), writes=[sgubb])
        tr.op("DVE", lambda e: e.tensor_copy(out=sgub_hi[:], in_=sgub32[:]), reads=[sgubb], writes=[sgubb])
        tr.op("DVE", lambda e: e.tensor_tensor(out=sgub32[:], in0=sgub32[:], in1=sgub_hi[:], op=ALU.subtract), reads=[sgubb], writes=[sgubb])
        tr.op("DVE", lambda e: e.tensor_copy(out=sgub_lo[:], in_=sgub32[:]), reads=[sgubb], writes=[sgubb])
        lng = sb("lng", [128, 512], F32)
        foxb = sb("foxb", [8, 1], F32)
        ones8 = sb("ones8", [8, 512], F32)
        lb = Buf("p1_lng")
        tr.dma("SP", lng[:], c.sgu_ln_g[l].partition_broadcast(128), writes=[lb])
        tr.dma("SP", foxb[:], c.fox_bias[l].rearrange("(h o) -> h o", o=1), writes=[lb])
        tr.op("DVE", lambda e: e.memset(ones8[:], 1.0), reads=[lb], writes=[lb])
        gcol = c.smallv[:, c.o_gmix + l * KD:c.o_gmix + (l + 1) * KD]

        ht = [sb(f"h{i}", [128, KD, 512], F32) for i in range(2)]
        hb = [Buf(f"p1_h{i}") for i in range(2)]
        xn = [sb(f"xn{i}", [128, KD, 512], BF16) for i in range(2)]
        xnb = [Buf(f"p1_xn{i}") for i in range(2)]
        sq = sb("sq", [128, KD, 512], BF16)
        sqb = Buf("p1_sq")
        rstd = sb("rstd", [128, 512], F32)
        rstdb = Buf("p1_rstd")
        fam = {n_: sb("fam_" + n_, [128, 4, 512], BF16) for n_ in ("qa", "ka", "qb", "kb", "va", "vb")}
        famb = {n_: Buf("p1_fam_" + n_) for n_ in fam}
        ut = sb("ut", [128, 4, 512], F32)
        utb = Buf("p1_ut")
        v32 = sb("v32", [128, 4, 512], F32)
        v32b = [Buf(f"p1_v32_{i}") for i in range(4)]
        vn = sb("vn", [128, 2, 512], BF16)
        vnb = [Buf(f"p1_vn{i}") for i in range(2)]
        stt = sb("stt", [128, 4, 16], F32)
        sttb = Buf("p1_stt")
        yc = sb("yc", [128, 4, 512], BF16)
        ycb = Buf("p1_yc")
        fl = sb("fl", [8, 512], F32)
        flb = Buf("p1_fl")
        lc = sb("lc", [8, 512], F32)
        lcx = fl
        carry = sb("carry", [8, 1], F32)
        lcb = Buf("p1_lc")
        pcs = sb("pc", [8, 6, 512], BF16)
        pcsb = Buf("p1_pcs")

        groups = [(b, g) for b in range(NB) for g in range(NG)]

        def load_h(gi):
            b, g = groups[gi]
            tr.dma("SP", ht[gi % 2][:], c.hT[b].rearrange("(k p) t -> p k t", p=128)[:, :, g * 512:(g + 1) * 512], writes=[hb[gi % 2]])

        def norm(gi):
            emit_rmsnorm(c, ht[gi % 2], hb[gi % 2], xn[gi % 2], xnb[gi % 2], gcol, sq, sqb, rstd, rstdb, 0)

        load_h(0)
        norm(0)
        ifm = 0
        itm = 0
        ibank_fm = 0
        ibank_tm = 0
        for gi, (b, g) in enumerate(groups):
            XN, XNB = xn[gi % 2], xnb[gi % 2]
            tsl = slice(g * 512, (g + 1) * 512)
            if gi + 1 < len(groups):
                load_h(gi + 1)
            if g == 0:
                tr.op("DVE", lambda e: e.memset(carry[:], 0.0), reads=[lcb], writes=[lcb])
            tr.dma("SP", c.xnT[b].rearrange("(k p) t -> p k t", p=128)[:, :, tsl], XN[:], reads=[XNB])
            fm_jobs = []
            for j in range(4):
                fm_jobs.append((C_QA + j * 128, "qa", 0.125, j))
            for j in range(4):
                fm_jobs.append((C_KA + j * 128, "ka", 1.0, j))
            for j in range(4):
                fm_jobs.append((C_QB + j * 128, "qb", 0.125, j))
            for j in range(4):
                fm_jobs.append((C_KB + j * 128, "kb", 1.0, j))
            for j in range(4):
                fm_jobs.append((C_U + j * 128, "gelu", 1.0, j))
            for (col, kind, scale, j) in fm_jobs:
                pi = 1 + (ibank_fm % 2)
                ibank_fm += 1
                P, PB = c.ps[pi], c.psb[pi]
                tr.mm_group([(lambda e, k=k, P=P, col=col, XN=XN: e.matmul(
                    P[:], W[:, k, col:col + 128], XN[:, k, :], start=(k == 0), stop=(k == KD - 1))) for k in range(KD)],
                    reads=wfind(col, col + 128) + [XNB], writes=[PB])
                if kind == "gelu":
                    tr.op("ACT", lambda e, P=P, j=j: e.activation(out=ut[:, j, :], in_=P[:], func=AF.Gelu),
                          reads=[PB], writes=[utb])
                    continue
                F, FB = fam[kind], famb[kind]
                if ifm % 2 == 0:
                    tr.op("ACT", lambda e, P=P, F=F, scale=scale, j=j: e.activation(out=F[:, j, :], in_=P[:], func=AF.Copy, scale=scale),
                          reads=[PB], writes=[FB])
                else:
                    tr.op("DVE", lambda e, P=P, F=F, scale=scale, j=j: e.tensor_scalar(
                        out=F[:, j, :], in0=P[:], scalar1=scale, scalar2=None, op0=ALU.mult), reads=[PB], writes=[FB])
                ifm += 1
                if j == 3:
                    if kind in ("qa", "ka"):
                        T = c.QaT if kind == "qa" else c.KaT
                        tr.dma("SP", T[b].rearrange("(j p) t -> p j t", p=128)[:, :, tsl], F[:], reads=[FB])
                    else:
                        T = c.QbT if kind == "qb" else c.KbT
                        Tv = T[b].rearrange("(j hh) r t -> hh j r t", hh=2)
                        for hh in range(2):
                            tr.dma("SP", Tv[hh, :, 0:64, tsl].rearrange("j d t -> d j t"), F[hh * 64:(hh + 1) * 64, :, :], reads=[FB])
            PF, PFB = c.ps[5], c.psb[5]
            tr.mm_group([(lambda e, k=k, XN=XN: e.matmul(PF[0:8, :], W[:, k, C_F:C_F + 8], XN[:, k, :], start=(k == 0), stop=(k == KD - 1)))
                         for k in range(KD)], reads=wfind(C_F, C_F + 8) + [XNB], writes=[PFB])
            tr.op("DVE", lambda e: e.tensor_scalar(out=fl[:], in0=PF[0:8, :], scalar1=foxb[:, 0:1], scalar2=None, op0=ALU.add),
                  reads=[PFB, lb], writes=[flb])
            for tb in range(4):
                bsl = slice(tb * 128, (tb + 1) * 128)
                for (col, kind, dstT) in ((C_VA, "va", c.Va), (C_VB, "vb", c.Vb), (C_V, "gelu", None)):
                    pi = 3 + (ibank_tm % 2)
                    ibank_tm += 1
                    P, PB = c.ps[pi], c.psb[pi]
                    tr.mm_group([(lambda e, k=k, P=P, col=col, XN=XN, bsl=bsl: e.matmul(
                        P[:], XN[:, k, bsl], W[:, k, col:col + 512], start=(k == 0), stop=(k == KD - 1))) for k in range(KD)],
                        reads=wfind(col, col + 512) + [XNB], writes=[PB])
                    if kind == "gelu":
                        tr.op("ACT", lambda e, P=P, tb=tb: e.activation(out=v32[:, tb, :], in_=P[:], func=AF.Gelu),
                              reads=[PB], writes=[v32b[tb]])
                        tr.op("DVE", lambda e, tb=tb: e.bn_stats(out=stt[:, tb, 0:6], in_=v32[:, tb, :]), reads=[v32b[tb]], writes=[sttb])
                        tr.op("DVE", lambda e, tb=tb: e.bn_aggr(out=stt[:, tb, 6:8], in_=stt[:, tb, 0:6]), reads=[sttb], writes=[sttb])
                    else:
                        T_, TB_ = fam[kind], famb[kind]
                        if itm % 2 == 0:
                            tr.op("DVE", lambda e, P=P, T_=T_, tb=tb: e.tensor_copy(out=T_[:, tb, :], in_=P[:]), reads=[PB], writes=[TB_])
                        else:
                            tr.op("ACT", lambda e, P=P, T_=T_, tb=tb: e.activation(out=T_[:, tb, :], in_=P[:], func=AF.Copy), reads=[PB], writes=[TB_])
                        itm += 1
                        if tb == 3:
                            tr.dma("SP", dstT[b, g * 512:(g + 1) * 512, :].rearrange("(tb p) c -> p tb c", p=128), T_[:], reads=[TB_])
            if gi + 1 < len(groups):
                norm(gi + 1)
            tr.op("ACT", lambda e: e.activation(out=stt[:, :, 8:9], in_=stt[:, :, 7:8], func=AF.Ln, bias=c.eps_col[:, 0:1]),
                  reads=[sttb, c.b_const], writes=[sttb])
            tr.op("ACT", lambda e: e.activation(out=stt[:, :, 9:10], in_=stt[:, :, 8:9], func=AF.Exp, scale=-0.5),
                  reads=[sttb], writes=[sttb])
            tr.op("DVE", lambda e: e.scalar_tensor_tensor(out=stt[:, :, 10:11], in0=stt[:, :, 6:7], scalar=-1.0, in1=stt[:, :, 9:10],
                                                          op0=ALU.mult, op1=ALU.mult), reads=[sttb], writes=[sttb])
            tr.op("ACT", lambda e: e.activation(out=fl[:], in_=fl[:], func=AF.Exp, scale=-1.0), reads=[flb], writes=[flb])
            tr.op("ACT", lambda e: e.activation(out=fl[:], in_=fl[:], func=AF.Ln, bias=c.one_col[0:8, 0:1]), reads=[flb, c.b_const], writes=[flb])
            for tb in range(4):
                bsl = slice(tb * 128, (tb + 1) * 128)
                tr.op("ACT", lambda e, tb=tb: e.activation(out=v32[:, tb, :], in_=v32[:, tb, :], func=AF.Identity,
                                                          scale=stt[:, tb, 9:10], bias=stt[:, tb, 10:11]),
                      reads=[sttb, v32b[tb]], writes=[v32b[tb]])
                tr.op("DVE", lambda e, tb=tb: e.tensor_tensor(out=vn[:, tb % 2, :], in0=v32[:, tb, :], in1=lng[:], op=ALU.mult),
                      reads=[v32b[tb], lb], writes=[vnb[tb % 2]])
                PM, PMB = c.ps[6 + tb % 2], c.psb[6 + tb % 2]
                fns = []
                for gg in range(8):
                    cc, hh = gg // 2, gg % 2
                    o = PM[hh * 64:(hh + 1) * 64, cc * 128:(cc + 1) * 128]
                    fns.append(lambda e, o=o, gg=gg, tb=tb: e.matmul(o, vn[:, tb % 2, gg * 64:(gg + 1) * 64], wst[:, gg, :], start=True, stop=False))
                    fns.append(lambda e, o=o, gg=gg: e.matmul(o, c.ones_bf[0:1, 0:64], sgub_hi[0:1, gg * 128:(gg + 1) * 128], start=False, stop=False))
                    fns.append(lambda e, o=o, gg=gg: e.matmul(o, c.ones_bf[0:1, 0:64], sgub_lo[0:1, gg * 128:(gg + 1) * 128], start=False, stop=True))
                tr.mm_group(fns, reads=[vnb[tb % 2], wstb, sgubb, c.b_const], writes=[PMB])
                tr.op("DVE", lambda e, PM=PM, bsl=bsl: e.tensor_tensor(
                    out=yc[:, :, bsl], in0=PM[:].rearrange("p (c t) -> p c t", c=4), in1=ut[:, :, bsl], op=ALU.mult),
                    reads=[PMB, utb], writes=[ycb])
            tr.dma("SP", c.YcT[b].rearrange("(c p) t -> p c t", p=128)[:, :, tsl], yc[:], reads=[ycb])
            tr.op("DVE", lambda e: e.tensor_tensor_scan(out=lc[:], data0=ones8[:], data1=fl[:], initial=carry[:, 0:1], op0=ALU.mult, op1=ALU.add),
                  reads=[flb, lcb, lb], writes=[lcb])
            tr.op("DVE", lambda e: e.tensor_copy(out=carry[:], in_=lc[:, 511:512]), reads=[lcb], writes=[lcb])
            tr.op("DVE", lambda e: e.tensor_scalar(out=lcx[:], in0=lc[:], scalar1=-1.0, scalar2=None, op0=ALU.mult), reads=[lcb, flb], writes=[lcb, flb])
            for j in range(3):
                tr.op("DVE", lambda e, j=j: e.tensor_copy(out=pcs[:, j, :], in_=lcx[:]), reads=[lcb, flb, pcsb], writes=[pcsb])
                tr.op("DVE", lambda e, j=j: e.tensor_scalar(out=pcs[:, 3 + j, :], in0=pcs[:, j, :], scalar1=-1.0, scalar2=None, op0=ALU.mult),
                      reads=[pcsb], writes=[pcsb])
                if j < 2:
                    tr.op("DVE", lambda e, j=j: e.tensor_tensor(out=lcx[:], in0=lcx[:], in1=pcs[:, j, :], op=ALU.subtract),
                          reads=[lcb, flb, pcsb], writes=[lcb, flb])
            tr.dma("SP", c.QbT[b, :, 64:67, tsl], pcs[:, 0:3, :], reads=[pcsb])
            tr.dma("SP", c.KbT[b, :, 67:70, tsl], pcs[:, 3:6, :], reads=[pcsb])


def phase_attn(c, l, kind):
    nc, tr = c.nc, c.tr
    NB, S, NG, NBLK = c.NB, c.S, c.NG, c.NBLK
    sb_mode = (kind == "sb")
    KR = 128 if sb_mode else AUG
    NHU = NH // 2 if sb_mode else NH
    with ExitStack() as st:
        def sbt(name, shape, dt):
            return st.enter_context(nc.sbuf_tensor(uq(f"p{kind}_" + name), list(shape), dt))
        Vts = [sbt(f"v{i}", [128, NBLK, 512] if sb_mode else [128, NBLK, NH, 128], BF16) for i in range(2)]
        VBs = [Buf(f"at_v{i}") for i in range(2)]
        KT = [sbt(f"k{i}", [KR, S], BF16) for i in range(2)]
        QT = [sbt(f"q{i}", [KR, S], BF16) for i in range(2)]
        QTB2 = [sbt(f"qb{i}", [KR, S], BF16) for i in range(2)] if sb_mode else None
        KQB = [Buf(f"at_kq{i}") for i in range(2)]
        NR = 3
        E32 = [sbt(f"e{i}", [128, 2, 512], F32) for i in range(NR)] if sb_mode else None
        LS = [sbt(f"ls{i}", [128, 2, 512], BF16) for i in range(NR)] if sb_mode else None
        LSB = [Buf(f"at_ls{i}") for i in range(NR)]
        SS = [sbt(f"ss{i}", [128, 2, 512], BF16) for i in range(2)] if sb_mode else None
        SSB = [Buf(f"at_ss{i}") for i in range(2)]
        WT = [sbt(f"w{i}", [128, 2, 512] if sb_mode else [128, 512], BF16) for i in range(NR)]
        WTB = [Buf(f"at_w{i}") for i in range(NR)]
        YS = [sbt(f"y{i}", [128 if sb_mode else 64, 512], BF16) for i in range(2)]
        YSB = [Buf(f"at_y{i}") for i in range(2)]
        RC = [sbt(f"r{i}", [128, 512], F32) for i in range(2)] if not sb_mode else None
        RC0 = [sbt(f"r0_{i}", [64, 512], F32) for i in range(2)] if not sb_mode else None
        RCB = [Buf(f"at_r{i}") for i in range(2)]
        RC0B = [Buf(f"at_r0_{i}") for i in range(2)]
        nbias = c.nb_sb if sb_mode else c.nb_fox
        Ksrc = c.KaT if sb_mode else None
        dstT = c.YaT if sb_mode else c.YbT
        Vsrc = c.Va if sb_mode else c.Vb

        if sb_mode:
            for i_ in range(2):
                tr.op("POOL", lambda e, i_=i_: e.memset(QT[i_][64:128, :], 0.0), writes=[KQB[i_]])
                tr.op("POOL", lambda e, i_=i_: e.memset(QTB2[i_][0:64, :], 0.0), writes=[KQB[i_]])
        units = []
        ihead = 0
        igroup = 0
        for b in range(NB):
            for h in range(NHU):
                for g in range(NG):
                    kbs = list(range(4 * g + 3, -1, -1)) if sb_mode else list(range(0, 4 * g + 4))
                    for j, kb in enumerate(kbs):
                        units.append(dict(b=b, h=h, g=g, kb=kb, j=j, first=(j == 0), last=(j == len(kbs) - 1),
                                          ihead=ihead, igroup=igroup, newhead=(g == 0 and j == 0), newbatch=(h == 0 and g == 0 and j == 0)))
                    igroup += 1
                ihead += 1
        n = len(units)

        def stage_A(i):
            u = units[i]
            b, h, g, kb = u["b"], u["h"], u["g"], u["kb"]
            if u["newbatch"]:
                if sb_mode:
                    tr.dma("SP", Vts[b % 2][:], Vsrc[b].rearrange("(n p) c -> p n c", p=128), writes=[VBs[b % 2]])
                else:
                    tr.op("POOL", lambda e: e.memset(Vts[b % 2][:], 1.0), writes=[VBs[b % 2]])
                    vsrc4 = Vsrc[b].rearrange("(n p) (h d) -> p n h d", p=128, d=64)
                    for n0 in range(NBLK):
                        tr.dma("SP", Vts[b % 2][:, n0, :, 0:64], vsrc4[:, n0, :, :], writes=[VBs[b % 2]], part=(n0 > 0))
            if u["newhead"]:
                hb_ = u["ihead"] % 2
                if sb_mode:
                    tr.dma("SP", KT[hb_][:], c.KaT[b, h * 128:(h + 1) * 128, :], writes=[KQB[hb_]])
                    tr.dma("SP", QT[hb_][0:64, :], c.QaT[b, h * 128:h * 128 + 64, :], writes=[KQB[hb_]])
                    tr.dma("SP", QTB2[hb_][64:128, :], c.QaT[b, h * 128 + 64:(h + 1) * 128, :], writes=[KQB[hb_]])
                else:
                    tr.dma("SP", KT[hb_][:], c.KbT[b, h, :, :], writes=[KQB[hb_]])
                    tr.dma("SP", QT[hb_][:], c.QbT[b, h, :, :], writes=[KQB[hb_]])
            hb_ = u["ihead"] % 2
            c0 = max(0, kb - 4 * g) * 128
            diag = kb >= 4 * g
            u["c0"], u["diag"] = c0, diag
            if sb_mode:
                PZ2 = c.ps2[i % 3]
                PZBs = [c.psb[2 * (i % 3)], c.psb[2 * (i % 3) + 1]]
                u["PZ2"], u["PZBs"] = PZ2, PZBs
                fns = []
                for hh in range(2):
                    Qz = QT[hb_] if hh == 0 else QTB2[hb_]
                    fns.append(lambda e, hh=hh, Qz=Qz: e.matmul(PZ2[:, hh * 512 + c0:(hh + 1) * 512], KT[hb_][:, kb * 128:(kb + 1) * 128],
                                                                Qz[:, g * 512 + c0:(g + 1) * 512], start=True, stop=True))
                    if diag:
                        fns.append(lambda e, hh=hh: e.matmul(PZ2[:, hh * 512 + c0:hh * 512 + c0 + 128], c.ident_bf[:], nbias[:],
                                                             start=False, stop=True, skip_group_check=True))
                tr.mm_group(fns, reads=[KQB[hb_], c.b_const], writes=PZBs)
                return
            PZ, PZB = c.ps[i % 4], c.psb[i % 4]
            u["PZ"], u["PZB"] = PZ, PZB
            fns = [lambda e: e.matmul(PZ[:, c0:512], KT[hb_][:, kb * 128:(kb + 1) * 128], QT[hb_][:, g * 512 + c0:(g + 1) * 512],
                                      start=True, stop=True)]
            if diag:
                fns.append(lambda e: e.matmul(PZ[:, c0:c0 + 128], c.ident_bf[:], nbias[:], start=False, stop=True, skip_group_check=True))
            tr.mm_group(fns, reads=[KQB[hb_], c.b_const], writes=[PZB])

        def stage_B(i):
            u = units[i]
            c0 = u["c0"]
            r = i % NR
            if sb_mode:
                PZv = u["PZ2"][:].rearrange("p (h t) -> p h t", h=2)[:, :, c0:512]
                tr.op("ACT", lambda e: e.activation(out=E32[r][:, :, c0:512], in_=PZv, func=AF.Exp), reads=u["PZBs"], writes=[LSB[r]])
                tr.op("ACT", lambda e: e.activation(out=LS[r][:, :, c0:512], in_=E32[r][:, :, c0:512], func=AF.Ln, bias=c.one_col[:, 0:1]),
                      reads=[LSB[r], c.b_const], writes=[LSB[r]])
            else:
                PZ, PZB = u["PZ"], u["PZB"]
                tr.op("ACT", lambda e: e.activation(out=WT[r][:, c0:512], in_=PZ[:, c0:512], func=AF.Exp), reads=[PZB], writes=[WTB[r]])

        def stage_C(i):
            if not sb_mode:
                return
            u = units[i]
            c0, PZ2, PZBs, j = u["c0"], u["PZ2"], u["PZBs"], u["j"]
            r = i % NR
            if u["first"]:
                for k2 in range(2):
                    tr.op("POOL", lambda e, k2=k2: e.memset(SS[k2][:], 0.0), writes=[SSB[k2]])
            fns = []
            reads = [LSB[r], c.b_const]
            for hh in range(2):
                o = PZ2[:, hh * 512 + c0:(hh + 1) * 512]
                fns.append(lambda e, o=o, hh=hh: e.matmul(o, c.negtri_bf[:, 0, :], LS[r][:, hh, c0:512], start=False, stop=True, skip_group_check=True))
                if j > 0:
                    fns.append(lambda e, o=o, hh=hh: e.matmul(o, c.negtri_bf[:, 1, :], SS[j % 2][:, hh, c0:512], start=False, stop=True,
                                                              skip_group_check=True))
            if j > 0:
                reads.append(SSB[j % 2])
            tr.mm_group(fns, reads=reads, writes=PZBs)
            if not u["last"]:
                tr.op("DVE", lambda e: e.tensor_tensor(out=SS[(j + 1) % 2][:, :, c0:512], in0=SS[j % 2][:, :, c0:512], in1=LS[r][:, :, c0:512], op=ALU.add),
                      reads=[SSB[j % 2], LSB[r]], writes=[SSB[(j + 1) % 2]])
            PZv = PZ2[:].rearrange("p (h t) -> p h t", h=2)[:, :, c0:512]
            tr.op("ACT", lambda e: e.activation(out=WT[r][:, :, c0:512], in_=PZv, func=AF.Exp), reads=PZBs, writes=[WTB[r]])

        def stage_F(i):
            u = units[i]
            b, h, g, kb, c0 = u["b"], u["h"], u["g"], u["kb"], u["c0"]
            r = i % NR
            gi = u["igroup"] % 2
            PO, POB = (c.ps[6 + gi], c.psb[6 + gi]) if sb_mode else (c.ps[4 + gi], c.psb[4 + gi])
            Vt, VB = Vts[b % 2], VBs[b % 2]
            if sb_mode:
                fns = [(lambda e, hh=hh: e.matmul(PO[hh * 64:(hh + 1) * 64, c0:512], Vt[:, kb, (2 * h + hh) * 64:(2 * h + hh + 1) * 64],
                                                  WT[r][:, hh, c0:512], start=u["first"], stop=True, skip_group_check=True)) for hh in range(2)]
            else:
                fns = [lambda e: e.matmul(PO[:, c0:512], Vt[:, kb, h, :], WT[r][:, c0:512], start=u["first"], stop=True, skip_group_check=True)]
            tr.mm_group(fns, reads=[VB, WTB[r], c.b_const], writes=[POB])
            if u["last"]:
                Y, YB = YS[gi], YSB[gi]
                if sb_mode:
                    tr.op("DVE", lambda e: e.tensor_copy(out=Y[:], in_=PO[:, :]), reads=[POB], writes=[YB])
                    tr.dma("SP", dstT[b, h * 128:(h + 1) * 128, g * 512:(g + 1) * 512], Y[:], reads=[YB])
                    return
                else:
                    tr.op("DVE", lambda e: e.reciprocal(out=RC[gi][64:128, :], in_=PO[64:128, :]), reads=[POB], writes=[RCB[gi]])
                    tr.dma("SP", RC0[gi][:], RC[gi][64:128, :], reads=[RCB[gi]], writes=[RC0B[gi]])
                    tr.op("DVE", lambda e: e.tensor_tensor(out=Y[:], in0=PO[0:64, :], in1=RC0[gi][:], op=ALU.mult),
                          reads=[POB, RC0B[gi]], writes=[YB])
                tr.dma("SP", dstT[b, h * 64:(h + 1) * 64, g * 512:(g + 1) * 512], Y[:], reads=[YB])

        if sb_mode:
            for s_ in range(-2, n + 1):
                if 0 <= s_ + 2 < n:
                    stage_A(s_ + 2)
                if 0 <= s_ + 1 < n:
                    stage_B(s_ + 1)
                if 0 <= s_ < n:
                    stage_C(s_)
                if 0 <= s_ - 1 < n:
                    stage_F(s_ - 1)
        else:
            for s_ in range(-2, n):
                if 0 <= s_ + 2 < n:
                    stage_A(s_ + 2)
                    stage_B(s_ + 2)
                if 0 <= s_ < n:
                    stage_F(s_)


def phase_merge(c, l):
    nc, tr = c.nc, c.tr
    NB, S, NG = c.NB, c.S, c.NG
    with ExitStack() as st:
        def sb(name, shape, dt):
            return st.enter_context(nc.sbuf_tensor(uq("p5_" + name), list(shape), dt))
        WG = sb("wg", [128, KD, 3 * D], BF16)
        WBR = sb("wbr", [128, 12, D], BF16)
        WO = sb("wo", [128, KD, D], BF16)
        WB = Buf("p5_w")
        load_w_cast(c, WG, WB, c.w_in[l], C_G, 3 * D, KD)
        load_w_cast(c, WBR, WB, c.w_branch[l].rearrange("i k n -> (i k) n"), 0, D, 12)
        load_w_cast(c, WO, WB, c.w_out[l], 0, D, KD)
        xn = [sb(f"xn{i}", [128, KD, 512], BF16) for i in range(2)]
        ys = [sb(f"y{i}", [128, 12, 512], BF16) for i in range(2)]
        inb = [Buf(f"p5_in{i}") for i in range(2)]
        H = sb("h", [128, KD, 512], F32)
        HB = Buf("p5_h")
        MT = sb("mt", [128, KD, 512], BF16)
        MTB = Buf("p5_mt")
        G = [sb(f"g{i}", [128, 512], F32) for i in range(2)]
        GB = [Buf(f"p5_g{i}") for i in range(2)]
        T = [sb(f"t{i}", [128, 512], F32) for i in range(3)]
        TB = [Buf(f"p5_t{i}") for i in range(3)]
        M = [sb(f"m{i}", [128, 512], F32) for i in range(2)]
        MB = [Buf(f"p5_m{i}") for i in range(2)]
        ysrc = (c.YaT, c.YbT, c.YcT)

        def load_inputs(b, g, gi):
            tsl = slice(g * 512, (g + 1) * 512)
            tr.dma("SP", xn[gi % 2][:], c.xnT[b].rearrange("(k p) t -> p k t", p=128)[:, :, tsl], writes=[inb[gi % 2]])
            for i in range(3):
                tr.dma("SP", ys[gi % 2][:, i * 4:(i + 1) * 4, :], ysrc[i][b].rearrange("(k p) t -> p k t", p=128)[:, :, tsl], writes=[inb[gi % 2]])

        groups = [(b, g) for b in range(NB) for g in range(NG)]
        load_inputs(*groups[0], 0)
        ia = 0
        it = 0
        for gi, (b, g) in enumerate(groups):
            tsl = slice(g * 512, (g + 1) * 512)
            if gi + 1 < len(groups):
                load_inputs(*groups[gi + 1], gi + 1)
            XN, YS, INB = xn[gi % 2], ys[gi % 2], inb[gi % 2]
            hT_v = c.hT[b].rearrange("(k p) t -> p k t", p=128)
            tr.dma("SP", H[:], hT_v[:, :, tsl], writes=[HB])
            for cc in range(KD):
                csl = slice(cc * 128, (cc + 1) * 128)
                for i in range(3):
                    PA, PAB = c.ps[ia % 2], c.psb[ia % 2]
                    PBk, PBB = c.ps[2 + ia % 2], c.psb[2 + ia % 2]
                    Gt, GtB = G[ia % 2], GB[ia % 2]
                    tr.mm_group([(lambda e, k=k, PA=PA, i=i, csl=csl, XN=XN: e.matmul(
                        PA[:], WG[:, k, i * D + csl.start:i * D + csl.stop], XN[:, k, :], start=(k == 0), stop=(k == KD - 1))) for k in range(KD)],
                        reads=[WB, INB], writes=[PAB])
                    tr.mm_group([(lambda e, k=k, PBk=PBk, i=i, csl=csl, YS=YS: e.matmul(
                        PBk[:], WBR[:, i * 4 + k, csl], YS[:, i * 4 + k, :], start=(k == 0), stop=(k == 3))) for k in range(4)],
                        reads=[WB, INB], writes=[PBB])
                    tr.op("ACT", lambda e, PA=PA, Gt=Gt: e.activation(out=Gt[:], in_=PA[:], func=AF.Sigmoid), reads=[PAB], writes=[GtB])
                    Tt, TtB = T[it % 3], TB[it % 3]
                    it += 1
                    tr.op("DVE", lambda e, PBk=PBk, Gt=Gt, Tt=Tt: e.tensor_tensor(out=Tt[:], in0=PBk[:], in1=Gt[:], op=ALU.mult),
                          reads=[PBB, GtB], writes=[TtB])
                    if i == 0:
                        T0, T0B = Tt, TtB
                    elif i == 1:
                        Mt, MtB = M[cc % 2], MB[cc % 2]
                        tr.op("POOL", lambda e, Mt=Mt, T0=T0, Tt=Tt: e.tensor_tensor(out=Mt[:], in0=T0[:], in1=Tt[:], op=ALU.add),
                              reads=[T0B, TtB], writes=[MtB])
                    else:
                        tr.op("POOL", lambda e, Mt=Mt, Tt=Tt, cc=cc: e.tensor_tensor(out=MT[:, cc, :], in0=Mt[:], in1=Tt[:], op=ALU.add),
                              reads=[MtB, TtB], writes=[MTB])
                    ia += 1
            for cc in range(KD):
                csl = slice(cc * 128, (cc + 1) * 128)
                PO, POB = c.ps[4 + cc % 2], c.psb[4 + cc % 2]
                tr.mm_group([(lambda e, k=k, PO=PO, csl=csl: e.matmul(PO[:], WO[:, k, csl], MT[:, k, :], start=(k == 0), stop=(k == KD - 1)))
                             for k in range(KD)], reads=[WB, MTB], writes=[POB])
                tr.op("DVE", lambda e, PO=PO, cc=cc: e.tensor_tensor(out=H[:, cc, :], in0=PO[:], in1=H[:, cc, :], op=ALU.add),
                      reads=[POB, HB], writes=[HB])
            tr.dma("SP", hT_v[:, :, tsl], H[:], reads=[HB])


def phase_ffn_up(c, l):
    nc, tr = c.nc, c.tr
    NB, S, NG = c.NB, c.S, c.NG
    with ExitStack() as st:
        def sb(name, shape, dt):
            return st.enter_context(nc.sbuf_tensor(uq("p6_" + name), list(shape), dt))
        WU = sb("wu", [128, KD, 2 * D_FF], BF16)
        segs = []
        for jb in range(0, NFF, 4):
            n_ = (min(jb + 4, NFF) - jb) * 128
            segs += [(jb * 128, n_), (D_FF + jb * 128, n_)]
        wfind = load_w_segments(c, WU, c.w_up[l], segs, KD, "p6_w")
        gcol = c.smallv[:, c.o_gffn + l * KD:c.o_gffn + (l + 1) * KD]
        ht = [sb(f"h{i}", [128, KD, 512], F32) for i in range(2)]
        hb = [Buf(f"p6_h{i}") for i in range(2)]
        XN = sb("xn", [128, KD, 512], BF16)
        XNB = Buf("p6_xn")
        sq = sb("sq", [128, KD, 512], BF16)
        sqb = Buf("p6_sq")
        rstd = sb("rstd", [128, 512], F32)
        rstdb = Buf("p6_rstd")
        NR = 3
        GT = [sb(f"gt{i}", [128, 514], F32) for i in range(NR)]
        GTB = [Buf(f"p6_gt{i}") for i in range(NR)]
        A0 = [sb(f"a0_{i}", [128, 512], F32) for i in range(NR)]
        A0B = [Buf(f"p6_a0_{i}") for i in range(NR)]
        A1 = [sb(f"a1_{i}", [128, 512], F32) for i in range(NR)]
        A1B = [Buf(f"p6_a1_{i}") for i in range(NR)]
        HM = [sb(f"hm{i}", [128, 512], BF16) for i in range(NR)]
        HMB = [Buf(f"p6_hm{i}") for i in range(NR)]
        HALO = sb("halo", [128, NFF, 2], F32)
        HALOB = [Buf(f"p6_halo{j}") for j in range(NFF)]
        groups = [(b, g) for b in range(NB) for g in range(NG)]
        tr.dma("SP", ht[0][:], c.hT[groups[0][0]].rearrange("(k p) t -> p k t", p=128)[:, :, 0:512], writes=[hb[0]])
        it = 0
        for gi, (b, g) in enumerate(groups):
            tsl = slice(g * 512, (g + 1) * 512)
            if gi + 1 < len(groups):
                b2, g2 = groups[gi + 1]
                tr.dma("SP", ht[(gi + 1) % 2][:], c.hT[b2].rearrange("(k p) t -> p k t", p=128)[:, :, g2 * 512:(g2 + 1) * 512],
                       writes=[hb[(gi + 1) % 2]])
            H, HB = ht[gi % 2], hb[gi % 2]
            emit_rmsnorm(c, H, HB, XN, XNB, gcol, sq, sqb, rstd, rstdb, 0)
            if g == 0:
                tr.op("POOL", lambda e: e.memset(HALO[:], 0.0), writes=HALOB)
            for j in range(NFF):
                r = it % NR
                it += 1
                PG, PGB = c.ps[1 + (j % 2)], c.psb[1 + (j % 2)]
                PV, PVB = c.ps[3 + (j % 2)], c.psb[3 + (j % 2)]
                tr.mm_group([(lambda e, k=k, PG=PG, j=j: e.matmul(PG[:], WU[:, k, j * 128:(j + 1) * 128], XN[:, k, :],
                                                                  start=(k == 0), stop=(k == KD - 1))) for k in range(KD)],
                            reads=wfind(j * 128, (j + 1) * 128) + [XNB], writes=[PGB])
                tr.mm_group([(lambda e, k=k, PV=PV, j=j: e.matmul(PV[:], WU[:, k, D_FF + j * 128:D_FF + (j + 1) * 128], XN[:, k, :],
                                                                  start=(k == 0), stop=(k == KD - 1))) for k in range(KD)],
                            reads=wfind(D_FF + j * 128, D_FF + (j + 1) * 128) + [XNB], writes=[PVB])
                cw = lambda i_, j=j: c.smallv[:, c.o_cw + l * 3 * NFF + i_ * NFF + j:c.o_cw + l * 3 * NFF + i_ * NFF + j + 1]
                cb = c.smallv[:, c.o_cb + l * NFF + j:c.o_cb + l * NFF + j + 1]
                tr.op("ACT", lambda e, r=r, PG=PG: e.activation(out=GT[r][:, 2:514], in_=PG[:], func=AF.Copy), reads=[PGB], writes=[GTB[r]])
                tr.op("POOL", lambda e, r=r, j=j: e.tensor_copy(out=GT[r][:, 0:2], in_=HALO[:, j, :]), reads=[HALOB[j], GTB[r]], writes=[GTB[r]])
                tr.op("POOL", lambda e, r=r, j=j: e.tensor_copy(out=HALO[:, j, :], in_=GT[r][:, 512:514]), reads=[GTB[r]], writes=[HALOB[j]])
                tr.op("ACT", lambda e, r=r, PG=PG, cw=cw, cb=cb: e.activation(out=A0[r][:], in_=PG[:], func=AF.Identity, scale=cw(2), bias=cb),
                      reads=[PGB, c.b_const], writes=[A0B[r]])
                tr.op("DVE", lambda e, r=r, cw=cw: e.scalar_tensor_tensor(out=A1[r][:], in0=GT[r][:, 1:513], scalar=cw(1), in1=A0[r][:],
                                                                          op0=ALU.mult, op1=ALU.add),
                      reads=[GTB[r], A0B[r], c.b_const], writes=[A1B[r]])
                tr.op("DVE", lambda e, r=r, cw=cw: e.scalar_tensor_tensor(out=A0[r][:], in0=GT[r][:, 0:512], scalar=cw(0), in1=A1[r][:],
                                                                          op0=ALU.mult, op1=ALU.add),
                      reads=[GTB[r], A1B[r], c.b_const], writes=[A0B[r]])
                tr.op("ACT", lambda e, r=r: e.activation(out=A1[r][:], in_=A0[r][:], func=AF.Gelu), reads=[A0B[r]], writes=[A1B[r]])
                tr.op("DVE", lambda e, r=r, PV=PV: e.tensor_tensor(out=HM[r][:], in0=PV[:], in1=A1[r][:], op=ALU.mult),
                      reads=[PVB, A1B[r]], writes=[HMB[r]])
                tr.dma("SP", c.hmid[b, j * 128:(j + 1) * 128, tsl], HM[r][:], reads=[HMB[r]])


def phase_ffn_down(c, l):
    nc, tr = c.nc, c.tr
    NB, S, NG = c.NB, c.S, c.NG
    with ExitStack() as st:
        def sb(name, shape, dt):
            return st.enter_context(nc.sbuf_tensor(uq("p6b_" + name), list(shape), dt))
        WD = sb("wd", [128, NFF, D], BF16)
        wfind = load_w_segments(c, WD, c.w_down[l], [(o, 256) for o in range(0, D, 256)], NFF, "p7_w")
        hm = [sb(f"hm{i}", [128, NFF, 512], BF16) for i in range(2)]
        hmb = [Buf(f"p7_hm{i}") for i in range(2)]
        ht = [sb(f"h{i}", [128, KD, 512], F32) for i in range(2)]
        hb = [Buf(f"p7_h{i}") for i in range(2)]
        groups = [(b, g) for b in range(NB) for g in range(NG)]

        def load(gi):
            b, g = groups[gi]
            tsl = slice(g * 512, (g + 1) * 512)
            tr.dma("SP", hm[gi % 2][:], c.hmid[b].rearrange("(k p) t -> p k t", p=128)[:, :, tsl], writes=[hmb[gi % 2]])
            tr.dma("SP", ht[gi % 2][:], c.hT[b].rearrange("(k p) t -> p k t", p=128)[:, :, tsl], writes=[hb[gi % 2]])
        load(0)
        for gi, (b, g) in enumerate(groups):
            tsl = slice(g * 512, (g + 1) * 512)
            if gi + 1 < len(groups):
                load(gi + 1)
            HMt, HMtB, H, HB = hm[gi % 2], hmb[gi % 2], ht[gi % 2], hb[gi % 2]
            for cc in range(KD):
                csl = slice(cc * 128, (cc + 1) * 128)
                PO, POB = c.ps[cc % 4], c.psb[cc % 4]
                tr.mm_group([(lambda e, k=k, PO=PO, csl=csl, HMt=HMt: e.matmul(PO[:], WD[:, k, csl], HMt[:, k, :], start=(k == 0), stop=(k == NFF - 1)))
                             for k in range(NFF)], reads=wfind(csl.start, csl.stop) + [HMtB], writes=[POB])
                tr.op("DVE", lambda e, PO=PO, cc=cc, H=H: e.tensor_tensor(out=H[:, cc, :], in0=PO[:], in1=H[:, cc, :], op=ALU.add),
                      reads=[POB, HB], writes=[HB])
            tr.dma("SP", c.hT[b].rearrange("(k p) t -> p k t", p=128)[:, :, tsl], H[:], reads=[HB])


def phase_final(c):
    nc, tr = c.nc, c.tr
    with ExitStack() as st:
        def sb(name, shape, dt):
            return st.enter_context(nc.sbuf_tensor(name, list(shape), dt))
        gcol = sb("p7_g", [128, KD], F32)
        gb = Buf("p7_g")
        tr.dma("SP", gcol[:], c.final_g.rearrange("(k p) -> p k", p=128), writes=[c.b_const], allow_slow_non_contiguous=True)
        ht = [sb(f"p7_h{i}", [128, KD, 512], F32) for i in range(2)]
        hb = [Buf(f"p7_h{i}") for i in range(2)]
        xn = [sb(f"p7_xn{i}", [128, KD, 512], F32) for i in range(2)]
        xnb = [Buf(f"p7_xn{i}") for i in range(2)]
        sq = sb("p7_sq", [128, KD, 512], BF16)
        sqb = Buf("p7_sq")
        rstd = sb("p7_rstd", [128, 512], F32)
        rstdb = Buf("p7_rstd")
        ot = [sb(f"p7_o{i}", [128, D], F32) for i in range(2)]
        ob = [Buf(f"p7_o{i}") for i in range(2)]
        it = 0
        gi = 0
        for b in range(c.NB):
            hT_v = c.hT[b].rearrange("(k p) t -> p k t", p=128)
            for g in range(c.NG):
                H, HB = ht[gi % 2], hb[gi % 2]
                XN, XNB = xn[gi % 2], xnb[gi % 2]
                tr.dma("SP", H[:], hT_v[:, :, g * 512:(g + 1) * 512], writes=[HB])
                emit_rmsnorm(c, H, HB, XN, XNB, gcol, sq, sqb, rstd, rstdb, 4)
                for tb in range(4):
                    O, OB = ot[it % 2], ob[it % 2]
                    for half in range(2):
                        pi = (it * 2 + half) % 4
                        P, PB = c.ps[pi], c.psb[pi]
                        tr.mm_group(
                            [(lambda e, j=j, P=P, XN=XN, half=half, tb=tb: e.transpose(
                                P[:, j * 128:(j + 1) * 128], XN[:, half * 4 + j, tb * 128:(tb + 1) * 128], c.ident[:]))
                             for j in range(4)],
                            reads=[XNB, c.b_const], writes=[PB])
                        if half == 0:
                            tr.op("ACT", lambda e, P=P, O=O, half=half: e.activation(
                                out=O[:, half * 512:(half + 1) * 512], in_=P[:], func=AF.Copy), reads=[PB], writes=[OB])
                        else:
                            tr.op("DVE", lambda e, P=P, O=O, half=half: e.tensor_copy(
                                out=O[:, half * 512:(half + 1) * 512], in_=P[:]), reads=[PB], writes=[OB])
                    t0 = g * 512 + tb * 128
                    tr.dma("SP", c.out[b, t0:t0 + 128, :], O[:], reads=[OB])
                    it += 1
                gi += 1


_CACHE = {}


def host_consts():
    ident = np.eye(128, dtype=np.float32)
    s = np.arange(128)[:, None]
    t = np.arange(128)[None, :]
    masks = np.zeros((128, 4, 128), np.float32)
    masks[:, 0, :] = (s < t)
    masks[:, 1, :] = (s <= t)
    masks[:, 2, :] = (s >= t)
    masks[:, 3, :] = 1.0
    return ident, masks


def kernel(**inputs):
    NCORES = 8
    x = np.ascontiguousarray(inputs["x"], dtype=np.float32)
    B, S, _ = x.shape
    NB = B // NCORES
    depth = inputs["w_in"].shape[0]
    key = (NB, S, depth)
    if key not in _CACHE:
        _CACHE[key] = build(NB, S, depth)
    nc = _CACHE[key]
    ident, masks = host_consts()
    shared = {k: np.ascontiguousarray(v, dtype=np.float32) for k, v in inputs.items() if k != "x"}
    shared["k_ident"] = ident
    shared["k_masks"] = masks
    in_maps = []
    for i in range(NCORES):
        m = dict(shared)
        m["x"] = x[i * NB:(i + 1) * NB]
        in_maps.append(m)
    res = run_bass_kernel_spmd(nc, in_maps, core_ids=list(range(NCORES)))
    return np.concatenate([r["out"] for r in res.results], axis=0)
```
